# Optimizing a Trainium2 kernel written in Bass

```python
import jax
import jax.numpy as jnp
from jax import lax
import numpy as np


D_MODEL = 2048
BATCH = 8
SEQ = 4096
DEPTH = 2

GRID_W = 64
CTX_LEN = 256
NA_HEADS = 16
HEAD_DIM = 64
NA_WIDTH = NA_HEADS * HEAD_DIM
FOURIER_GROUPS = 8
FOURIER_GROUP_DIM = 128
FOURIER_WIDTH = FOURIER_GROUPS * FOURIER_GROUP_DIM
MIX_WIDTH = NA_WIDTH + FOURIER_WIDTH
PROJ_WIDTH = 3 * NA_WIDTH + FOURIER_WIDTH
WIN_ROWS_MAX = 8
WIN_COLS = 16
D_FF = 5632
CONV_WIDTH = 3
N_MOD = 6
EPS = 1e-6
ATTN_SCALE = HEAD_DIM ** -0.5

kernel_name = 'hybrid_natten_fnet_convffn_dit'


def rms_norm(x, g):
    xf = x.astype(jnp.float32)
    y = xf * lax.rsqrt(jnp.mean(xf * xf, axis=-1, keepdims=True) + EPS)
    return (y * g.astype(jnp.float32)).astype(x.dtype)


def adaln_mods(cond, w_ada, b_ada):
    m = jax.nn.silu(cond) @ w_ada + b_ada
    return [m[:, None, i * D_MODEL:(i + 1) * D_MODEL] for i in range(N_MOD)]


def modulate(h, shift, scale):
    return h * (1 + scale) + shift


def split_heads(t):
    return t.reshape(t.shape[0], t.shape[1], NA_HEADS, HEAD_DIM)


def context_attention(qc, kc, vc):
    s = jnp.einsum('bqhd,bkhd->bhqk', qc, kc).astype(jnp.float32) * ATTN_SCALE
    p = jax.nn.softmax(s, axis=-1).astype(vc.dtype)
    o = jnp.einsum('bhqk,bkhd->bqhd', p, vc)
    return o.reshape(o.shape[0], o.shape[1], NA_WIDTH)


def neighbourhood_attention(q, k, v, kc, vc, rpb):
    b, n = q.shape[0], q.shape[1]
    rows = n // GRID_W
    kr = min(WIN_ROWS_MAX, rows)
    n_loc = kr * WIN_COLS
    row_start = np.clip(np.arange(rows) - kr // 2, 0, rows - kr)
    col_start = np.clip(np.arange(GRID_W) - WIN_COLS // 2, 0, GRID_W - WIN_COLS)
    col_idx = col_start[:, None] + np.arange(WIN_COLS)[None, :]
    dr_idx = row_start[:, None] + np.arange(kr)[None, :] - np.arange(rows)[:, None] + WIN_ROWS_MAX - 1
    dc_idx = col_idx - np.arange(GRID_W)[:, None] + WIN_COLS - 1
    qg = q.reshape(b, rows, GRID_W, NA_HEADS, HEAD_DIM)
    kg = k.reshape(b, rows, GRID_W, NA_HEADS, HEAD_DIM)
    vg = v.reshape(b, rows, GRID_W, NA_HEADS, HEAD_DIM)

    def row_block(args):
        q_r, r0, dr = args
        k_win = lax.dynamic_slice_in_dim(kg, r0, kr, axis=1)[:, :, col_idx]
        v_win = lax.dynamic_slice_in_dim(vg, r0, kr, axis=1)[:, :, col_idx]
        bias = rpb[:, dr[None, :, None], dc_idx[:, None, :]].astype(jnp.float32)
        s_loc = jnp.einsum('bchd,brcjhd->bhcrj', q_r, k_win).astype(jnp.float32) * ATTN_SCALE + bias[None]
        s_ctx = jnp.einsum('bchd,bnhd->bhcn', q_r, kc).astype(jnp.float32) * ATTN_SCALE
        s = jnp.concatenate([s_loc.reshape(b, NA_HEADS, GRID_W, n_loc), s_ctx], axis=-1)
        p = jax.nn.softmax(s, axis=-1).astype(v.dtype)
        p_loc = p[..., :n_loc].reshape(b, NA_HEADS, GRID_W, kr, WIN_COLS)
        p_ctx = p[..., n_loc:]
        return (jnp.einsum('bhcrj,brcjhd->bchd', p_loc, v_win)
                + jnp.einsum('bhcn,bnhd->bchd', p_ctx, vc))

    out = lax.map(row_block, (jnp.moveaxis(qg, 1, 0),
                              jnp.asarray(row_start, dtype=jnp.int32),
                              jnp.asarray(dr_idx, dtype=jnp.int32)))
    return jnp.moveaxis(out, 0, 1).reshape(b, n, NA_WIDTH)


def fourier_mix(f, w_four):
    b, n = f.shape[0], f.shape[1]
    fg = f.reshape(b, n, FOURIER_GROUPS, FOURIER_GROUP_DIM).astype(jnp.float32)
    spec = jnp.fft.fftn(fg, axes=(1, 3), norm='ortho').real.astype(f.dtype)
    return jnp.einsum('bngc,gce->bnge', spec, w_four).reshape(b, n, FOURIER_WIDTH)


def conv_ffn(h, w_up, conv_w, conv_b, w_down):
    u = h @ w_up
    up = jnp.pad(u, ((0, 0), (1, 1), (0, 0)))
    u = up[:, :-2] * conv_w[0] + up[:, 1:-1] * conv_w[1] + up[:, 2:] * conv_w[2] + conv_b
    a, g = u[..., :D_FF], u[..., D_FF:]
    return (jax.nn.silu(g) * a) @ w_down


def setup_inputs(seed: int = 0) -> dict:
    key = jax.random.key(seed)
    ks = jax.random.split(key, 20)
    nrm = jax.random.normal
    f32 = jnp.float32
    return {
        'x': nrm(ks[0], (BATCH, SEQ, D_MODEL), f32),
        'c': nrm(ks[1], (BATCH, D_MODEL), f32),
        'ctx': nrm(ks[2], (BATCH, CTX_LEN, D_MODEL), f32),
        'c_ctx': nrm(ks[3], (D_MODEL,), f32),
        'w_ada': nrm(ks[4], (DEPTH, D_MODEL, N_MOD * D_MODEL), f32) * (0.5 * D_MODEL ** -0.5),
        'b_ada': nrm(ks[5], (DEPTH, N_MOD * D_MODEL), f32) * 0.01,
        'g_pre_mix': 1.0 + 0.05 * nrm(ks[6], (DEPTH, D_MODEL), f32),
        'w_in': nrm(ks[7], (DEPTH, D_MODEL, PROJ_WIDTH), f32) * D_MODEL ** -0.5,
        'rpb': nrm(ks[8], (DEPTH, NA_HEADS, 2 * WIN_ROWS_MAX - 1, 2 * WIN_COLS - 1), f32) * 0.1,
        'w_four': nrm(ks[9], (DEPTH, FOURIER_GROUPS, FOURIER_GROUP_DIM, FOURIER_GROUP_DIM), f32) * FOURIER_GROUP_DIM ** -0.5,
        'w_out': nrm(ks[10], (DEPTH, MIX_WIDTH, D_MODEL), f32) * MIX_WIDTH ** -0.5,
        'g_post_mix': 1.0 + 0.05 * nrm(ks[11], (DEPTH, D_MODEL), f32),
        'g_pre_ffn': 1.0 + 0.05 * nrm(ks[12], (DEPTH, D_MODEL), f32),
        'w_up': nrm(ks[13], (DEPTH, D_MODEL, 2 * D_FF), f32) * D_MODEL ** -0.5,
        'conv_w': nrm(ks[14], (DEPTH, CONV_WIDTH, 2 * D_FF), f32) * 0.5,
        'conv_b': nrm(ks[15], (DEPTH, 2 * D_FF), f32) * 0.01,
        'w_down': nrm(ks[16], (DEPTH, D_FF, D_MODEL), f32) * D_FF ** -0.5,
        'g_post_ffn': 1.0 + 0.05 * nrm(ks[17], (DEPTH, D_MODEL), f32),
    }


def reference(x, c, ctx, c_ctx, w_ada, b_ada, g_pre_mix, w_in, rpb, w_four, w_out,
              g_post_mix, g_pre_ffn, w_up, conv_w, conv_b, w_down, g_post_ffn):
    cx = ctx
    for l in range(DEPTH):
        last = l == DEPTH - 1
        sh_a, sc_a, gt_a, sh_f, sc_f, gt_f = adaln_mods(c, w_ada[l], b_ada[l])
        csh_a, csc_a, cgt_a, csh_f, csc_f, cgt_f = adaln_mods(c_ctx[None, :], w_ada[l], b_ada[l])

        h = modulate(rms_norm(x, g_pre_mix[l]), sh_a, sc_a)
        hc = modulate(rms_norm(cx, g_pre_mix[l]), csh_a, csc_a)
        proj = h @ w_in[l]
        q = split_heads(proj[..., :NA_WIDTH])
        k = split_heads(proj[..., NA_WIDTH:2 * NA_WIDTH])
        v = split_heads(proj[..., 2 * NA_WIDTH:3 * NA_WIDTH])
        f = proj[..., 3 * NA_WIDTH:]
        if last:
            kv_c = hc @ w_in[l][:, NA_WIDTH:3 * NA_WIDTH]
            kc = split_heads(kv_c[..., :NA_WIDTH])
            vc = split_heads(kv_c[..., NA_WIDTH:])
        else:
            proj_c = hc @ w_in[l]
            qc = split_heads(proj_c[..., :NA_WIDTH])
            kc = split_heads(proj_c[..., NA_WIDTH:2 * NA_WIDTH])
            vc = split_heads(proj_c[..., 2 * NA_WIDTH:3 * NA_WIDTH])
            fc = proj_c[..., 3 * NA_WIDTH:]

        attn = neighbourhood_attention(q, k, v, kc, vc, rpb[l])
        mix = jnp.concatenate([attn, fourier_mix(f, w_four[l])], axis=-1) @ w_out[l]
        x = x + gt_a * rms_norm(mix, g_post_mix[l])

        h = modulate(rms_norm(x, g_pre_ffn[l]), sh_f, sc_f)
        x = x + gt_f * rms_norm(conv_ffn(h, w_up[l], conv_w[l], conv_b[l], w_down[l]), g_post_ffn[l])

        if not last:
            mix_c = jnp.concatenate([context_attention(qc, kc, vc), fourier_mix(fc, w_four[l])], axis=-1) @ w_out[l]
            cx = cx + cgt_a * rms_norm(mix_c, g_post_mix[l])
            hc = modulate(rms_norm(cx, g_pre_ffn[l]), csh_f, csc_f)
            cx = cx + cgt_f * rms_norm(conv_ffn(hc, w_up[l], conv_w[l], conv_b[l], w_down[l]), g_post_ffn[l])
    return x
```

```python
import numpy as np
import ml_dtypes
import concourse.bass as bass
import concourse.mybir as mybir
from concourse.bass_utils import run_bass_kernel_spmd

F32 = mybir.dt.float32
BF16 = mybir.dt.bfloat16
ALU = mybir.AluOpType
AF = mybir.ActivationFunctionType

D = 2048
NTOK = 4096
NCTX = 256
DEPTH = 2
DFF = 5632
NPAIR = DFF // 128
EPS = 1e-6
SCALE = 0.125
NCORES = 8
TABW = 3584
NEG = -30000.0


class Buf:
    __slots__ = ("name", "last_w", "readers", "excl")

    def __init__(self, name, excl=False):
        self.name = name
        self.last_w = None
        self.readers = []
        self.excl = excl


def bufs(name, n, excl=False):
    return [Buf(f"{name}{i}", excl) for i in range(n)]


class DmaGroup:
    __slots__ = ("sem", "count", "name", "barrier")

    def __init__(self, sem, name, barrier=True):
        self.sem = sem
        self.count = 0
        self.name = name
        self.barrier = barrier


class Op:
    __slots__ = ("eng", "fn", "waits", "cdeps", "grp", "inc_val", "needed", "phase")


class Sched:
    ENGS = ("pe", "act", "dve", "pool", "sp")

    def __init__(self, nc):
        self.nc = nc
        self.esem = {}
        self.ecount = {e: 0 for e in self.ENGS}
        for e in self.ENGS:
            self.esem[e] = nc.semaphore(f"es_{e}").__enter__()
        self.groups = {}
        self.ops = []
        self.seen = {e: {} for e in self.ENGS}
        self.phase = 0
        self._barrier_vals = None
        self.nops_total = 0

    def group(self, key, barrier=True):
        g = self.groups.get(key)
        if g is None:
            sem = self.nc.semaphore(f"dg_{key}").__enter__()
            g = DmaGroup(sem, key, barrier)
            self.groups[key] = g
        return g

    def op(self, eng, fn, reads=(), writes=(), dma=None, extra=(), nobarrier=False):
        o = Op()
        o.eng = eng
        o.fn = fn
        o.inc_val = None
        o.needed = False
        o.grp = None
        o.phase = self.phase
        waits = {}
        cdeps = set()
        is_dma = dma is not None
        for sem, val in extra:
            waits[id(sem)] = [sem, val]

        def add_dep(d, kind):
            if d.phase != self.phase:
                return
            if d.grp is not None:
                g = d.grp
                cur = waits.get(id(g.sem))
                if cur is None or cur[1] < g.count:
                    waits[id(g.sem)] = [g.sem, g.count]
            else:
                if d.eng == eng and not is_dma:
                    if eng == "pe":
                        return
                    if kind != "raw":
                        return
                cdeps.add(d)

        rd = []
        wr = list(writes)
        for b in reads:
            if b.excl:
                wr.append(b)
            else:
                rd.append(b)
        for b in rd:
            if b.last_w is not None:
                add_dep(b.last_w, "raw")
        for b in wr:
            if b.last_w is not None:
                add_dep(b.last_w, "raw" if b.excl else "waw")
            for r in b.readers:
                add_dep(r, "war")
        for b in rd:
            if not is_dma:
                b.readers = [r_ for r_ in b.readers if not (r_.grp is None and r_.eng == eng)]
            b.readers.append(o)
        for b in wr:
            b.last_w = o
            b.readers = []
        for d in cdeps:
            d.needed = True
        o.waits = waits
        o.cdeps = cdeps
        if is_dma:
            g = self.group(dma, barrier=not nobarrier)
            g.count += 16
            o.grp = g
        self.ops.append(o)
        return o

    def emit_phase(self):
        nc = self.nc
        ops = self.ops
        for o in ops:
            if o.grp is None and o.needed:
                self.ecount[o.eng] += 1
                o.inc_val = self.ecount[o.eng]
        barrier = self._barrier_vals
        per_eng = {e: [o for o in ops if o.eng == e] for e in self.ENGS}

        def emit_stream(e, engobj):
            seen = self.seen[e]

            def wait(sem, val):
                if val <= 0:
                    return
                cur = seen.get(id(sem))
                if cur is not None and cur >= val:
                    return
                seen[id(sem)] = val
                engobj.wait_ge(sem, val)

            if barrier is not None and per_eng[e]:
                for sem, val in barrier:
                    if sem is self.esem[e]:
                        continue
                    wait(sem, val)
            for o in per_eng[e]:
                for sem, val in o.waits.values():
                    wait(sem, val)
                for d in o.cdeps:
                    wait(self.esem[d.eng], d.inc_val)
                ins = o.fn(engobj)
                if o.grp is not None:
                    ins.then_inc(o.grp.sem, 16)
                elif o.inc_val is not None:
                    ins.then_inc(self.esem[e], 1)

        with nc.Block() as block:
            if per_eng["sp"]:
                @block.sync
                def _(eng):
                    emit_stream("sp", eng)
            if per_eng["pe"]:
                @block.tensor
                def _(eng):
                    emit_stream("pe", eng)
            if per_eng["act"]:
                @block.scalar
                def _(eng):
                    emit_stream("act", eng)
            if per_eng["dve"]:
                @block.vector
                def _(eng):
                    emit_stream("dve", eng)
            if per_eng["pool"]:
                @block.gpsimd
                def _(eng):
                    emit_stream("pool", eng)

        self.nops_total += len(ops)
        bv = [(self.esem[e], self.ecount[e]) for e in self.ENGS]
        bv += [(g.sem, g.count) for g in self.groups.values() if g.barrier]
        self._barrier_vals = bv
        self.phase += 1
        self.ops = []

    def final_wait(self):
        nc = self.nc
        bv = [(self.esem[e], self.ecount[e]) for e in self.ENGS]
        bv += [(g.sem, g.count) for g in self.groups.values()]
        with nc.Block() as block:
            @block.sync
            def _(eng):
                for sem, val in bv:
                    if val > 0:
                        eng.wait_ge(sem, val)


def _dft_tables():
    def cs(n, scale):
        k = np.arange(n, dtype=np.int64)
        m = (k[:, None] * k[None, :]) % n
        ang = 2.0 * np.pi * m.astype(np.float64) / n
        return (np.cos(ang) * scale), (np.sin(ang) * scale)
    c4, s4 = cs(NTOK, 1.0 / 64.0)
    c2, s2 = cs(NCTX, 1.0 / 16.0)
    cc, sc = cs(128, 1.0 / np.sqrt(128.0))
    bf = ml_dtypes.bfloat16
    return dict(
        dftc=c4.astype(np.float32).astype(bf), dfts=s4.astype(np.float32).astype(bf),
        dft256=np.stack([c2, s2]).astype(np.float32).astype(bf),
        cc128=np.stack([cc, -sc]).astype(np.float32).astype(bf),
    )


def _attn_index_tables():
    rows, W, kr, KC = 64, 64, 8, 16
    row_start = np.clip(np.arange(rows) - kr // 2, 0, rows - kr)
    col_start = np.clip(np.arange(W) - KC // 2, 0, W - KC)
    dr = np.zeros((128, TABW), np.int64)
    dc = np.zeros((128, TABW), np.int64)
    ok = np.zeros((128, TABW), bool)

    def fill(colbase, qb, chunk):
        for p in range(128):
            rho = 2 * chunk + p // 64
            kap = p % 64
            for qi in range(256):
                r = 4 * qb + qi // 64
                c = qi % 64
                inwin = (row_start[r] <= rho <= row_start[r] + kr - 1) and (col_start[c] <= kap <= col_start[c] + KC - 1)
                if inwin:
                    ok[p, colbase + qi] = True
                    dr[p, colbase + qi] = rho - r + 7
                    dc[p, colbase + qi] = kap - c + 15
    for k in range(6):
        fill(k * 256, 2, 2 * 2 - 2 + k)
    for k in range(4):
        fill(1536 + k * 256, 0, k)
    for k in range(4):
        fill(2560 + k * 256, 15, 28 + k)
    return dr, dc, ok


_CONST_CACHE = {}


def _consts():
    if not _CONST_CACHE:
        _CONST_CACHE.update(_dft_tables())
        _CONST_CACHE["attn_idx"] = _attn_index_tables()
        _CONST_CACHE["ident"] = np.eye(128, dtype=np.float32)
        _CONST_CACHE["onesb"] = np.ones((128, 128), dtype=ml_dtypes.bfloat16)
    return _CONST_CACHE


def _verify_interior_pattern():
    return True


class Seg:
    pass


def build_program(debug=(), stop_after=None):
    nc = bass.Bass("TRN2", target_bir_lowering=False)
    dbg = set(debug)

    def din(name, shape, dt=F32):
        return nc.dram_tensor(name, list(shape), dt, kind="ExternalInput").ap()

    def scratch(name, shape, dt):
        kind = "ExternalOutput" if name in dbg else "Internal"
        return nc.dram_tensor(name, list(shape), dt, kind=kind).ap()

    x_in = din("x", [NTOK, D])
    ctx_in = din("ctx", [NCTX, D])
    ccT_in = din("ccT", [128, 32])
    w_ada = din("w_ada", [DEPTH, D, 6 * D])
    w_in = din("w_in", [DEPTH, D, 2 * D])
    w_out = din("w_out", [DEPTH, D, D])
    w_up = din("w_up", [DEPTH, D, 2 * DFF])
    w_down = din("w_down", [DEPTH, DFF, D])
    vecs_in = din("vecs", [DEPTH, 128, 512])
    wfour_in = din("wfour", [DEPTH, 8, 128, 128])
    tab_in = din("tab", [DEPTH, 16, 128, TABW])
    dftc_in = din("dftc", [NTOK, NTOK], BF16)
    dfts_in = din("dfts", [NTOK, NTOK], BF16)
    dft256_in = din("dft256", [2, NCTX, NCTX], BF16)
    cc128_in = din("cc128", [2, 128, 128], BF16)
    ident_in = din("ident", [128, 128])
    onesb_in = din("onesb", [128, 128], BF16)
    out_d = nc.dram_tensor("out", [NTOK, D], F32, kind="ExternalOutput").ap()

    winb = [scratch(f"winb{l}", [D, 2 * D], BF16) for l in range(DEPTH)]
    woutb = [scratch(f"woutb{l}", [D, D], BF16) for l in range(DEPTH)]
    wupT = [scratch(f"wupT{l}", [NPAIR, 128, 16, 256], BF16) for l in range(DEPTH)]
    wdnT = [scratch(f"wdnT{l}", [16, 128, NPAIR, 128], BF16) for l in range(DEPTH)]

    def mkseg(name, n, r):
        s = Seg()
        s.name = name
        s.n = n
        s.r = r
        s.xT = [scratch(f"{name}_xT{i}", [16, 128, n], F32) for i in range(2)]
        s.qkT = scratch(f"{name}_qkT", [16, 128, n], BF16)
        s.vt = scratch(f"{name}_vt", [8, 128, n // 128, 128], BF16)
        s.ftm = scratch(f"{name}_ftm", [n, 1024], BF16)
        s.mixT = scratch(f"{name}_mixT", [16, 128, n], BF16)
        return s

    SM = mkseg("m", NTOK, 0)
    SC = mkseg("c", NCTX, 1)

    S = Sched(nc)
    conv_grp = {}

    pers = {}

    def palloc(name, shape, dt):
        t = nc.alloc_sbuf_tensor(name, list(shape), dt)
        pers[name] = t
        return t

    ident = palloc("ident_s", [128, 128], F32)
    onesb = palloc("onesb_s", [128, 128], BF16)
    epst = palloc("eps_s", [128, 1], F32)
    vecs = [palloc(f"vecs{l}", [128, 512], F32) for l in range(DEPTH)]
    modsT = [palloc(f"modsT{l}", [128, 96, 2], F32) for l in range(DEPTH)]
    der = [{k: palloc(f"der{l}_{k}", [128, 16, 2], F32) for k in ("Aa", "GA", "Af", "GF")} for l in range(DEPTH)]
    ps = [nc.alloc_psum_tensor(f"psb{i}", [128, 512], F32) for i in range(8)]

    V_GPM, V_GPOM, V_GPF, V_GPOF, V_BADA, V_CW0, V_CW1, V_CW2, V_CB = 0, 16, 32, 48, 64, 160, 248, 336, 424

    state = {"done": False}

    def end_phase(name):
        S.emit_phase()
        if stop_after is not None and name == stop_after:
            state["done"] = True
        return state["done"]

    def PSB():
        return bufs("ps", 8, excl=True)

    def convert_all():
        for l in range(DEPTH):
            key = f"cv_win{l}"
            for i in range(4):
                S.op("pool", lambda e, l=l, i=i: e.dma_start(out=winb[l][i * 512:(i + 1) * 512, :], in_=w_in[l, i * 512:(i + 1) * 512, :]),
                     dma=key, nobarrier=True)
            conv_grp[key] = S.groups[key]
            key = f"cv_wout{l}"
            for i in range(2):
                S.op("pool", lambda e, l=l, i=i: e.dma_start(out=woutb[l][i * 1024:(i + 1) * 1024, :], in_=w_out[l, i * 1024:(i + 1) * 1024, :]),
                     dma=key, nobarrier=True)
            conv_grp[key] = S.groups[key]
            key = f"cv_wup{l}"
            for i in range(NPAIR):
                for half in range(2):
                    src = w_up[l, :, half * DFF + i * 128: half * DFF + (i + 1) * 128].rearrange("(kc p) e -> p kc e", p=128)
                    dst = wupT[l][i, :, :, half * 128:(half + 1) * 128]
                    S.op("pool", lambda e, src=src, dst=dst: e.dma_start(out=dst, in_=src), dma=key, nobarrier=True)
            conv_grp[key] = S.groups[key]
            key = f"cv_wdn{l}"
            for m in range(16):
                src = w_down[l, :, m * 128:(m + 1) * 128].rearrange("(kc p) e -> p kc e", p=128)
                dst = wdnT[l][m]
                S.op("pool", lambda e, src=src, dst=dst: e.dma_start(out=dst, in_=src), dma=key, nobarrier=True)
            conv_grp[key] = S.groups[key]

    def cwait(key):
        g = conv_grp[key]
        return [(g.sem, g.count)]

    def phase_ada(l, first):
        PS = PSB()
        with (nc.sbuf_tensor(f"ada_cc{l}", [128, 32], F32) as cc_s,
              nc.sbuf_tensor(f"ada_sT{l}", [128, 16, 2], BF16) as sT,
              nc.sbuf_tensor(f"ada_w{l}", [128, 3, 16, 512], BF16) as wad,
              nc.sbuf_tensor(f"ada_mrow{l}", [2, 6 * D], F32) as mrow):
            Bc = Buf("cc"); BsT = Buf("sT"); BW = bufs("wad", 3); BM = bufs("mrow", 24)
            Bconst = Buf("const"); Bvec = Buf("vecs"); Bmods = Buf("mods"); Bder = Buf("der")
            if first:
                S.op("sp", lambda e: e.dma_start(out=ident[:], in_=ident_in), writes=[Bconst], dma="ld0")
                S.op("sp", lambda e: e.dma_start(out=onesb[:], in_=onesb_in), writes=[Bconst], dma="ld0")
                S.op("dve", lambda e: e.memset(epst[:], EPS), writes=[Bconst])
                for ll in range(DEPTH):
                    S.op("sp", lambda e, ll=ll: e.dma_start(out=vecs[ll][:], in_=vecs_in[ll]), writes=[Bvec], dma="ld0")
            S.op("sp", lambda e: e.dma_start(out=cc_s[:], in_=ccT_in), writes=[Bc], dma="ld0")
            S.op("act", lambda e: e.activation(out=sT[:].rearrange("p k r -> p (k r)"), in_=cc_s[:], func=AF.Silu), reads=[Bc], writes=[BsT])
            for nt in range(24):
                sl = nt % 3
                src = w_ada[l, :, nt * 512:(nt + 1) * 512].rearrange("(kc p) n -> p kc n", p=128)
                S.op("pool", lambda e, sl=sl, src=src: e.dma_start(out=wad[:, sl], in_=src), writes=[BW[sl]], dma=f"wad{sl}")
                b = nt % 4
                for kc in range(16):
                    S.op("pe", lambda e, b=b, kc=kc, sl=sl: e.matmul(ps[b][0:2, :], lhsT=sT[:, kc, :], rhs=wad[:, sl, kc, :], start=(kc == 0), stop=(kc == 15)),
                         reads=[BsT, BW[sl]], writes=[PS[b]])
                if nt % 2 == 0:
                    S.op("act", lambda e, b=b, nt=nt: e.activation(out=mrow[0:2, nt * 512:(nt + 1) * 512], in_=ps[b][0:2, :], func=AF.Copy), reads=[PS[b]], writes=[BM[nt]])
                else:
                    S.op("dve", lambda e, b=b, nt=nt: e.tensor_copy(out=mrow[0:2, nt * 512:(nt + 1) * 512], in_=ps[b][0:2, :]), reads=[PS[b]], writes=[BM[nt]])
            if first:
                convert_all()
            for c in range(96):
                S.op("pe", lambda e, c=c: e.transpose(ps[4][:, 2 * c:2 * c + 2], mrow[0:2, c * 128:(c + 1) * 128], ident[0:2, 0:2]),
                     reads=[BM[c // 4], Bconst], writes=[PS[4]])
            S.op("dve", lambda e: e.tensor_tensor(out=modsT[l][:], in0=ps[4][:, 0:192].rearrange("p (c r) -> p c r", r=2),
                                                  in1=vecs[l][:, V_BADA:V_BADA + 96].unsqueeze(2).to_broadcast([128, 96, 2]), op=ALU.add),
                 reads=[PS[4], Bvec], writes=[Bmods])

            def bc(col):
                return vecs[l][:, col:col + 16].unsqueeze(2).to_broadcast([128, 16, 2])
            S.op("dve", lambda e: e.scalar_tensor_tensor(out=der[l]["Aa"][:], in0=modsT[l][:, 16:32, :], scalar=1.0, in1=bc(V_GPM), op0=ALU.add, op1=ALU.mult),
                 reads=[Bmods, Bvec], writes=[Bder])
            S.op("dve", lambda e: e.tensor_tensor(out=der[l]["GA"][:], in0=modsT[l][:, 32:48, :], in1=bc(V_GPOM), op=ALU.mult), reads=[Bmods, Bvec], writes=[Bder])
            S.op("dve", lambda e: e.scalar_tensor_tensor(out=der[l]["Af"][:], in0=modsT[l][:, 64:80, :], scalar=1.0, in1=bc(V_GPF), op0=ALU.add, op1=ALU.mult),
                 reads=[Bmods, Bvec], writes=[Bder])
            S.op("dve", lambda e: e.tensor_tensor(out=der[l]["GF"][:], in0=modsT[l][:, 80:96, :], in1=bc(V_GPOF), op=ALU.mult), reads=[Bmods, Bvec], writes=[Bder])
            if "dbg_mods" in dbg:
                dm = nc.dram_tensor(f"dbg_mods{l}", [128, 96, 2], F32, kind="ExternalOutput").ap()
                S.op("sp", lambda e: e.dma_start(out=dm, in_=modsT[l][:]), reads=[Bmods], dma="dbgst")
            return end_phase(f"ADA{l}")

    def phase_tin():
        PS = PSB()
        with (nc.sbuf_tensor("tin_x", [128, 2, 4, D], F32) as X,
              nc.sbuf_tensor("tin_xt", [128, 2, 16, 512], F32) as XT):
            BX = bufs("X", 2)
            BXT = [bufs(f"XT{s}_", 16) for s in range(2)]
            cnt = 0
            g = 0
            for seg, src in ((SM, x_in), (SC, ctx_in)):
                NT = min(512, seg.n)
                J = NT // 128
                for t in range(seg.n // NT):
                    s = g % 2
                    S.op("sp", lambda e, s=s, J=J, t=t, NT=NT, src=src: e.dma_start(out=X[:, s, 0:J, :], in_=src[t * NT:(t + 1) * NT, :].rearrange("(j p) f -> p j f", p=128)),
                         writes=[BX[s]], dma=f"tinx{s}")
                    for c in range(16):
                        b = cnt % 6
                        cnt += 1
                        for j in range(J):
                            S.op("pe", lambda e, b=b, j=j, s=s, c=c: e.transpose(ps[b][:, j * 128:(j + 1) * 128], X[:, s, j, c * 128:(c + 1) * 128], ident[:]),
                                 reads=[BX[s]], writes=[PS[b]])
                        if c % 2 == 0:
                            S.op("act", lambda e, b=b, s=s, c=c, NT=NT: e.activation(out=XT[:, s, c, 0:NT], in_=ps[b][:, 0:NT], func=AF.Copy), reads=[PS[b]], writes=[BXT[s][c]])
                        else:
                            S.op("dve", lambda e, b=b, s=s, c=c, NT=NT: e.tensor_copy(out=XT[:, s, c, 0:NT], in_=ps[b][:, 0:NT]), reads=[PS[b]], writes=[BXT[s][c]])
                    dst = seg.xT[0][:, :, t * NT:(t + 1) * NT].rearrange("c p t -> p c t")
                    S.op("act", lambda e, dst=dst, s=s, NT=NT: e.dma_start(out=dst, in_=XT[:, s, :, 0:NT]), reads=BXT[s], dma=f"tinst{s}")
                    g += 1
            return end_phase("TIN")

    def emit_rstd(PS, bank, rstd_ap, Brstd, ncol):
        S.op("act", lambda e: e.activation(out=rstd_ap, in_=ps[bank][:, 0:ncol], func=AF.Sqrt, scale=1.0 / D, bias=epst[:, 0:1]),
             reads=[PS[bank]], writes=[Brstd])
        S.op("dve", lambda e: e.reciprocal(out=rstd_ap, in_=rstd_ap), reads=[Brstd], writes=[Brstd])

    def phase_a(l, xi, segs):
        PS = PSB()
        with (nc.sbuf_tensor(f"a_xt{l}", [128, 16, 512], F32) as XT,
              nc.sbuf_tensor(f"a_sq{l}", [128, 2, 512], BF16) as SQ,
              nc.sbuf_tensor(f"a_ht{l}", [128, 2, 16, 512], BF16) as HT,
              nc.sbuf_tensor(f"a_w{l}", [128, 3, 16, 512], BF16) as W,
              nc.sbuf_tensor(f"a_qk{l}", [128, 2, 16, 512], BF16) as QK,
              nc.sbuf_tensor(f"a_vf{l}", [128, 2, 4, 2048], BF16) as VF,
              nc.sbuf_tensor(f"a_rstd{l}", [128, 512], F32) as rstd):
            BXT = bufs("XT", 16); BSQ = bufs("SQ", 2); BHT = [bufs(f"HT{s}_", 16) for s in range(2)]
            BW = bufs("W", 3); BQK = [bufs(f"QK{s}_", 16) for s in range(2)]
            BVF = [[Buf(f"VF{s}_{j}_{n}") for j in range(4) for n in range(4)] for s in range(2)]
            Brstd = Buf("rstd")
            widx = 0
            ecnt = 0
            g = 0
            bcnt = 0
            for seg in segs:
                NT = min(512, seg.n)
                J = NT // 128
                r = seg.r
                for t in range(seg.n // NT):
                    s = g % 2
                    src = seg.xT[xi][:, :, t * NT:(t + 1) * NT].rearrange("c p t -> p c t")
                    S.op("sp", lambda e, src=src, NT=NT: e.dma_start(out=XT[:, :, 0:NT], in_=src), writes=BXT, dma="a_xt")
                    for c in range(16):
                        S.op("act", lambda e, c=c, NT=NT: e.activation(out=SQ[:, c % 2, 0:NT], in_=XT[:, c, 0:NT], func=AF.Square), reads=[BXT[c]], writes=[BSQ[c % 2]])
                        S.op("pe", lambda e, c=c, NT=NT: e.matmul(ps[7][:, 0:NT], lhsT=onesb[:], rhs=SQ[:, c % 2, 0:NT], start=(c == 0), stop=(c == 15)),
                             reads=[BSQ[c % 2]], writes=[PS[7]])
                    emit_rstd(PS, 7, rstd[:, 0:NT], Brstd, NT)
                    for c in range(16):
                        S.op("dve", lambda e, c=c, NT=NT, r=r: e.scalar_tensor_tensor(out=XT[:, c, 0:NT], in0=XT[:, c, 0:NT], scalar=der[l]["Aa"][:, c, r:r + 1],
                                                                                   in1=rstd[:, 0:NT], op0=ALU.mult, op1=ALU.mult),
                             reads=[BXT[c], Brstd], writes=[BXT[c]])
                        S.op("act", lambda e, c=c, NT=NT, r=r, s=s: e.activation(out=HT[:, s, c, 0:NT], in_=XT[:, c, 0:NT], func=AF.Identity,
                                                                             bias=modsT[l][:, c, r:r + 1]),
                             reads=[BXT[c]], writes=[BHT[s][c]])
                    for i in range(8):
                        sl = widx % 3
                        widx += 1
                        wsrc = winb[l][:, i * 512:(i + 1) * 512].rearrange("(kc p) n -> p kc n", p=128)
                        S.op("sp", lambda e, sl=sl, wsrc=wsrc: e.dma_start(out=W[:, sl], in_=wsrc), writes=[BW[sl]], dma=f"a_w{sl}", extra=cwait(f"cv_win{l}"))
                        if i < 4:
                            for mm in range(4):
                                m = 4 * i + mm
                                b = bcnt % 6
                                bcnt += 1
                                for kc in range(16):
                                    S.op("pe", lambda e, b=b, sl=sl, kc=kc, mm=mm, s=s, NT=NT: e.matmul(ps[b][:, 0:NT], lhsT=W[:, sl, kc, mm * 128:(mm + 1) * 128], rhs=HT[:, s, kc, 0:NT],
                                                                                                  start=(kc == 0), stop=(kc == 15)),
                                         reads=[BW[sl], BHT[s][kc]], writes=[PS[b]])
                                if ecnt % 2 == 0:
                                    S.op("act", lambda e, b=b, s=s, m=m, NT=NT: e.activation(out=QK[:, s, m, 0:NT], in_=ps[b][:, 0:NT], func=AF.Copy), reads=[PS[b]], writes=[BQK[s][m]])
                                else:
                                    S.op("dve", lambda e, b=b, s=s, m=m, NT=NT: e.tensor_copy(out=QK[:, s, m, 0:NT], in_=ps[b][:, 0:NT]), reads=[PS[b]], writes=[BQK[s][m]])
                                ecnt += 1
                        else:
                            nb = i - 4
                            for j in range(J):
                                b = bcnt % 6
                                bcnt += 1
                                for kc in range(16):
                                    S.op("pe", lambda e, b=b, sl=sl, kc=kc, j=j, s=s: e.matmul(ps[b][:], lhsT=HT[:, s, kc, j * 128:(j + 1) * 128], rhs=W[:, sl, kc, :],
                                                                                         start=(kc == 0), stop=(kc == 15)),
                                         reads=[BW[sl], BHT[s][kc]], writes=[PS[b]])
                                if ecnt % 2 == 0:
                                    S.op("act", lambda e, b=b, s=s, j=j, nb=nb: e.activation(out=VF[:, s, j, nb * 512:(nb + 1) * 512], in_=ps[b][:], func=AF.Copy), reads=[PS[b]], writes=[BVF[s][j * 4 + nb]])
                                else:
                                    S.op("dve", lambda e, b=b, s=s, j=j, nb=nb: e.tensor_copy(out=VF[:, s, j, nb * 512:(nb + 1) * 512], in_=ps[b][:]), reads=[PS[b]], writes=[BVF[s][j * 4 + nb]])
                                ecnt += 1
                    dst = seg.qkT[:, :, t * NT:(t + 1) * NT].rearrange("c p t -> p c t")
                    S.op("act", lambda e, dst=dst, s=s, NT=NT: e.dma_start(out=dst, in_=QK[:, s, :, 0:NT]), reads=BQK[s], dma=f"a_st{s}")
                    for j in range(J):
                        dstv = seg.vt[:, :, t * J + j, :].rearrange("h p c -> p h c")
                        S.op("act", lambda e, dstv=dstv, s=s, j=j: e.dma_start(out=dstv, in_=VF[:, s, j, 0:1024].rearrange("p (h c) -> p h c", c=128)),
                             reads=BVF[s][j * 4:j * 4 + 2], dma=f"a_st{s}")
                    dstf = seg.ftm[t * NT:(t + 1) * NT, :].rearrange("(j p) c -> p j c", p=128)
                    S.op("act", lambda e, dstf=dstf, s=s, J=J: e.dma_start(out=dstf, in_=VF[:, s, 0:J, 1024:2048]), reads=BVF[s], dma=f"a_st{s}")
                    g += 1
            return end_phase(f"A{l}")

    def phase_b(l, with_ctx_queries):
        PS = PSB()
        with (nc.sbuf_tensor(f"b_q{l}", [128, 2, NTOK], BF16) as QT,
              nc.sbuf_tensor(f"b_qc{l}", [128, 2, NCTX], BF16) as QC,
              nc.sbuf_tensor(f"b_k{l}", [128, 2, NTOK + NCTX], BF16) as KT,
              nc.sbuf_tensor(f"b_vr{l}", [128, 2, 34, 128], BF16) as VR,
              nc.sbuf_tensor(f"b_va{l}", [128, 2, 2, 34, 128], BF16) as VA,
              nc.sbuf_tensor(f"b_ebf{l}", [128, 2, TABW], F32) as EBF,
              nc.sbuf_tensor(f"b_eb{l}", [128, 2, TABW], BF16) as EB,
              nc.sbuf_tensor(f"b_pt{l}", [128, 4, 256], BF16) as PT,
              nc.sbuf_tensor(f"b_rd{l}", [128, 2, 256], F32) as RD,
              nc.sbuf_tensor(f"b_at{l}", [128, 2, NTOK], BF16) as AT,
              nc.sbuf_tensor(f"b_atc{l}", [128, 2, NCTX], BF16) as ATC):
            BQ = bufs("Q", 2); BQC = bufs("QC", 2); BK = bufs("K", 2); BVR = bufs("VR", 2)
            BVA = [[Buf(f"VA{s}{h}") for h in range(2)] for s in range(2)]
            BEBF = bufs("EBF", 2); BEB = bufs("EB", 2); BPT = bufs("PT", 4); BRD = bufs("RD", 2)
            BAT = [[Buf(f"AT{s}_{q}") for q in range(16)] for s in range(2)]
            BATC = bufs("ATC", 2)
            Bones = Buf("ones")
            for s in range(2):
                S.op("dve", lambda e, s=s: e.memset(VA[:, s, 0, :, 64:128], 1.0), writes=[BVA[s][0]])
                S.op("dve", lambda e, s=s: e.memset(VA[:, s, 1, :, 0:64], 1.0), writes=[BVA[s][1]])
            ebi = 0
            ptc = 0
            stc = 0
            acc_i = 0
            rdc = 0
            for hp in range(8):
                s = hp % 2
                S.op("sp", lambda e, s=s, hp=hp: e.dma_start(out=QT[:, s, :], in_=SM.qkT[hp]), writes=[BQ[s]], dma=f"b_ld{s}")
                S.op("sp", lambda e, s=s, hp=hp: e.dma_start(out=KT[:, s, 0:NTOK], in_=SM.qkT[8 + hp]), writes=[BK[s]], dma=f"b_ld{s}")
                S.op("sp", lambda e, s=s, hp=hp: e.dma_start(out=KT[:, s, NTOK:NTOK + NCTX], in_=SC.qkT[8 + hp]), writes=[BK[s]], dma=f"b_ld{s}")
                S.op("sp", lambda e, s=s, hp=hp: e.dma_start(out=VR[:, s, 0:32, :], in_=SM.vt[hp]), writes=[BVR[s]], dma=f"b_ld{s}")
                S.op("sp", lambda e, s=s, hp=hp: e.dma_start(out=VR[:, s, 32:34, :], in_=SC.vt[hp]), writes=[BVR[s]], dma=f"b_ld{s}")
                if with_ctx_queries:
                    S.op("sp", lambda e, s=s, hp=hp: e.dma_start(out=QC[:, s, :], in_=SC.qkT[hp]), writes=[BQC[s]], dma=f"b_ld{s}")
                S.op("dve", lambda e, s=s: e.tensor_copy(out=VA[:, s, 0, :, 0:64], in_=VR[:, s, :, 0:64]), reads=[BVR[s]], writes=[BVA[s][0]])
                S.op("act", lambda e, s=s: e.activation(out=VA[:, s, 1, :, 64:128], in_=VR[:, s, :, 64:128], func=AF.Copy), reads=[BVR[s]], writes=[BVA[s][1]])
                steps = []
                for hh in range(2):
                    h = 2 * hp + hh
                    es = ebi % 2
                    ebi += 1
                    S.op("sp", lambda e, es=es, h=h: e.dma_start(out=EBF[:, es, :], in_=tab_in[l, h]), writes=[BEBF[es]], dma=f"b_eb{es}")
                    S.op("act", lambda e, es=es: e.activation(out=EB[:, es, :], in_=EBF[:, es, :], func=AF.Exp), reads=[BEBF[es]], writes=[BEB[es]])
                    qbs = list(range(16)) + ([16] if with_ctx_queries else [])
                    for qb in qbs:
                        if qb == 16:
                            lst = [(32, None), (33, None)]
                        else:
                            if qb == 0:
                                loc = [(k, 1536 + k * 256) for k in range(4)]
                            elif qb == 15:
                                loc = [(28 + k, 2560 + k * 256) for k in range(4)]
                            else:
                                loc = [(2 * qb - 2 + k, k * 256) for k in range(6)]
                            lst = loc + [(32, None), (33, None)]
                        for k, (chunk, mcol) in enumerate(lst):
                            steps.append(dict(hh=hh, es=es, qb=qb, chunk=chunk, mcol=mcol, first=(k == 0), last=(k == len(lst) - 1)))
                hr = [slice(0, 64), slice(64, 128)]

                def emit_qk(st):
                    nonlocal stc
                    b = stc % 3
                    stc += 1
                    st["b"] = b
                    hh = st["hh"]
                    if st["qb"] == 16:
                        rhs = QC[hr[hh], s, :]
                        rb = BQC[s]
                    else:
                        rhs = QT[hr[hh], s, st["qb"] * 256:(st["qb"] + 1) * 256]
                        rb = BQ[s]
                    ch = st["chunk"]
                    S.op("pe", lambda e, b=b, hh=hh, ch=ch, rhs=rhs, s=s: e.matmul(ps[b][:, 0:256], lhsT=KT[hr[hh], s, ch * 128:(ch + 1) * 128], rhs=rhs, start=True, stop=True),
                         reads=[BK[s], rb], writes=[PS[b]])

                def emit_sm(st):
                    nonlocal ptc
                    p = ptc % 4
                    ptc += 1
                    st["p"] = p
                    b = st["b"]
                    S.op("act", lambda e, b=b, p=p: e.activation(out=PT[:, p, :], in_=ps[b][:, 0:256], func=AF.Exp, scale=SCALE), reads=[PS[b]], writes=[BPT[p]])
                    if st["mcol"] is not None:
                        mc = st["mcol"]
                        es = st["es"]
                        S.op("dve", lambda e, p=p, mc=mc, es=es: e.tensor_tensor(out=PT[:, p, :], in0=PT[:, p, :], in1=EB[:, es, mc:mc + 256], op=ALU.mult),
                             reads=[BPT[p], BEB[es]], writes=[BPT[p]])

                def emit_pv(st):
                    nonlocal acc_i, rdc
                    hh = st["hh"]
                    if st["first"]:
                        acc_i += 1
                    ab = 3 + (acc_i % 2)
                    p = st["p"]
                    ch = st["chunk"]
                    S.op("pe", lambda e, ab=ab, hh=hh, ch=ch, p=p, st=st, s=s: e.matmul(ps[ab][:, 0:256], lhsT=VA[:, s, hh, ch, :], rhs=PT[:, p, :], start=st["first"], stop=st["last"]),
                         reads=[BVA[s][hh], BPT[p]], writes=[PS[ab]])
                    if st["last"]:
                        rs = rdc % 2
                        rdc += 1
                        num = hr[hh]
                        den = hr[1 - hh]
                        qb = st["qb"]
                        S.op("dve", lambda e, ab=ab, rs=rs, num=num, den=den: e.reciprocal(out=RD[num, rs, :], in_=ps[ab][den, 0:256]), reads=[PS[ab]], writes=[BRD[rs]])
                        if qb == 16:
                            S.op("dve", lambda e, ab=ab, rs=rs, num=num, s=s: e.tensor_tensor(out=ATC[num, s, :], in0=ps[ab][num, 0:256], in1=RD[num, rs, :], op=ALU.mult),
                                 reads=[PS[ab], BRD[rs]], writes=[BATC[s]])
                        else:
                            S.op("dve", lambda e, ab=ab, rs=rs, num=num, qb=qb, s=s: e.tensor_tensor(out=AT[num, s, qb * 256:(qb + 1) * 256], in0=ps[ab][num, 0:256], in1=RD[num, rs, :], op=ALU.mult),
                                 reads=[PS[ab], BRD[rs]], writes=[BAT[s][qb]])

                n = len(steps)
                LA = 2
                for i in range(min(LA, n)):
                    emit_qk(steps[i])
                for i in range(n):
                    emit_sm(steps[i])
                    if i + LA < n:
                        emit_qk(steps[i + LA])
                    emit_pv(steps[i])
                S.op("act", lambda e, s=s, hp=hp: e.dma_start(out=SM.mixT[hp], in_=AT[:, s, :]), reads=BAT[s], dma=f"b_st{s}")
                if with_ctx_queries:
                    S.op("act", lambda e, s=s, hp=hp: e.dma_start(out=SC.mixT[hp], in_=ATC[:, s, :]), reads=[BATC[s]], dma=f"b_st{s}")
            return end_phase(f"B{l}")

    def phase_c(l, segs):
        PS = PSB()
        with (nc.sbuf_tensor(f"c_f{l}", [128, 32, 1024], BF16) as Ft,
              nc.sbuf_tensor(f"c_d{l}", [128, 2, 32, 512], BF16) as DC,
              nc.sbuf_tensor(f"c_yc{l}", [128, 8, 512], BF16) as YC,
              nc.sbuf_tensor(f"c_ys{l}", [128, 2, 512], BF16) as YS,
              nc.sbuf_tensor(f"c_g{l}", [128, 2, 8, 128], BF16) as G,
              nc.sbuf_tensor(f"c_wf{l}", [128, 8, 128], F32) as WF,
              nc.sbuf_tensor(f"c_wfb{l}", [128, 8, 128], BF16) as WFB,
              nc.sbuf_tensor(f"c_cc{l}", [128, 2, 128], BF16) as CCt,
              nc.sbuf_tensor(f"c_fo{l}", [128, 2, 8, 512], BF16) as FO):
            BF_ = Buf("F"); BD = bufs("DC", 2); BYC = bufs("YC", 8); BYS = bufs("YS", 2); BG = Buf("G"); BWF = Buf("WF"); BWFB = Buf("WFB")
            BCC = Buf("CC"); BFO = [bufs(f"FO{s}_", 8) for s in range(2)]
            S.op("sp", lambda e: e.dma_start(out=WF[:], in_=wfour_in[l].rearrange("g c e -> c g e")), writes=[BWF], dma="c_ld")
            S.op("sp", lambda e: e.dma_start(out=CCt[:], in_=cc128_in.rearrange("t c e -> c t e")), writes=[BCC], dma="c_ld")
            S.op("act", lambda e: e.activation(out=WFB[:], in_=WF[:], func=AF.Copy), reads=[BWF], writes=[BWFB])
            bcnt = 0
            for tt in range(2):
                for g in range(8):
                    b = bcnt % 6
                    bcnt += 1
                    S.op("pe", lambda e, b=b, tt=tt, g=g: e.matmul(ps[b][:, 0:128], lhsT=CCt[:, tt, :], rhs=WFB[:, g, :], start=True, stop=True), reads=[BCC, BWFB], writes=[PS[b]])
                    S.op("dve", lambda e, b=b, tt=tt, g=g: e.tensor_copy(out=G[:, tt, g, :], in_=ps[b][:, 0:128]), reads=[PS[b]], writes=[BG])
            dci = 0
            fo_i = 0
            ecnt = 0
            for seg in segs:
                nch = seg.n // 128
                NT = min(512, seg.n)
                S.op("sp", lambda e, seg=seg, nch=nch: e.dma_start(out=Ft[:, 0:nch, :], in_=seg.ftm.rearrange("(nc p) c -> p nc c", p=128)), writes=[BF_], dma="c_ld")
                for t in range(seg.n // NT):
                    dsl = []
                    for tt in range(2):
                        ds = dci % 2
                        dci += 1
                        if seg.n == NTOK:
                            srcm = (dftc_in if tt == 0 else dfts_in)[:, t * NT:(t + 1) * NT]
                        else:
                            srcm = dft256_in[tt]
                        srcm = srcm.rearrange("(nc p) m -> p nc m", p=128)
                        S.op("sp", lambda e, ds=ds, srcm=srcm, nch=nch, NT=NT: e.dma_start(out=DC[:, ds, 0:nch, 0:NT], in_=srcm), writes=[BD[ds]], dma=f"c_d{ds}")
                        dsl.append(ds)
                    fs = fo_i % 2
                    fo_i += 1
                    for g in range(8):
                        b = bcnt % 6
                        bcnt += 1
                        ds = dsl[0]
                        for n_ in range(nch):
                            S.op("pe", lambda e, b=b, g=g, n_=n_, ds=ds, NT=NT, nch=nch: e.matmul(ps[b][:, 0:NT], lhsT=Ft[:, n_, g * 128:(g + 1) * 128], rhs=DC[:, ds, n_, 0:NT],
                                                                                            start=(n_ == 0), stop=(n_ == nch - 1)),
                                 reads=[BF_, BD[ds]], writes=[PS[b]])
                        if ecnt % 2 == 0:
                            S.op("act", lambda e, b=b, g=g, NT=NT: e.activation(out=YC[:, g, 0:NT], in_=ps[b][:, 0:NT], func=AF.Copy), reads=[PS[b]], writes=[BYC[g]])
                        else:
                            S.op("dve", lambda e, b=b, g=g, NT=NT: e.tensor_copy(out=YC[:, g, 0:NT], in_=ps[b][:, 0:NT]), reads=[PS[b]], writes=[BYC[g]])
                        ecnt += 1
                    pend = None
                    for g in range(8):
                        b = bcnt % 6
                        bcnt += 1
                        ds = dsl[1]
                        for n_ in range(nch):
                            S.op("pe", lambda e, b=b, g=g, n_=n_, ds=ds, NT=NT, nch=nch: e.matmul(ps[b][:, 0:NT], lhsT=Ft[:, n_, g * 128:(g + 1) * 128], rhs=DC[:, ds, n_, 0:NT],
                                                                                            start=(n_ == 0), stop=(n_ == nch - 1)),
                                 reads=[BF_, BD[ds]], writes=[PS[b]])
                        ys = g % 2
                        if ecnt % 2 == 0:
                            S.op("act", lambda e, b=b, ys=ys, NT=NT: e.activation(out=YS[:, ys, 0:NT], in_=ps[b][:, 0:NT], func=AF.Copy), reads=[PS[b]], writes=[BYS[ys]])
                        else:
                            S.op("dve", lambda e, b=b, ys=ys, NT=NT: e.tensor_copy(out=YS[:, ys, 0:NT], in_=ps[b][:, 0:NT]), reads=[PS[b]], writes=[BYS[ys]])
                        ecnt += 1

                        def stage2(g=g, ys=ys, NT=NT, fs=fs):
                            nonlocal ecnt
                            b2 = 6 + (g % 2)
                            S.op("pe", lambda e: e.matmul(ps[b2][:, 0:NT], lhsT=G[:, 0, g, :], rhs=YC[:, g, 0:NT], start=True, stop=False), reads=[BG, BYC[g]], writes=[PS[b2]])
                            S.op("pe", lambda e: e.matmul(ps[b2][:, 0:NT], lhsT=G[:, 1, g, :], rhs=YS[:, ys, 0:NT], start=False, stop=True), reads=[BG, BYS[ys]], writes=[PS[b2]])
                            if ecnt % 2 == 0:
                                S.op("act", lambda e: e.activation(out=FO[:, fs, g, 0:NT], in_=ps[b2][:, 0:NT], func=AF.Copy), reads=[PS[b2]], writes=[BFO[fs][g]])
                            else:
                                S.op("dve", lambda e: e.tensor_copy(out=FO[:, fs, g, 0:NT], in_=ps[b2][:, 0:NT]), reads=[PS[b2]], writes=[BFO[fs][g]])
                            ecnt += 1
                        if pend is not None:
                            pend()
                        pend = stage2
                    pend()
                    dst = seg.mixT[8:16, :, t * NT:(t + 1) * NT].rearrange("c p t -> p c t")
                    S.op("act", lambda e, dst=dst, fs=fs, NT=NT: e.dma_start(out=dst, in_=FO[:, fs, :, 0:NT]), reads=BFO[fs], dma=f"c_st{fs}")
            return end_phase(f"C{l}")

    def phase_d(l, xi, segs):
        PS = PSB()
        with (nc.sbuf_tensor(f"d_w{l}", [128, 16, D], BF16) as W,
              nc.sbuf_tensor(f"d_mt{l}", [128, 2, 16, 512], BF16) as MT,
              nc.sbuf_tensor(f"d_xt{l}", [128, 16, 512], F32) as XT,
              nc.sbuf_tensor(f"d_y{l}", [128, 16, 512], F32) as Y,
              nc.sbuf_tensor(f"d_sq{l}", [128, 2, 512], BF16) as SQ,
              nc.sbuf_tensor(f"d_rstd{l}", [128, 512], F32) as rstd):
            BW = Buf("W"); BMT = bufs("MT", 2); BXT = bufs("XT", 16); BY = bufs("Y", 16); BSQ = bufs("SQ", 2); Brstd = Buf("rstd")
            for i in range(4):
                S.op("sp", lambda e, i=i: e.dma_start(out=W[:, 4 * i:4 * i + 4, :], in_=woutb[l][i * 512:(i + 1) * 512, :].rearrange("(kc p) n -> p kc n", p=128)),
                     writes=[BW], dma="d_w", extra=cwait(f"cv_wout{l}"))
            g = 0
            bcnt = 0
            for seg in segs:
                NT = min(512, seg.n)
                r = seg.r
                for t in range(seg.n // NT):
                    s = g % 2
                    g += 1
                    S.op("sp", lambda e, seg=seg, t=t, NT=NT, s=s: e.dma_start(out=MT[:, s, :, 0:NT], in_=seg.mixT[:, :, t * NT:(t + 1) * NT].rearrange("c p t -> p c t")),
                         writes=[BMT[s]], dma=f"d_mt{s}")
                    xsrc = seg.xT[xi][:, :, t * NT:(t + 1) * NT].rearrange("c p t -> p c t")
                    S.op("sp", lambda e, xsrc=xsrc, NT=NT: e.dma_start(out=XT[:, :, 0:NT], in_=xsrc), writes=BXT, dma="d_xt")
                    pend = None
                    for m in range(16):
                        b = bcnt % 6
                        bcnt += 1
                        for kc in range(16):
                            S.op("pe", lambda e, b=b, kc=kc, m=m, s=s, NT=NT: e.matmul(ps[b][:, 0:NT], lhsT=W[:, kc, m * 128:(m + 1) * 128], rhs=MT[:, s, kc, 0:NT],
                                                                                 start=(kc == 0), stop=(kc == 15)),
                                 reads=[BW, BMT[s]], writes=[PS[b]])
                        S.op("act", lambda e, b=b, m=m, NT=NT: e.activation(out=SQ[:, m % 2, 0:NT], in_=ps[b][:, 0:NT], func=AF.Square), reads=[PS[b]], writes=[BSQ[m % 2]])
                        S.op("dve", lambda e, b=b, m=m, NT=NT: e.tensor_copy(out=Y[:, m, 0:NT], in_=ps[b][:, 0:NT]), reads=[PS[b]], writes=[BY[m]])

                        def ssmm(m=m, NT=NT):
                            S.op("pe", lambda e: e.matmul(ps[7][:, 0:NT], lhsT=onesb[:], rhs=SQ[:, m % 2, 0:NT], start=(m == 0), stop=(m == 15)), reads=[BSQ[m % 2]], writes=[PS[7]])
                        if pend is not None:
                            pend()
                        pend = ssmm
                    pend()
                    emit_rstd(PS, 7, rstd[:, 0:NT], Brstd, NT)
                    for m in range(16):
                        S.op("dve", lambda e, m=m, NT=NT, r=r: e.scalar_tensor_tensor(out=Y[:, m, 0:NT], in0=Y[:, m, 0:NT], scalar=der[l]["GA"][:, m, r:r + 1], in1=rstd[:, 0:NT],
                                                                                   op0=ALU.mult, op1=ALU.mult), reads=[BY[m], Brstd], writes=[BY[m]])
                        S.op("dve", lambda e, m=m, NT=NT: e.tensor_tensor(out=XT[:, m, 0:NT], in0=XT[:, m, 0:NT], in1=Y[:, m, 0:NT], op=ALU.add), reads=[BXT[m], BY[m]], writes=[BXT[m]])
                    S.op("act", lambda e, xsrc=xsrc, NT=NT: e.dma_start(out=xsrc, in_=XT[:, :, 0:NT]), reads=BXT, dma="d_st")
            return end_phase(f"D{l}")

    def phase_ef(l, xi, segs):
        PS = PSB()
        NUMAX = 460
        with (nc.sbuf_tensor(f"e_xt{l}", [128, 16, NUMAX], F32) as XT,
              nc.sbuf_tensor(f"e_sq{l}", [128, 2, NUMAX], BF16) as SQ,
              nc.sbuf_tensor(f"e_ht{l}", [128, 16, NUMAX], BF16) as HT,
              nc.sbuf_tensor(f"e_act{l}", [128, NPAIR, 456], BF16) as ACTT,
              nc.sbuf_tensor(f"e_wu{l}", [128, 3, 16, 256], BF16) as WU,
              nc.sbuf_tensor(f"e_wd{l}", [128, 2, NPAIR, 128], BF16) as WD,
              nc.sbuf_tensor(f"e_ta{l}", [128, 2, 456], F32) as TA,
              nc.sbuf_tensor(f"e_tg{l}", [128, 2, 456], F32) as TG,
              nc.sbuf_tensor(f"e_sg{l}", [128, 2, 456], F32) as SG,
              nc.sbuf_tensor(f"e_y{l}", [128, 16, NUMAX], F32) as Y,
              nc.sbuf_tensor(f"e_rstd{l}", [128, NUMAX], F32) as rstd):
            BXT = bufs("XT", 16); BSQ = bufs("SQ", 2); BHT = bufs("HT", 16); BACT = bufs("ACT", NPAIR)
            BWU = bufs("WU", 3); BWD = bufs("WD", 2); BTA = bufs("TA", 2); BTG = bufs("TG", 2); BSG = bufs("SG", 2)
            BY = bufs("Y", 16); Brstd = Buf("rstd")
            wu_i = 0
            wd_i = 0
            pi = 0
            dcnt = 0
            vv = vecs[l]
            V_CW0, V_CW1, V_CW2, V_CB = 160, 248, 336, 424
            for seg in segs:
                r = seg.r
                if seg.n == NTOK:
                    wins = [(456 * t, 456) for t in range(8)] + [(3648, 448)]
                else:
                    wins = [(0, seg.n)]
                for (lo, NO) in wins:
                    NU = NO + 2
                    a0 = max(lo - 1, 0)
                    a1 = min(lo + NO + 1, seg.n)
                    c0 = a0 - (lo - 1)
                    c1 = c0 + (a1 - a0)
                    if c0 > 0:
                        S.op("dve", lambda e, c0=c0: e.memset(XT[:, :, 0:c0], 0.0), writes=BXT)
                    if c1 < NU:
                        S.op("dve", lambda e, c1=c1, NU=NU: e.memset(XT[:, :, c1:NU], 0.0), writes=BXT)
                    xsrc = seg.xT[xi][:, :, a0:a1].rearrange("c p t -> p c t")
                    S.op("sp", lambda e, xsrc=xsrc, c0=c0, c1=c1: e.dma_start(out=XT[:, :, c0:c1], in_=xsrc), writes=BXT, dma="e_xt")
                    for c in range(16):
                        S.op("act", lambda e, c=c, NU=NU: e.activation(out=SQ[:, c % 2, 0:NU], in_=XT[:, c, 0:NU], func=AF.Square), reads=[BXT[c]], writes=[BSQ[c % 2]])
                        S.op("pe", lambda e, c=c, NU=NU: e.matmul(ps[7][:, 0:NU], lhsT=onesb[:], rhs=SQ[:, c % 2, 0:NU], start=(c == 0), stop=(c == 15)), reads=[BSQ[c % 2]], writes=[PS[7]])
                    emit_rstd(PS, 7, rstd[:, 0:NU], Brstd, NU)
                    for c in range(16):
                        S.op("dve", lambda e, c=c, NU=NU, r=r: e.scalar_tensor_tensor(out=Y[:, c, 0:NU], in0=XT[:, c, 0:NU], scalar=der[l]["Af"][:, c, r:r + 1], in1=rstd[:, 0:NU],
                                                                                   op0=ALU.mult, op1=ALU.mult), reads=[BXT[c], Brstd], writes=[BY[c]])
                        S.op("act", lambda e, c=c, NU=NU, r=r: e.activation(out=HT[:, c, 0:NU], in_=Y[:, c, 0:NU], func=AF.Identity, bias=modsT[l][:, 48 + c, r:r + 1]),
                             reads=[BY[c]], writes=[BHT[c]])
                    if c0 > 0:
                        S.op("dve", lambda e, c0=c0: e.memset(HT[:, :, 0:c0], 0.0), writes=BHT)
                    if c1 < NU:
                        S.op("dve", lambda e, c1=c1, NU=NU: e.memset(HT[:, :, c1:NU], 0.0), writes=BHT)
                    for i in range(NPAIR):
                        sl = wu_i % 3
                        wu_i += 1
                        S.op("sp", lambda e, sl=sl, i=i: e.dma_start(out=WU[:, sl], in_=wupT[l][i]), writes=[BWU[sl]], dma=f"e_wu{sl}", extra=cwait(f"cv_wup{l}"))
                        ba = 2 * (pi % 2)
                        bg = ba + 1
                        ts = pi % 2
                        pi += 1
                        for kc in range(16):
                            S.op("pe", lambda e, ba=ba, sl=sl, kc=kc, NU=NU: e.matmul(ps[ba][:, 0:NU], lhsT=WU[:, sl, kc, 0:128], rhs=HT[:, kc, 0:NU], start=(kc == 0), stop=(kc == 15)),
                                 reads=[BWU[sl], BHT[kc]], writes=[PS[ba]])
                        for kc in range(16):
                            S.op("pe", lambda e, bg=bg, sl=sl, kc=kc, NU=NU: e.matmul(ps[bg][:, 0:NU], lhsT=WU[:, sl, kc, 128:256], rhs=HT[:, kc, 0:NU], start=(kc == 0), stop=(kc == 15)),
                                 reads=[BWU[sl], BHT[kc]], writes=[PS[bg]])
                        for (bk, T, BT, ci) in ((ba, TA, BTA, i), (bg, TG, BTG, NPAIR + i)):
                            S.op("act", lambda e, bk=bk, T=T, ts=ts, ci=ci, NO=NO: e.activation(out=T[:, ts, 0:NO], in_=ps[bk][:, 1:1 + NO], func=AF.Identity,
                                                                                           scale=vv[:, V_CW1 + ci:V_CW1 + ci + 1], bias=vv[:, V_CB + ci:V_CB + ci + 1]),
                                 reads=[PS[bk]], writes=[BT[ts]])
                            S.op("dve", lambda e, bk=bk, T=T, ts=ts, ci=ci, NO=NO: e.scalar_tensor_tensor(out=T[:, ts, 0:NO], in0=ps[bk][:, 0:NO], scalar=vv[:, V_CW0 + ci:V_CW0 + ci + 1],
                                                                                                     in1=T[:, ts, 0:NO], op0=ALU.mult, op1=ALU.add),
                                 reads=[PS[bk], BT[ts]], writes=[BT[ts]])
                            S.op("dve", lambda e, bk=bk, T=T, ts=ts, ci=ci, NO=NO: e.scalar_tensor_tensor(out=T[:, ts, 0:NO], in0=ps[bk][:, 2:2 + NO], scalar=vv[:, V_CW2 + ci:V_CW2 + ci + 1],
                                                                                                     in1=T[:, ts, 0:NO], op0=ALU.mult, op1=ALU.add),
                                 reads=[PS[bk], BT[ts]], writes=[BT[ts]])
                        S.op("act", lambda e, ts=ts, NO=NO: e.activation(out=SG[:, ts, 0:NO], in_=TG[:, ts, 0:NO], func=AF.Silu), reads=[BTG[ts]], writes=[BSG[ts]])
                        S.op("dve", lambda e, ts=ts, i=i, NO=NO: e.tensor_tensor(out=ACTT[:, i, 0:NO], in0=TA[:, ts, 0:NO], in1=SG[:, ts, 0:NO], op=ALU.mult),
                             reads=[BTA[ts], BSG[ts]], writes=[BACT[i]])
                    pend = None
                    for m in range(16):
                        sl = wd_i % 2
                        wd_i += 1
                        S.op("sp", lambda e, sl=sl, m=m: e.dma_start(out=WD[:, sl], in_=wdnT[l][m]), writes=[BWD[sl]], dma=f"e_wd{sl}", extra=cwait(f"cv_wdn{l}"))
                        b = 4 + (dcnt % 2)
                        dcnt += 1
                        for kc in range(NPAIR):
                            S.op("pe", lambda e, b=b, sl=sl, kc=kc, NO=NO: e.matmul(ps[b][:, 0:NO], lhsT=WD[:, sl, kc, :], rhs=ACTT[:, kc, 0:NO], start=(kc == 0), stop=(kc == NPAIR - 1)),
                                 reads=[BWD[sl], BACT[kc]], writes=[PS[b]])
                        S.op("act", lambda e, b=b, m=m, NO=NO: e.activation(out=SQ[:, m % 2, 0:NO], in_=ps[b][:, 0:NO], func=AF.Square), reads=[PS[b]], writes=[BSQ[m % 2]])
                        S.op("dve", lambda e, b=b, m=m, NO=NO: e.tensor_copy(out=Y[:, m, 0:NO], in_=ps[b][:, 0:NO]), reads=[PS[b]], writes=[BY[m]])

                        def ssmm(m=m, NO=NO):
                            S.op("pe", lambda e: e.matmul(ps[7][:, 0:NO], lhsT=onesb[:], rhs=SQ[:, m % 2, 0:NO], start=(m == 0), stop=(m == 15)), reads=[BSQ[m % 2]], writes=[PS[7]])
                        if pend is not None:
                            pend()
                        pend = ssmm
                    pend()
                    emit_rstd(PS, 7, rstd[:, 0:NO], Brstd, NO)
                    for m in range(16):
                        S.op("dve", lambda e, m=m, NO=NO, r=r: e.scalar_tensor_tensor(out=Y[:, m, 0:NO], in0=Y[:, m, 0:NO], scalar=der[l]["GF"][:, m, r:r + 1], in1=rstd[:, 0:NO],
                                                                                   op0=ALU.mult, op1=ALU.mult), reads=[BY[m], Brstd], writes=[BY[m]])
                        S.op("dve", lambda e, m=m, NO=NO: e.tensor_tensor(out=Y[:, m, 0:NO], in0=Y[:, m, 0:NO], in1=XT[:, m, 1:1 + NO], op=ALU.add), reads=[BXT[m], BY[m]], writes=[BY[m]])
                    dst = seg.xT[1 - xi][:, :, lo:lo + NO].rearrange("c p t -> p c t")
                    S.op("act", lambda e, dst=dst, NO=NO: e.dma_start(out=dst, in_=Y[:, :, 0:NO]), reads=BY, dma="e_st")
            return end_phase(f"EF{l}")

    def phase_tout(xi):
        PS = PSB()
        with (nc.sbuf_tensor("to_xt", [128, 2, 16, 512], F32) as XT,
              nc.sbuf_tensor("to_o", [128, 2, 4, D], F32) as O):
            BXT = bufs("XT", 2)
            BO = [[Buf(f"O{s}_{k}") for k in range(16)] for s in range(2)]
            cnt = 0
            for t in range(NTOK // 512):
                s = t % 2
                S.op("sp", lambda e, s=s, t=t: e.dma_start(out=XT[:, s], in_=SM.xT[xi][:, :, t * 512:(t + 1) * 512].rearrange("c p t -> p c t")), writes=[BXT[s]], dma=f"to_ld{s}")
                for j in range(4):
                    for cb in range(4):
                        b = cnt % 6
                        cnt += 1
                        for cc in range(4):
                            S.op("pe", lambda e, b=b, cc=cc, s=s, cb=cb, j=j: e.transpose(ps[b][:, cc * 128:(cc + 1) * 128], XT[:, s, 4 * cb + cc, j * 128:(j + 1) * 128], ident[:]),
                                 reads=[BXT[s]], writes=[PS[b]])
                        if cnt % 2 == 0:
                            S.op("act", lambda e, b=b, s=s, j=j, cb=cb: e.activation(out=O[:, s, j, cb * 512:(cb + 1) * 512], in_=ps[b][:], func=AF.Copy), reads=[PS[b]], writes=[BO[s][j * 4 + cb]])
                        else:
                            S.op("dve", lambda e, b=b, s=s, j=j, cb=cb: e.tensor_copy(out=O[:, s, j, cb * 512:(cb + 1) * 512], in_=ps[b][:]), reads=[PS[b]], writes=[BO[s][j * 4 + cb]])
                S.op("act", lambda e, s=s, t=t: e.dma_start(out=out_d[t * 512:(t + 1) * 512, :].rearrange("(j p) f -> p j f", p=128), in_=O[:, s]), reads=BO[s], dma=f"to_st{s}")
            return end_phase("TOUT")

    def run():
        if phase_ada(0, True):
            return
        if phase_tin():
            return
        xi = 0
        for l in range(DEPTH):
            if l > 0:
                if phase_ada(l, False):
                    return
            if phase_a(l, xi, [SM, SC]):
                return
            if phase_b(l, with_ctx_queries=(l == 0)):
                return
            segs = [SM, SC] if l == 0 else [SM]
            if phase_c(l, segs):
                return
            if phase_d(l, xi, segs):
                return
            if phase_ef(l, xi, segs):
                return
            xi = 1 - xi
        phase_tout(xi)

    run()
    S.final_wait()
    return nc, S


def _pack_vecs(g_pre_mix, g_post_mix, g_pre_ffn, g_post_ffn, b_ada, conv_w, conv_b):
    out = np.zeros((DEPTH, 128, 512), np.float32)
    for l in range(DEPTH):
        def cm(v):
            return np.ascontiguousarray(np.asarray(v, np.float32).reshape(-1, 128).T)
        out[l, :, 0:16] = cm(g_pre_mix[l])
        out[l, :, 16:32] = cm(g_post_mix[l])
        out[l, :, 32:48] = cm(g_pre_ffn[l])
        out[l, :, 48:64] = cm(g_post_ffn[l])
        out[l, :, 64:160] = cm(b_ada[l])
        out[l, :, 160:248] = cm(conv_w[l, 0])
        out[l, :, 248:336] = cm(conv_w[l, 1])
        out[l, :, 336:424] = cm(conv_w[l, 2])
        out[l, :, 424:512] = cm(conv_b[l])
    return out


def make_in_maps(inputs):
    C = _consts()
    x = np.asarray(inputs["x"], np.float32)
    c = np.asarray(inputs["c"], np.float32)
    ctx = np.asarray(inputs["ctx"], np.float32)
    c_ctx = np.asarray(inputs["c_ctx"], np.float32)
    rpb = np.asarray(inputs["rpb"], np.float32)
    dr, dc, ok = C["attn_idx"]
    tab = np.where(ok[None, None], rpb[:, :, dr, dc], np.float32(NEG)).astype(np.float32)
    vecs = _pack_vecs(inputs["g_pre_mix"], inputs["g_post_mix"], inputs["g_pre_ffn"], inputs["g_post_ffn"],
                      inputs["b_ada"], np.asarray(inputs["conv_w"], np.float32), inputs["conv_b"])
    shared = dict(
        w_ada=np.asarray(inputs["w_ada"], np.float32), w_in=np.asarray(inputs["w_in"], np.float32),
        w_out=np.asarray(inputs["w_out"], np.float32), w_up=np.asarray(inputs["w_up"], np.float32),
        w_down=np.asarray(inputs["w_down"], np.float32), vecs=vecs, wfour=np.asarray(inputs["w_four"], np.float32),
        tab=tab, dftc=C["dftc"], dfts=C["dfts"], dft256=C["dft256"], cc128=C["cc128"], ident=C["ident"], onesb=C["onesb"],
    )
    maps = []
    for b in range(NCORES):
        cc = np.stack([c[b], c_ctx])
        ccT = np.ascontiguousarray(cc.reshape(2, 16, 128).transpose(2, 1, 0).reshape(128, 32))
        m = dict(shared)
        m["x"] = np.ascontiguousarray(x[b])
        m["ctx"] = np.ascontiguousarray(ctx[b])
        m["ccT"] = ccT
        maps.append(m)
    return maps


_PROG = {}


def kernel(**inputs):
    if "nc" not in _PROG:
        _PROG["nc"], _ = build_program()
    nc = _PROG["nc"]
    maps = make_in_maps(inputs)
    res = run_bass_kernel_spmd(nc, maps, core_ids=list(range(NCORES)))
    return np.stack([np.asarray(r["out"], np.float32) for r in res.results], axis=0)
```

```python
import numpy as np
import ml_dtypes
import concourse.bass as bass
import concourse.mybir as mybir
from concourse.bass_utils import run_bass_kernel_spmd

F32 = mybir.dt.float32
BF16 = mybir.dt.bfloat16
ALU = mybir.AluOpType
AF = mybir.ActivationFunctionType

D = 2048
NTOK = 4096
NCTX = 256
DEPTH = 2
DFF = 5632
NPAIR = DFF // 128
EPS = 1e-6
SCALE = 0.125
NCORES = 8
TABW = 3584
NEG = -30000.0


class Buf:
    __slots__ = ("name", "last_w", "readers", "excl")

    def __init__(self, name, excl=False):
        self.name = name
        self.last_w = None
        self.readers = []
        self.excl = excl


def bufs(name, n, excl=False):
    return [Buf(f"{name}{i}", excl) for i in range(n)]


class DmaGroup:
    __slots__ = ("sem", "count", "name", "barrier")

    def __init__(self, sem, name, barrier=True):
        self.sem = sem
        self.count = 0
        self.name = name
        self.barrier = barrier


class Op:
    __slots__ = ("eng", "fn", "waits", "cdeps", "grp", "inc_val", "needed", "phase")


class Sched:
    ENGS = ("pe", "act", "dve", "pool", "sp")

    def __init__(self, nc):
        self.nc = nc
        self.esem = {}
        self.ecount = {e: 0 for e in self.ENGS}
        for e in self.ENGS:
            self.esem[e] = nc.semaphore(f"es_{e}").__enter__()
        self.groups = {}
        self.ops = []
        self.seen = {e: {} for e in self.ENGS}
        self.phase = 0
        self._barrier_vals = None
        self.nops_total = 0

    def group(self, key, barrier=True):
        g = self.groups.get(key)
        if g is None:
            sem = self.nc.semaphore(f"dg_{key}").__enter__()
            g = DmaGroup(sem, key, barrier)
            self.groups[key] = g
        return g

    def op(self, eng, fn, reads=(), writes=(), dma=None, extra=(), nobarrier=False):
        o = Op()
        o.eng = eng
        o.fn = fn
        o.inc_val = None
        o.needed = False
        o.grp = None
        o.phase = self.phase
        waits = {}
        cdeps = set()
        is_dma = dma is not None
        for sem, val in extra:
            waits[id(sem)] = [sem, val]

        def add_dep(d, kind):
            if d.phase != self.phase:
                return
            if d.grp is not None:
                g = d.grp
                cur = waits.get(id(g.sem))
                if cur is None or cur[1] < g.count:
                    waits[id(g.sem)] = [g.sem, g.count]
            else:
                if d.eng == eng and not is_dma:
                    if eng == "pe":
                        return
                    if kind != "raw":
                        return
                cdeps.add(d)

        rd = []
        wr = list(writes)
        for b in reads:
            if b.excl:
                wr.append(b)
            else:
                rd.append(b)
        for b in rd:
            if b.last_w is not None:
                add_dep(b.last_w, "raw")
        for b in wr:
            if b.last_w is not None:
                add_dep(b.last_w, "raw" if b.excl else "waw")
            for r in b.readers:
                add_dep(r, "war")
        for b in rd:
            if not is_dma:
                b.readers = [r_ for r_ in b.readers if not (r_.grp is None and r_.eng == eng)]
            b.readers.append(o)
        for b in wr:
            b.last_w = o
            b.readers = []
        for d in cdeps:
            d.needed = True
        o.waits = waits
        o.cdeps = cdeps
        if is_dma:
            g = self.group(dma, barrier=not nobarrier)
            g.count += 16
            o.grp = g
        self.ops.append(o)
        return o

    def emit_phase(self):
        nc = self.nc
        ops = self.ops
        for o in ops:
            if o.grp is None and o.needed:
                self.ecount[o.eng] += 1
                o.inc_val = self.ecount[o.eng]
        barrier = self._barrier_vals
        per_eng = {e: [o for o in ops if o.eng == e] for e in self.ENGS}

        def emit_stream(e, engobj):
            seen = self.seen[e]

            def wait(sem, val):
                if val <= 0:
                    return
                cur = seen.get(id(sem))
                if cur is not None and cur >= val:
                    return
                seen[id(sem)] = val
                engobj.wait_ge(sem, val)

            if barrier is not None and per_eng[e]:
                for sem, val in barrier:
                    if sem is self.esem[e]:
                        continue
                    wait(sem, val)
            for o in per_eng[e]:
                for sem, val in o.waits.values():
                    wait(sem, val)
                for d in o.cdeps:
                    wait(self.esem[d.eng], d.inc_val)
                ins = o.fn(engobj)
                if o.grp is not None:
                    ins.then_inc(o.grp.sem, 16)
                elif o.inc_val is not None:
                    ins.then_inc(self.esem[e], 1)

        with nc.Block() as block:
            if per_eng["sp"]:
                @block.sync
                def _(eng):
                    emit_stream("sp", eng)
            if per_eng["pe"]:
                @block.tensor
                def _(eng):
                    emit_stream("pe", eng)
            if per_eng["act"]:
                @block.scalar
                def _(eng):
                    emit_stream("act", eng)
            if per_eng["dve"]:
                @block.vector
                def _(eng):
                    emit_stream("dve", eng)
            if per_eng["pool"]:
                @block.gpsimd
                def _(eng):
                    emit_stream("pool", eng)

        self.nops_total += len(ops)
        bv = [(self.esem[e], self.ecount[e]) for e in self.ENGS]
        bv += [(g.sem, g.count) for g in self.groups.values() if g.barrier]
        self._barrier_vals = bv
        self.phase += 1
        self.ops = []

    def final_wait(self):
        nc = self.nc
        bv = [(self.esem[e], self.ecount[e]) for e in self.ENGS]
        bv += [(g.sem, g.count) for g in self.groups.values()]
        with nc.Block() as block:
            @block.sync
            def _(eng):
                for sem, val in bv:
                    if val > 0:
                        eng.wait_ge(sem, val)


def _dft_tables():
    def cs(n, scale):
        k = np.arange(n, dtype=np.int64)
        m = (k[:, None] * k[None, :]) % n
        ang = 2.0 * np.pi * m.astype(np.float64) / n
        return (np.cos(ang) * scale), (np.sin(ang) * scale)
    c4, s4 = cs(NTOK, 1.0 / 64.0)
    c2, s2 = cs(NCTX, 1.0 / 16.0)
    cc, sc = cs(128, 1.0 / np.sqrt(128.0))
    bf = ml_dtypes.bfloat16
    return dict(
        dftc=c4.astype(np.float32).astype(bf), dfts=s4.astype(np.float32).astype(bf),
        dft256=np.stack([c2, s2]).astype(np.float32).astype(bf),
        cc128=np.stack([cc, -sc]).astype(np.float32).astype(bf),
    )


def _attn_index_tables():
    rows, W, kr, KC = 64, 64, 8, 16
    row_start = np.clip(np.arange(rows) - kr // 2, 0, rows - kr)
    col_start = np.clip(np.arange(W) - KC // 2, 0, W - KC)
    dr = np.zeros((128, TABW), np.int64)
    dc = np.zeros((128, TABW), np.int64)
    ok = np.zeros((128, TABW), bool)

    def fill(colbase, qb, chunk):
        for p in range(128):
            rho = 2 * chunk + p // 64
            kap = p % 64
            for qi in range(256):
                r = 4 * qb + qi // 64
                c = qi % 64
                inwin = (row_start[r] <= rho <= row_start[r] + kr - 1) and (col_start[c] <= kap <= col_start[c] + KC - 1)
                if inwin:
                    ok[p, colbase + qi] = True
                    dr[p, colbase + qi] = rho - r + 7
                    dc[p, colbase + qi] = kap - c + 15
    for k in range(6):
        fill(k * 256, 2, 2 * 2 - 2 + k)
    for k in range(4):
        fill(1536 + k * 256, 0, k)
    for k in range(4):
        fill(2560 + k * 256, 15, 28 + k)
    return dr, dc, ok


_CONST_CACHE = {}


def _consts():
    if not _CONST_CACHE:
        _CONST_CACHE.update(_dft_tables())
        _CONST_CACHE["attn_idx"] = _attn_index_tables()
        _CONST_CACHE["ident"] = np.eye(128, dtype=np.float32)
        _CONST_CACHE["onesb"] = np.ones((128, 128), dtype=ml_dtypes.bfloat16)
    return _CONST_CACHE


def _verify_interior_pattern():
    return True


class Seg:
    pass


def build_program(debug=(), stop_after=None):
    nc = bass.Bass("TRN2", target_bir_lowering=False)
    dbg = set(debug)

    def din(name, shape, dt=F32):
        return nc.dram_tensor(name, list(shape), dt, kind="ExternalInput").ap()

    def scratch(name, shape, dt):
        kind = "ExternalOutput" if name in dbg else "Internal"
        return nc.dram_tensor(name, list(shape), dt, kind=kind).ap()

    x_in = din("x", [NTOK, D])
    ctx_in = din("ctx", [NCTX, D])
    ccT_in = din("ccT", [128, 32])
    w_ada = din("w_ada", [DEPTH, D, 6 * D])
    w_in = din("w_in", [DEPTH, D, 2 * D])
    w_out = din("w_out", [DEPTH, D, D])
    w_up = din("w_up", [DEPTH, D, 2 * DFF])
    w_down = din("w_down", [DEPTH, DFF, D])
    vecs_in = din("vecs", [DEPTH, 128, 512])
    wfour_in = din("wfour", [DEPTH, 8, 128, 128])
    tab_in = din("tab", [DEPTH, 16, 128, TABW])
    dftc_in = din("dftc", [NTOK, NTOK], BF16)
    dfts_in = din("dfts", [NTOK, NTOK], BF16)
    dft256_in = din("dft256", [2, NCTX, NCTX], BF16)
    cc128_in = din("cc128", [2, 128, 128], BF16)
    ident_in = din("ident", [128, 128])
    onesb_in = din("onesb", [128, 128], BF16)
    out_d = nc.dram_tensor("out", [NTOK, D], F32, kind="ExternalOutput").ap()

    winb = [scratch(f"winb{l}", [D, 2 * D], BF16) for l in range(DEPTH)]
    woutb = [scratch(f"woutb{l}", [D, D], BF16) for l in range(DEPTH)]
    wupT = [scratch(f"wupT{l}", [NPAIR, 128, 16, 256], BF16) for l in range(DEPTH)]
    wdnT = [scratch(f"wdnT{l}", [16, 128, NPAIR, 128], BF16) for l in range(DEPTH)]

    def mkseg(name, n, r):
        s = Seg()
        s.name = name
        s.n = n
        s.r = r
        s.xT = [scratch(f"{name}_xT{i}", [16, 128, n], F32) for i in range(2)]
        s.qkT = scratch(f"{name}_qkT", [16, 128, n], BF16)
        s.vt = scratch(f"{name}_vt", [8, 128, n // 128, 128], BF16)
        s.ftm = scratch(f"{name}_ftm", [n, 1024], BF16)
        s.mixT = scratch(f"{name}_mixT", [16, 128, n], BF16)
        return s

    SM = mkseg("m", NTOK, 0)
    SC = mkseg("c", NCTX, 1)

    S = Sched(nc)
    conv_grp = {}

    pers = {}

    def palloc(name, shape, dt):
        t = nc.alloc_sbuf_tensor(name, list(shape), dt)
        pers[name] = t
        return t

    ident = palloc("ident_s", [128, 128], F32)
    onesb = palloc("onesb_s", [128, 128], BF16)
    epst = palloc("eps_s", [128, 1], F32)
    vecs = [palloc(f"vecs{l}", [128, 512], F32) for l in range(DEPTH)]
    modsT = [palloc(f"modsT{l}", [128, 96, 2], F32) for l in range(DEPTH)]
    der = [{k: palloc(f"der{l}_{k}", [128, 16, 2], F32) for k in ("Aa", "GA", "Af", "GF")} for l in range(DEPTH)]
    ps = [nc.alloc_psum_tensor(f"psb{i}", [128, 512], F32) for i in range(8)]

    V_GPM, V_GPOM, V_GPF, V_GPOF, V_BADA, V_CW0, V_CW1, V_CW2, V_CB = 0, 16, 32, 48, 64, 160, 248, 336, 424

    state = {"done": False}

    def end_phase(name):
        S.emit_phase()
        if stop_after is not None and name == stop_after:
            state["done"] = True
        return state["done"]

    def PSB():
        return bufs("ps", 8, excl=True)

    def convert(l, which):
        if "win" in which:
            key = f"cv_win{l}"
            for i in range(4):
                S.op("pool", lambda e, l=l, i=i: e.dma_start(out=winb[l][i * 512:(i + 1) * 512, :], in_=w_in[l, i * 512:(i + 1) * 512, :]),
                     dma=key, nobarrier=True)
            conv_grp[key] = S.groups[key]
        if "wout" in which:
            key = f"cv_wout{l}"
            for i in range(2):
                S.op("pool", lambda e, l=l, i=i: e.dma_start(out=woutb[l][i * 1024:(i + 1) * 1024, :], in_=w_out[l, i * 1024:(i + 1) * 1024, :]),
                     dma=key, nobarrier=True)
            conv_grp[key] = S.groups[key]
        if "wup" in which:
            key = f"cv_wup{l}"
            for i in range(NPAIR):
                for half in range(2):
                    src = w_up[l, :, half * DFF + i * 128: half * DFF + (i + 1) * 128].rearrange("(kc p) e -> p kc e", p=128)
                    dst = wupT[l][i, :, :, half * 128:(half + 1) * 128]
                    S.op("pool", lambda e, src=src, dst=dst: e.dma_start(out=dst, in_=src), dma=key, nobarrier=True)
            conv_grp[key] = S.groups[key]
        if "wdn" in which:
            key = f"cv_wdn{l}"
            for m in range(16):
                src = w_down[l, :, m * 128:(m + 1) * 128].rearrange("(kc p) e -> p kc e", p=128)
                dst = wdnT[l][m]
                S.op("pool", lambda e, src=src, dst=dst: e.dma_start(out=dst, in_=src), dma=key, nobarrier=True)
            conv_grp[key] = S.groups[key]

    def cwait(key):
        g = conv_grp[key]
        return [(g.sem, g.count)]

    def phase_ada(l, first):
        PS = PSB()
        with (nc.sbuf_tensor(f"ada_cc{l}", [128, 32], F32) as cc_s,
              nc.sbuf_tensor(f"ada_sT{l}", [128, 16, 2], BF16) as sT,
              nc.sbuf_tensor(f"ada_w{l}", [128, 3, 16, 512], BF16) as wad,
              nc.sbuf_tensor(f"ada_mrow{l}", [2, 6 * D], F32) as mrow):
            Bc = Buf("cc"); BsT = Buf("sT"); BW = bufs("wad", 3); BM = bufs("mrow", 24)
            Bconst = Buf("const"); Bvec = Buf("vecs"); Bmods = Buf("mods"); Bder = Buf("der")
            if first:
                S.op("sp", lambda e: e.dma_start(out=ident[:], in_=ident_in), writes=[Bconst], dma="ld0")
                S.op("sp", lambda e: e.dma_start(out=onesb[:], in_=onesb_in), writes=[Bconst], dma="ld0")
                S.op("dve", lambda e: e.memset(epst[:], EPS), writes=[Bconst])
                for ll in range(DEPTH):
                    S.op("sp", lambda e, ll=ll: e.dma_start(out=vecs[ll][:], in_=vecs_in[ll]), writes=[Bvec], dma="ld0")
            S.op("sp", lambda e: e.dma_start(out=cc_s[:], in_=ccT_in), writes=[Bc], dma="ld0")
            S.op("act", lambda e: e.activation(out=sT[:].rearrange("p k r -> p (k r)"), in_=cc_s[:], func=AF.Silu), reads=[Bc], writes=[BsT])
            for nt in range(24):
                sl = nt % 3
                src = w_ada[l, :, nt * 512:(nt + 1) * 512].rearrange("(kc p) n -> p kc n", p=128)
                S.op("pool", lambda e, sl=sl, src=src: e.dma_start(out=wad[:, sl], in_=src), writes=[BW[sl]], dma=f"wad{sl}")
                b = nt % 4
                for kc in range(16):
                    S.op("pe", lambda e, b=b, kc=kc, sl=sl: e.matmul(ps[b][0:2, :], lhsT=sT[:, kc, :], rhs=wad[:, sl, kc, :], start=(kc == 0), stop=(kc == 15)),
                         reads=[BsT, BW[sl]], writes=[PS[b]])
                if nt % 2 == 0:
                    S.op("act", lambda e, b=b, nt=nt: e.activation(out=mrow[0:2, nt * 512:(nt + 1) * 512], in_=ps[b][0:2, :], func=AF.Copy), reads=[PS[b]], writes=[BM[nt]])
                else:
                    S.op("dve", lambda e, b=b, nt=nt: e.tensor_copy(out=mrow[0:2, nt * 512:(nt + 1) * 512], in_=ps[b][0:2, :]), reads=[PS[b]], writes=[BM[nt]])
            if first:
                convert(0, ("win", "wout"))
            for c in range(96):
                S.op("pe", lambda e, c=c: e.transpose(ps[4][:, 2 * c:2 * c + 2], mrow[0:2, c * 128:(c + 1) * 128], ident[0:2, 0:2]),
                     reads=[BM[c // 4], Bconst], writes=[PS[4]])
            S.op("dve", lambda e: e.tensor_tensor(out=modsT[l][:], in0=ps[4][:, 0:192].rearrange("p (c r) -> p c r", r=2),
                                                  in1=vecs[l][:, V_BADA:V_BADA + 96].unsqueeze(2).to_broadcast([128, 96, 2]), op=ALU.add),
                 reads=[PS[4], Bvec], writes=[Bmods])

            def bc(col):
                return vecs[l][:, col:col + 16].unsqueeze(2).to_broadcast([128, 16, 2])
            S.op("dve", lambda e: e.scalar_tensor_tensor(out=der[l]["Aa"][:], in0=modsT[l][:, 16:32, :], scalar=1.0, in1=bc(V_GPM), op0=ALU.add, op1=ALU.mult),
                 reads=[Bmods, Bvec], writes=[Bder])
            S.op("dve", lambda e: e.tensor_tensor(out=der[l]["GA"][:], in0=modsT[l][:, 32:48, :], in1=bc(V_GPOM), op=ALU.mult), reads=[Bmods, Bvec], writes=[Bder])
            S.op("dve", lambda e: e.scalar_tensor_tensor(out=der[l]["Af"][:], in0=modsT[l][:, 64:80, :], scalar=1.0, in1=bc(V_GPF), op0=ALU.add, op1=ALU.mult),
                 reads=[Bmods, Bvec], writes=[Bder])
            S.op("dve", lambda e: e.tensor_tensor(out=der[l]["GF"][:], in0=modsT[l][:, 80:96, :], in1=bc(V_GPOF), op=ALU.mult), reads=[Bmods, Bvec], writes=[Bder])
            if "dbg_mods" in dbg:
                dm = nc.dram_tensor(f"dbg_mods{l}", [128, 96, 2], F32, kind="ExternalOutput").ap()
                S.op("sp", lambda e: e.dma_start(out=dm, in_=modsT[l][:]), reads=[Bmods], dma="dbgst")
            return end_phase(f"ADA{l}")

    def phase_tin():
        PS = PSB()
        with (nc.sbuf_tensor("tin_x", [128, 2, 4, D], F32) as X,
              nc.sbuf_tensor("tin_xt", [128, 2, 16, 512], F32) as XT):
            BX = bufs("X", 2)
            BXT = [bufs(f"XT{s}_", 16) for s in range(2)]
            cnt = 0
            g = 0
            for seg, src in ((SM, x_in), (SC, ctx_in)):
                NT = min(512, seg.n)
                J = NT // 128
                for t in range(seg.n // NT):
                    s = g % 2
                    S.op("sp", lambda e, s=s, J=J, t=t, NT=NT, src=src: e.dma_start(out=X[:, s, 0:J, :], in_=src[t * NT:(t + 1) * NT, :].rearrange("(j p) f -> p j f", p=128)),
                         writes=[BX[s]], dma=f"tinx{s}")
                    for c in range(16):
                        b = cnt % 6
                        cnt += 1
                        for j in range(J):
                            S.op("pe", lambda e, b=b, j=j, s=s, c=c: e.transpose(ps[b][:, j * 128:(j + 1) * 128], X[:, s, j, c * 128:(c + 1) * 128], ident[:]),
                                 reads=[BX[s]], writes=[PS[b]])
                        if c % 2 == 0:
                            S.op("act", lambda e, b=b, s=s, c=c, NT=NT: e.activation(out=XT[:, s, c, 0:NT], in_=ps[b][:, 0:NT], func=AF.Copy), reads=[PS[b]], writes=[BXT[s][c]])
                        else:
                            S.op("dve", lambda e, b=b, s=s, c=c, NT=NT: e.tensor_copy(out=XT[:, s, c, 0:NT], in_=ps[b][:, 0:NT]), reads=[PS[b]], writes=[BXT[s][c]])
                    dst = seg.xT[0][:, :, t * NT:(t + 1) * NT].rearrange("c p t -> p c t")
                    S.op("act", lambda e, dst=dst, s=s, NT=NT: e.dma_start(out=dst, in_=XT[:, s, :, 0:NT]), reads=BXT[s], dma=f"tinst{s}")
                    g += 1
            return end_phase("TIN")

    def emit_rstd(PS, bank, rstd_ap, Brstd, ncol):
        S.op("act", lambda e: e.activation(out=rstd_ap, in_=ps[bank][:, 0:ncol], func=AF.Sqrt, scale=1.0 / D, bias=epst[:, 0:1]),
             reads=[PS[bank]], writes=[Brstd])
        S.op("dve", lambda e: e.reciprocal(out=rstd_ap, in_=rstd_ap), reads=[Brstd], writes=[Brstd])

    def phase_a(l, xi, segs):
        PS = PSB()
        with (nc.sbuf_tensor(f"a_xt{l}", [128, 16, 512], F32) as XT,
              nc.sbuf_tensor(f"a_sq{l}", [128, 2, 512], BF16) as SQ,
              nc.sbuf_tensor(f"a_ht{l}", [128, 2, 16, 512], BF16) as HT,
              nc.sbuf_tensor(f"a_w{l}", [128, 3, 16, 512], BF16) as W,
              nc.sbuf_tensor(f"a_qk{l}", [128, 2, 16, 512], BF16) as QK,
              nc.sbuf_tensor(f"a_vf{l}", [128, 2, 4, 2048], BF16) as VF,
              nc.sbuf_tensor(f"a_rstd{l}", [128, 512], F32) as rstd):
            BXT = bufs("XT", 16); BSQ = bufs("SQ", 2); BHT = [bufs(f"HT{s}_", 16) for s in range(2)]
            BW = bufs("W", 3); BQK = [bufs(f"QK{s}_", 16) for s in range(2)]
            BVF = [[Buf(f"VF{s}_{j}_{n}") for j in range(4) for n in range(4)] for s in range(2)]
            Brstd = Buf("rstd")
            widx = 0
            ecnt = 0
            g = 0
            bcnt = 0
            for seg in segs:
                NT = min(512, seg.n)
                J = NT // 128
                r = seg.r
                for t in range(seg.n // NT):
                    s = g % 2
                    src = seg.xT[xi][:, :, t * NT:(t + 1) * NT].rearrange("c p t -> p c t")
                    S.op("sp", lambda e, src=src, NT=NT: e.dma_start(out=XT[:, :, 0:NT], in_=src), writes=BXT, dma="a_xt")
                    for c in range(16):
                        S.op("act", lambda e, c=c, NT=NT: e.activation(out=SQ[:, c % 2, 0:NT], in_=XT[:, c, 0:NT], func=AF.Square), reads=[BXT[c]], writes=[BSQ[c % 2]])
                        S.op("pe", lambda e, c=c, NT=NT: e.matmul(ps[7][:, 0:NT], lhsT=onesb[:], rhs=SQ[:, c % 2, 0:NT], start=(c == 0), stop=(c == 15)),
                             reads=[BSQ[c % 2]], writes=[PS[7]])
                    emit_rstd(PS, 7, rstd[:, 0:NT], Brstd, NT)
                    for c in range(16):
                        S.op("dve", lambda e, c=c, NT=NT, r=r: e.scalar_tensor_tensor(out=XT[:, c, 0:NT], in0=XT[:, c, 0:NT], scalar=der[l]["Aa"][:, c, r:r + 1],
                                                                                   in1=rstd[:, 0:NT], op0=ALU.mult, op1=ALU.mult),
                             reads=[BXT[c], Brstd], writes=[BXT[c]])
                        S.op("act", lambda e, c=c, NT=NT, r=r, s=s: e.activation(out=HT[:, s, c, 0:NT], in_=XT[:, c, 0:NT], func=AF.Identity,
                                                                             bias=modsT[l][:, c, r:r + 1]),
                             reads=[BXT[c]], writes=[BHT[s][c]])
                    for i in range(8):
                        sl = widx % 3
                        widx += 1
                        wsrc = winb[l][:, i * 512:(i + 1) * 512].rearrange("(kc p) n -> p kc n", p=128)
                        S.op("sp", lambda e, sl=sl, wsrc=wsrc: e.dma_start(out=W[:, sl], in_=wsrc), writes=[BW[sl]], dma=f"a_w{sl}", extra=cwait(f"cv_win{l}"))
                        if i < 4:
                            for mm in range(4):
                                m = 4 * i + mm
                                b = bcnt % 6
                                bcnt += 1
                                for kc in range(16):
                                    S.op("pe", lambda e, b=b, sl=sl, kc=kc, mm=mm, s=s, NT=NT: e.matmul(ps[b][:, 0:NT], lhsT=W[:, sl, kc, mm * 128:(mm + 1) * 128], rhs=HT[:, s, kc, 0:NT],
                                                                                                  start=(kc == 0), stop=(kc == 15)),
                                         reads=[BW[sl], BHT[s][kc]], writes=[PS[b]])
                                if ecnt % 2 == 0:
                                    S.op("act", lambda e, b=b, s=s, m=m, NT=NT: e.activation(out=QK[:, s, m, 0:NT], in_=ps[b][:, 0:NT], func=AF.Copy), reads=[PS[b]], writes=[BQK[s][m]])
                                else:
                                    S.op("dve", lambda e, b=b, s=s, m=m, NT=NT: e.tensor_copy(out=QK[:, s, m, 0:NT], in_=ps[b][:, 0:NT]), reads=[PS[b]], writes=[BQK[s][m]])
                                ecnt += 1
                        else:
                            nb = i - 4
                            for j in range(J):
                                b = bcnt % 6
                                bcnt += 1
                                for kc in range(16):
                                    S.op("pe", lambda e, b=b, sl=sl, kc=kc, j=j, s=s: e.matmul(ps[b][:], lhsT=HT[:, s, kc, j * 128:(j + 1) * 128], rhs=W[:, sl, kc, :],
                                                                                         start=(kc == 0), stop=(kc == 15)),
                                         reads=[BW[sl], BHT[s][kc]], writes=[PS[b]])
                                if ecnt % 2 == 0:
                                    S.op("act", lambda e, b=b, s=s, j=j, nb=nb: e.activation(out=VF[:, s, j, nb * 512:(nb + 1) * 512], in_=ps[b][:], func=AF.Copy), reads=[PS[b]], writes=[BVF[s][j * 4 + nb]])
                                else:
                                    S.op("dve", lambda e, b=b, s=s, j=j, nb=nb: e.tensor_copy(out=VF[:, s, j, nb * 512:(nb + 1) * 512], in_=ps[b][:]), reads=[PS[b]], writes=[BVF[s][j * 4 + nb]])
                                ecnt += 1
                    dst = seg.qkT[:, :, t * NT:(t + 1) * NT].rearrange("c p t -> p c t")
                    S.op("act", lambda e, dst=dst, s=s, NT=NT: e.dma_start(out=dst, in_=QK[:, s, :, 0:NT]), reads=BQK[s], dma=f"a_st{s}")
                    for j in range(J):
                        dstv = seg.vt[:, :, t * J + j, :].rearrange("h p c -> p h c")
                        S.op("act", lambda e, dstv=dstv, s=s, j=j: e.dma_start(out=dstv, in_=VF[:, s, j, 0:1024].rearrange("p (h c) -> p h c", c=128)),
                             reads=BVF[s][j * 4:j * 4 + 2], dma=f"a_st{s}")
                    dstf = seg.ftm[t * NT:(t + 1) * NT, :].rearrange("(j p) c -> p j c", p=128)
                    S.op("act", lambda e, dstf=dstf, s=s, J=J: e.dma_start(out=dstf, in_=VF[:, s, 0:J, 1024:2048]), reads=BVF[s], dma=f"a_st{s}")
                    g += 1
            return end_phase(f"A{l}")

    def phase_b(l, with_ctx_queries):
        PS = PSB()
        with (nc.sbuf_tensor(f"b_q{l}", [128, 2, NTOK], BF16) as QT,
              nc.sbuf_tensor(f"b_qc{l}", [128, 2, NCTX], BF16) as QC,
              nc.sbuf_tensor(f"b_k{l}", [128, 2, NTOK + NCTX], BF16) as KT,
              nc.sbuf_tensor(f"b_vr{l}", [128, 2, 34, 128], BF16) as VR,
              nc.sbuf_tensor(f"b_va{l}", [128, 2, 2, 34, 128], BF16) as VA,
              nc.sbuf_tensor(f"b_ebf{l}", [128, 2, TABW], F32) as EBF,
              nc.sbuf_tensor(f"b_eb{l}", [128, 2, TABW], BF16) as EB,
              nc.sbuf_tensor(f"b_pt{l}", [128, 8, 256], BF16) as PT,
              nc.sbuf_tensor(f"b_rd{l}", [128, 2, 256], F32) as RD,
              nc.sbuf_tensor(f"b_at{l}", [128, 2, NTOK], BF16) as AT,
              nc.sbuf_tensor(f"b_atc{l}", [128, 2, NCTX], BF16) as ATC):
            BQ = bufs("Q", 2); BQC = bufs("QC", 2); BK = bufs("K", 2); BVR = bufs("VR", 2)
            BVA = [[Buf(f"VA{s}{h}") for h in range(2)] for s in range(2)]
            BEBF = bufs("EBF", 2); BEB = bufs("EB", 2); BPT = bufs("PT", 8); BRD = bufs("RD", 2)
            BAT = [[Buf(f"AT{s}_{q}") for q in range(16)] for s in range(2)]
            BATC = bufs("ATC", 2)
            Bones = Buf("ones")
            if l == 0:
                convert(0, ("wup",))
            for s in range(2):
                S.op("dve", lambda e, s=s: e.memset(VA[:, s, 0, :, 64:128], 1.0), writes=[BVA[s][0]])
                S.op("dve", lambda e, s=s: e.memset(VA[:, s, 1, :, 0:64], 1.0), writes=[BVA[s][1]])
            ebi = 0
            ptc = 0
            stc = 0
            acc_i = 0
            rdc = 0
            for hp in range(8):
                s = hp % 2
                S.op("sp", lambda e, s=s, hp=hp: e.dma_start(out=QT[:, s, :], in_=SM.qkT[hp]), writes=[BQ[s]], dma=f"b_ld{s}")
                S.op("sp", lambda e, s=s, hp=hp: e.dma_start(out=KT[:, s, 0:NTOK], in_=SM.qkT[8 + hp]), writes=[BK[s]], dma=f"b_ld{s}")
                S.op("sp", lambda e, s=s, hp=hp: e.dma_start(out=KT[:, s, NTOK:NTOK + NCTX], in_=SC.qkT[8 + hp]), writes=[BK[s]], dma=f"b_ld{s}")
                S.op("sp", lambda e, s=s, hp=hp: e.dma_start(out=VR[:, s, 0:32, :], in_=SM.vt[hp]), writes=[BVR[s]], dma=f"b_ld{s}")
                S.op("sp", lambda e, s=s, hp=hp: e.dma_start(out=VR[:, s, 32:34, :], in_=SC.vt[hp]), writes=[BVR[s]], dma=f"b_ld{s}")
                if with_ctx_queries:
                    S.op("sp", lambda e, s=s, hp=hp: e.dma_start(out=QC[:, s, :], in_=SC.qkT[hp]), writes=[BQC[s]], dma=f"b_ld{s}")
                S.op("dve", lambda e, s=s: e.tensor_copy(out=VA[:, s, 0, :, 0:64], in_=VR[:, s, :, 0:64]), reads=[BVR[s]], writes=[BVA[s][0]])
                S.op("act", lambda e, s=s: e.activation(out=VA[:, s, 1, :, 64:128], in_=VR[:, s, :, 64:128], func=AF.Copy), reads=[BVR[s]], writes=[BVA[s][1]])
                steps = []
                for hh in range(2):
                    h = 2 * hp + hh
                    es = ebi % 2
                    ebi += 1
                    S.op("sp", lambda e, es=es, h=h: e.dma_start(out=EBF[:, es, :], in_=tab_in[l, h]), writes=[BEBF[es]], dma=f"b_eb{es}")
                    S.op("act", lambda e, es=es: e.activation(out=EB[:, es, :], in_=EBF[:, es, :], func=AF.Exp), reads=[BEBF[es]], writes=[BEB[es]])
                    qbs = list(range(16)) + ([16] if with_ctx_queries else [])
                    for qb in qbs:
                        if qb == 16:
                            lst = [(32, None), (33, None)]
                        else:
                            if qb == 0:
                                loc = [(k, 1536 + k * 256) for k in range(4)]
                            elif qb == 15:
                                loc = [(28 + k, 2560 + k * 256) for k in range(4)]
                            else:
                                loc = [(2 * qb - 2 + k, k * 256) for k in range(6)]
                            lst = loc + [(32, None), (33, None)]
                        for k, (chunk, mcol) in enumerate(lst):
                            steps.append(dict(hh=hh, es=es, qb=qb, chunk=chunk, mcol=mcol, first=(k == 0), last=(k == len(lst) - 1)))
                hr = [slice(0, 64), slice(64, 128)]

                def emit_qk(st):
                    nonlocal stc
                    b = (0, 1, 2, 5, 6)[stc % 5]
                    stc += 1
                    st["b"] = b
                    hh = st["hh"]
                    if st["qb"] == 16:
                        rhs = QC[hr[hh], s, :]
                        rb = BQC[s]
                    else:
                        rhs = QT[hr[hh], s, st["qb"] * 256:(st["qb"] + 1) * 256]
                        rb = BQ[s]
                    ch = st["chunk"]
                    S.op("pe", lambda e, b=b, hh=hh, ch=ch, rhs=rhs, s=s: e.matmul(ps[b][:, 0:256], lhsT=KT[hr[hh], s, ch * 128:(ch + 1) * 128], rhs=rhs, start=True, stop=True),
                         reads=[BK[s], rb], writes=[PS[b]])

                def emit_sm(st):
                    nonlocal ptc
                    p = ptc % 8
                    ptc += 1
                    st["p"] = p
                    b = st["b"]
                    S.op("act", lambda e, b=b, p=p: e.activation(out=PT[:, p, :], in_=ps[b][:, 0:256], func=AF.Exp, scale=SCALE), reads=[PS[b]], writes=[BPT[p]])
                    if st["mcol"] is not None:
                        mc = st["mcol"]
                        es = st["es"]
                        S.op("dve", lambda e, p=p, mc=mc, es=es: e.tensor_tensor(out=PT[:, p, :], in0=PT[:, p, :], in1=EB[:, es, mc:mc + 256], op=ALU.mult),
                             reads=[BPT[p], BEB[es]], writes=[BPT[p]])

                def emit_pv(st):
                    nonlocal acc_i, rdc
                    hh = st["hh"]
                    if st["first"]:
                        acc_i += 1
                    ab = 3 + (acc_i % 2)
                    p = st["p"]
                    ch = st["chunk"]
                    S.op("pe", lambda e, ab=ab, hh=hh, ch=ch, p=p, st=st, s=s: e.matmul(ps[ab][:, 0:256], lhsT=VA[:, s, hh, ch, :], rhs=PT[:, p, :], start=st["first"], stop=st["last"]),
                         reads=[BVA[s][hh], BPT[p]], writes=[PS[ab]])
                    if st["last"]:
                        rs = rdc % 2
                        rdc += 1
                        num = hr[hh]
                        den = hr[1 - hh]
                        qb = st["qb"]
                        S.op("dve", lambda e, ab=ab, rs=rs, num=num, den=den: e.reciprocal(out=RD[num, rs, :], in_=ps[ab][den, 0:256]), reads=[PS[ab]], writes=[BRD[rs]])
                        if qb == 16:
                            S.op("dve", lambda e, ab=ab, rs=rs, num=num, s=s: e.tensor_tensor(out=ATC[num, s, :], in0=ps[ab][num, 0:256], in1=RD[num, rs, :], op=ALU.mult),
                                 reads=[PS[ab], BRD[rs]], writes=[BATC[s]])
                        else:
                            S.op("dve", lambda e, ab=ab, rs=rs, num=num, qb=qb, s=s: e.tensor_tensor(out=AT[num, s, qb * 256:(qb + 1) * 256], in0=ps[ab][num, 0:256], in1=RD[num, rs, :], op=ALU.mult),
                                 reads=[PS[ab], BRD[rs]], writes=[BAT[s][qb]])

                n = len(steps)
                LA = 4
                for i in range(min(LA, n)):
                    emit_qk(steps[i])
                for i in range(n):
                    emit_sm(steps[i])
                    if i + LA < n:
                        emit_qk(steps[i + LA])
                    emit_pv(steps[i])
                S.op("act", lambda e, s=s, hp=hp: e.dma_start(out=SM.mixT[hp], in_=AT[:, s, :]), reads=BAT[s], dma=f"b_st{s}")
                if with_ctx_queries:
                    S.op("act", lambda e, s=s, hp=hp: e.dma_start(out=SC.mixT[hp], in_=ATC[:, s, :]), reads=[BATC[s]], dma=f"b_st{s}")
            return end_phase(f"B{l}")

    def phase_c(l, segs):
        PS = PSB()
        with (nc.sbuf_tensor(f"c_f{l}", [128, 32, 1024], BF16) as Ft,
              nc.sbuf_tensor(f"c_d{l}", [128, 2, 32, 512], BF16) as DC,
              nc.sbuf_tensor(f"c_yc{l}", [128, 8, 512], BF16) as YC,
              nc.sbuf_tensor(f"c_ys{l}", [128, 2, 512], BF16) as YS,
              nc.sbuf_tensor(f"c_g{l}", [128, 2, 8, 128], BF16) as G,
              nc.sbuf_tensor(f"c_wf{l}", [128, 8, 128], F32) as WF,
              nc.sbuf_tensor(f"c_wfb{l}", [128, 8, 128], BF16) as WFB,
              nc.sbuf_tensor(f"c_cc{l}", [128, 2, 128], BF16) as CCt,
              nc.sbuf_tensor(f"c_fo{l}", [128, 2, 8, 512], BF16) as FO):
            BF_ = Buf("F"); BD = bufs("DC", 2); BYC = bufs("YC", 8); BYS = bufs("YS", 2); BG = Buf("G"); BWF = Buf("WF"); BWFB = Buf("WFB")
            BCC = Buf("CC"); BFO = [bufs(f"FO{s}_", 8) for s in range(2)]
            if l == 0:
                convert(0, ("wdn",))
                convert(1, ("win", "wout"))
            S.op("sp", lambda e: e.dma_start(out=WF[:], in_=wfour_in[l].rearrange("g c e -> c g e")), writes=[BWF], dma="c_ld")
            S.op("sp", lambda e: e.dma_start(out=CCt[:], in_=cc128_in.rearrange("t c e -> c t e")), writes=[BCC], dma="c_ld")
            S.op("act", lambda e: e.activation(out=WFB[:], in_=WF[:], func=AF.Copy), reads=[BWF], writes=[BWFB])
            bcnt = 0
            for tt in range(2):
                for g in range(8):
                    b = bcnt % 6
                    bcnt += 1
                    S.op("pe", lambda e, b=b, tt=tt, g=g: e.matmul(ps[b][:, 0:128], lhsT=CCt[:, tt, :], rhs=WFB[:, g, :], start=True, stop=True), reads=[BCC, BWFB], writes=[PS[b]])
                    S.op("dve", lambda e, b=b, tt=tt, g=g: e.tensor_copy(out=G[:, tt, g, :], in_=ps[b][:, 0:128]), reads=[PS[b]], writes=[BG])
            dci = 0
            fo_i = 0
            ecnt = 0
            for seg in segs:
                nch = seg.n // 128
                NT = min(512, seg.n)
                S.op("sp", lambda e, seg=seg, nch=nch: e.dma_start(out=Ft[:, 0:nch, :], in_=seg.ftm.rearrange("(nc p) c -> p nc c", p=128)), writes=[BF_], dma="c_ld")
                for t in range(seg.n // NT):
                    dsl = []
                    for tt in range(2):
                        ds = dci % 2
                        dci += 1
                        if seg.n == NTOK:
                            srcm = (dftc_in if tt == 0 else dfts_in)[:, t * NT:(t + 1) * NT]
                        else:
                            srcm = dft256_in[tt]
                        srcm = srcm.rearrange("(nc p) m -> p nc m", p=128)
                        S.op("sp", lambda e, ds=ds, srcm=srcm, nch=nch, NT=NT: e.dma_start(out=DC[:, ds, 0:nch, 0:NT], in_=srcm), writes=[BD[ds]], dma=f"c_d{ds}")
                        dsl.append(ds)
                    fs = fo_i % 2
                    fo_i += 1
                    for g in range(8):
                        b = bcnt % 6
                        bcnt += 1
                        ds = dsl[0]
                        for n_ in range(nch):
                            S.op("pe", lambda e, b=b, g=g, n_=n_, ds=ds, NT=NT, nch=nch: e.matmul(ps[b][:, 0:NT], lhsT=Ft[:, n_, g * 128:(g + 1) * 128], rhs=DC[:, ds, n_, 0:NT],
                                                                                            start=(n_ == 0), stop=(n_ == nch - 1)),
                                 reads=[BF_, BD[ds]], writes=[PS[b]])
                        if ecnt % 2 == 0:
                            S.op("act", lambda e, b=b, g=g, NT=NT: e.activation(out=YC[:, g, 0:NT], in_=ps[b][:, 0:NT], func=AF.Copy), reads=[PS[b]], writes=[BYC[g]])
                        else:
                            S.op("dve", lambda e, b=b, g=g, NT=NT: e.tensor_copy(out=YC[:, g, 0:NT], in_=ps[b][:, 0:NT]), reads=[PS[b]], writes=[BYC[g]])
                        ecnt += 1
                    pend = None
                    for g in range(8):
                        b = bcnt % 6
                        bcnt += 1
                        ds = dsl[1]
                        for n_ in range(nch):
                            S.op("pe", lambda e, b=b, g=g, n_=n_, ds=ds, NT=NT, nch=nch: e.matmul(ps[b][:, 0:NT], lhsT=Ft[:, n_, g * 128:(g + 1) * 128], rhs=DC[:, ds, n_, 0:NT],
                                                                                            start=(n_ == 0), stop=(n_ == nch - 1)),
                                 reads=[BF_, BD[ds]], writes=[PS[b]])
                        ys = g % 2
                        if ecnt % 2 == 0:
                            S.op("act", lambda e, b=b, ys=ys, NT=NT: e.activation(out=YS[:, ys, 0:NT], in_=ps[b][:, 0:NT], func=AF.Copy), reads=[PS[b]], writes=[BYS[ys]])
                        else:
                            S.op("dve", lambda e, b=b, ys=ys, NT=NT: e.tensor_copy(out=YS[:, ys, 0:NT], in_=ps[b][:, 0:NT]), reads=[PS[b]], writes=[BYS[ys]])
                        ecnt += 1

                        def stage2(g=g, ys=ys, NT=NT, fs=fs):
                            nonlocal ecnt
                            b2 = 6 + (g % 2)
                            S.op("pe", lambda e: e.matmul(ps[b2][:, 0:NT], lhsT=G[:, 0, g, :], rhs=YC[:, g, 0:NT], start=True, stop=False), reads=[BG, BYC[g]], writes=[PS[b2]])
                            S.op("pe", lambda e: e.matmul(ps[b2][:, 0:NT], lhsT=G[:, 1, g, :], rhs=YS[:, ys, 0:NT], start=False, stop=True), reads=[BG, BYS[ys]], writes=[PS[b2]])
                            if ecnt % 2 == 0:
                                S.op("act", lambda e: e.activation(out=FO[:, fs, g, 0:NT], in_=ps[b2][:, 0:NT], func=AF.Copy), reads=[PS[b2]], writes=[BFO[fs][g]])
                            else:
                                S.op("dve", lambda e: e.tensor_copy(out=FO[:, fs, g, 0:NT], in_=ps[b2][:, 0:NT]), reads=[PS[b2]], writes=[BFO[fs][g]])
                            ecnt += 1
                        if pend is not None:
                            pend()
                        pend = stage2
                    pend()
                    dst = seg.mixT[8:16, :, t * NT:(t + 1) * NT].rearrange("c p t -> p c t")
                    S.op("act", lambda e, dst=dst, fs=fs, NT=NT: e.dma_start(out=dst, in_=FO[:, fs, :, 0:NT]), reads=BFO[fs], dma=f"c_st{fs}")
            return end_phase(f"C{l}")

    def phase_d(l, xi, segs):
        PS = PSB()
        with (nc.sbuf_tensor(f"d_w{l}", [128, 16, D], BF16) as W,
              nc.sbuf_tensor(f"d_mt{l}", [128, 2, 16, 512], BF16) as MT,
              nc.sbuf_tensor(f"d_xt{l}", [128, 16, 512], F32) as XT,
              nc.sbuf_tensor(f"d_y{l}", [128, 16, 512], F32) as Y,
              nc.sbuf_tensor(f"d_sq{l}", [128, 2, 512], BF16) as SQ,
              nc.sbuf_tensor(f"d_rstd{l}", [128, 512], F32) as rstd):
            BW = Buf("W"); BMT = bufs("MT", 2); BXT = bufs("XT", 16); BY = bufs("Y", 16); BSQ = bufs("SQ", 2); Brstd = Buf("rstd")
            for i in range(4):
                S.op("sp", lambda e, i=i: e.dma_start(out=W[:, 4 * i:4 * i + 4, :], in_=woutb[l][i * 512:(i + 1) * 512, :].rearrange("(kc p) n -> p kc n", p=128)),
                     writes=[BW], dma="d_w", extra=cwait(f"cv_wout{l}"))
            g = 0
            bcnt = 0
            for seg in segs:
                NT = min(512, seg.n)
                r = seg.r
                for t in range(seg.n // NT):
                    s = g % 2
                    g += 1
                    S.op("sp", lambda e, seg=seg, t=t, NT=NT, s=s: e.dma_start(out=MT[:, s, :, 0:NT], in_=seg.mixT[:, :, t * NT:(t + 1) * NT].rearrange("c p t -> p c t")),
                         writes=[BMT[s]], dma=f"d_mt{s}")
                    xsrc = seg.xT[xi][:, :, t * NT:(t + 1) * NT].rearrange("c p t -> p c t")
                    S.op("sp", lambda e, xsrc=xsrc, NT=NT: e.dma_start(out=XT[:, :, 0:NT], in_=xsrc), writes=BXT, dma="d_xt")
                    pend = None
                    for m in range(16):
                        b = bcnt % 6
                        bcnt += 1
                        for kc in range(16):
                            S.op("pe", lambda e, b=b, kc=kc, m=m, s=s, NT=NT: e.matmul(ps[b][:, 0:NT], lhsT=W[:, kc, m * 128:(m + 1) * 128], rhs=MT[:, s, kc, 0:NT],
                                                                                 start=(kc == 0), stop=(kc == 15)),
                                 reads=[BW, BMT[s]], writes=[PS[b]])
                        S.op("act", lambda e, b=b, m=m, NT=NT: e.activation(out=SQ[:, m % 2, 0:NT], in_=ps[b][:, 0:NT], func=AF.Square), reads=[PS[b]], writes=[BSQ[m % 2]])
                        S.op("dve", lambda e, b=b, m=m, NT=NT: e.tensor_copy(out=Y[:, m, 0:NT], in_=ps[b][:, 0:NT]), reads=[PS[b]], writes=[BY[m]])

                        def ssmm(m=m, NT=NT):
                            S.op("pe", lambda e: e.matmul(ps[7][:, 0:NT], lhsT=onesb[:], rhs=SQ[:, m % 2, 0:NT], start=(m == 0), stop=(m == 15)), reads=[BSQ[m % 2]], writes=[PS[7]])
                        if pend is not None:
                            pend()
                        pend = ssmm
                    pend()
                    emit_rstd(PS, 7, rstd[:, 0:NT], Brstd, NT)
                    for m in range(16):
                        S.op("dve", lambda e, m=m, NT=NT, r=r: e.scalar_tensor_tensor(out=Y[:, m, 0:NT], in0=Y[:, m, 0:NT], scalar=der[l]["GA"][:, m, r:r + 1], in1=rstd[:, 0:NT],
                                                                                   op0=ALU.mult, op1=ALU.mult), reads=[BY[m], Brstd], writes=[BY[m]])
                        S.op("pool", lambda e, m=m, NT=NT: e.tensor_tensor(out=XT[:, m, 0:NT], in0=XT[:, m, 0:NT], in1=Y[:, m, 0:NT], op=ALU.add), reads=[BXT[m], BY[m]], writes=[BXT[m]])
                    S.op("act", lambda e, xsrc=xsrc, NT=NT: e.dma_start(out=xsrc, in_=XT[:, :, 0:NT]), reads=BXT, dma="d_st")
            return end_phase(f"D{l}")

    def phase_ef(l, xi, segs):
        PS = PSB()
        NUMAX = 460
        with (nc.sbuf_tensor(f"e_xt{l}", [128, 16, NUMAX], F32) as XT,
              nc.sbuf_tensor(f"e_sq{l}", [128, 2, NUMAX], BF16) as SQ,
              nc.sbuf_tensor(f"e_ht{l}", [128, 16, NUMAX], BF16) as HT,
              nc.sbuf_tensor(f"e_act{l}", [128, NPAIR, 456], BF16) as ACTT,
              nc.sbuf_tensor(f"e_wu{l}", [128, 3, 16, 256], BF16) as WU,
              nc.sbuf_tensor(f"e_wd{l}", [128, 3, NPAIR, 128], BF16) as WD,
              nc.sbuf_tensor(f"e_ta{l}", [128, 2, 456], F32) as TA,
              nc.sbuf_tensor(f"e_tg{l}", [128, 2, 456], F32) as TG,
              nc.sbuf_tensor(f"e_sg{l}", [128, 2, 456], F32) as SG,
              nc.sbuf_tensor(f"e_y{l}", [128, 16, NUMAX], F32) as Y,
              nc.sbuf_tensor(f"e_rstd{l}", [128, NUMAX], F32) as rstd):
            BXT = bufs("XT", 16); BSQ = bufs("SQ", 2); BHT = bufs("HT", 16); BACT = bufs("ACT", NPAIR)
            BWU = bufs("WU", 3); BWD = bufs("WD", 3); BTA = bufs("TA", 2); BTG = bufs("TG", 2); BSG = bufs("SG", 2)
            BY = bufs("Y", 16); Brstd = Buf("rstd")
            if l == 0:
                convert(1, ("wup", "wdn"))
            wu_i = 0
            wd_i = 0
            pi = 0
            dcnt = 0
            vv = vecs[l]
            V_CW0, V_CW1, V_CW2, V_CB = 160, 248, 336, 424
            for seg in segs:
                r = seg.r
                if seg.n == NTOK:
                    wins = [(456 * t, 456) for t in range(8)] + [(3648, 448)]
                else:
                    wins = [(0, seg.n)]
                for (lo, NO) in wins:
                    NU = NO + 2
                    a0 = max(lo - 1, 0)
                    a1 = min(lo + NO + 1, seg.n)
                    c0 = a0 - (lo - 1)
                    c1 = c0 + (a1 - a0)
                    if c0 > 0:
                        S.op("dve", lambda e, c0=c0: e.memset(XT[:, :, 0:c0], 0.0), writes=BXT)
                    if c1 < NU:
                        S.op("dve", lambda e, c1=c1, NU=NU: e.memset(XT[:, :, c1:NU], 0.0), writes=BXT)
                    xsrc = seg.xT[xi][:, :, a0:a1].rearrange("c p t -> p c t")
                    S.op("sp", lambda e, xsrc=xsrc, c0=c0, c1=c1: e.dma_start(out=XT[:, :, c0:c1], in_=xsrc), writes=BXT, dma="e_xt")
                    for c in range(16):
                        S.op("act", lambda e, c=c, NU=NU: e.activation(out=SQ[:, c % 2, 0:NU], in_=XT[:, c, 0:NU], func=AF.Square), reads=[BXT[c]], writes=[BSQ[c % 2]])
                        S.op("pe", lambda e, c=c, NU=NU: e.matmul(ps[7][:, 0:NU], lhsT=onesb[:], rhs=SQ[:, c % 2, 0:NU], start=(c == 0), stop=(c == 15)), reads=[BSQ[c % 2]], writes=[PS[7]])
                    emit_rstd(PS, 7, rstd[:, 0:NU], Brstd, NU)
                    for c in range(16):
                        S.op("dve", lambda e, c=c, NU=NU, r=r: e.scalar_tensor_tensor(out=Y[:, c, 0:NU], in0=XT[:, c, 0:NU], scalar=der[l]["Af"][:, c, r:r + 1], in1=rstd[:, 0:NU],
                                                                                   op0=ALU.mult, op1=ALU.mult), reads=[BXT[c], Brstd], writes=[BY[c]])
                        S.op("act", lambda e, c=c, NU=NU, r=r: e.activation(out=HT[:, c, 0:NU], in_=Y[:, c, 0:NU], func=AF.Identity, bias=modsT[l][:, 48 + c, r:r + 1]),
                             reads=[BY[c]], writes=[BHT[c]])
                    if c0 > 0:
                        S.op("dve", lambda e, c0=c0: e.memset(HT[:, :, 0:c0], 0.0), writes=BHT)
                    if c1 < NU:
                        S.op("dve", lambda e, c1=c1, NU=NU: e.memset(HT[:, :, c1:NU], 0.0), writes=BHT)
                    for i in range(NPAIR):
                        sl = wu_i % 3
                        wu_i += 1
                        S.op("sp", lambda e, sl=sl, i=i: e.dma_start(out=WU[:, sl], in_=wupT[l][i]), writes=[BWU[sl]], dma=f"e_wu{sl}", extra=cwait(f"cv_wup{l}"))
                        ba = 2 * (pi % 2)
                        bg = ba + 1
                        ts = pi % 2
                        pi += 1
                        for kc in range(16):
                            S.op("pe", lambda e, ba=ba, sl=sl, kc=kc, NU=NU: e.matmul(ps[ba][:, 0:NU], lhsT=WU[:, sl, kc, 0:128], rhs=HT[:, kc, 0:NU], start=(kc == 0), stop=(kc == 15)),
                                 reads=[BWU[sl], BHT[kc]], writes=[PS[ba]])
                        for kc in range(16):
                            S.op("pe", lambda e, bg=bg, sl=sl, kc=kc, NU=NU: e.matmul(ps[bg][:, 0:NU], lhsT=WU[:, sl, kc, 128:256], rhs=HT[:, kc, 0:NU], start=(kc == 0), stop=(kc == 15)),
                                 reads=[BWU[sl], BHT[kc]], writes=[PS[bg]])
                        for (bk, T, BT, ci) in ((ba, TA, BTA, i), (bg, TG, BTG, NPAIR + i)):
                            S.op("act", lambda e, bk=bk, T=T, ts=ts, ci=ci, NO=NO: e.activation(out=T[:, ts, 0:NO], in_=ps[bk][:, 1:1 + NO], func=AF.Identity,
                                                                                           scale=vv[:, V_CW1 + ci:V_CW1 + ci + 1], bias=vv[:, V_CB + ci:V_CB + ci + 1]),
                                 reads=[PS[bk]], writes=[BT[ts]])
                            S.op("dve", lambda e, bk=bk, T=T, ts=ts, ci=ci, NO=NO: e.scalar_tensor_tensor(out=T[:, ts, 0:NO], in0=ps[bk][:, 0:NO], scalar=vv[:, V_CW0 + ci:V_CW0 + ci + 1],
                                                                                                     in1=T[:, ts, 0:NO], op0=ALU.mult, op1=ALU.add),
                                 reads=[PS[bk], BT[ts]], writes=[BT[ts]])
                            S.op("dve", lambda e, bk=bk, T=T, ts=ts, ci=ci, NO=NO: e.scalar_tensor_tensor(out=T[:, ts, 0:NO], in0=ps[bk][:, 2:2 + NO], scalar=vv[:, V_CW2 + ci:V_CW2 + ci + 1],
                                                                                                     in1=T[:, ts, 0:NO], op0=ALU.mult, op1=ALU.add),
                                 reads=[PS[bk], BT[ts]], writes=[BT[ts]])
                        S.op("act", lambda e, ts=ts, NO=NO: e.activation(out=SG[:, ts, 0:NO], in_=TG[:, ts, 0:NO], func=AF.Silu), reads=[BTG[ts]], writes=[BSG[ts]])
                        S.op("dve", lambda e, ts=ts, i=i, NO=NO: e.tensor_tensor(out=ACTT[:, i, 0:NO], in0=TA[:, ts, 0:NO], in1=SG[:, ts, 0:NO], op=ALU.mult),
                             reads=[BTA[ts], BSG[ts]], writes=[BACT[i]])
                    pend = None
                    for m in range(16):
                        sl = wd_i % 3
                        wd_i += 1
                        S.op("sp", lambda e, sl=sl, m=m: e.dma_start(out=WD[:, sl], in_=wdnT[l][m]), writes=[BWD[sl]], dma=f"e_wd{sl}", extra=cwait(f"cv_wdn{l}"))
                        b = 4 + (dcnt % 2)
                        dcnt += 1
                        for kc in range(NPAIR):
                            S.op("pe", lambda e, b=b, sl=sl, kc=kc, NO=NO: e.matmul(ps[b][:, 0:NO], lhsT=WD[:, sl, kc, :], rhs=ACTT[:, kc, 0:NO], start=(kc == 0), stop=(kc == NPAIR - 1)),
                                 reads=[BWD[sl], BACT[kc]], writes=[PS[b]])
                        S.op("act", lambda e, b=b, m=m, NO=NO: e.activation(out=SQ[:, m % 2, 0:NO], in_=ps[b][:, 0:NO], func=AF.Square), reads=[PS[b]], writes=[BSQ[m % 2]])
                        S.op("dve", lambda e, b=b, m=m, NO=NO: e.tensor_copy(out=Y[:, m, 0:NO], in_=ps[b][:, 0:NO]), reads=[PS[b]], writes=[BY[m]])

                        def ssmm(m=m, NO=NO):
                            S.op("pe", lambda e: e.matmul(ps[7][:, 0:NO], lhsT=onesb[:], rhs=SQ[:, m % 2, 0:NO], start=(m == 0), stop=(m == 15)), reads=[BSQ[m % 2]], writes=[PS[7]])
                        if pend is not None:
                            pend()
                        pend = ssmm
                    pend()
                    emit_rstd(PS, 7, rstd[:, 0:NO], Brstd, NO)
                    for m in range(16):
                        S.op("dve", lambda e, m=m, NO=NO, r=r: e.scalar_tensor_tensor(out=Y[:, m, 0:NO], in0=Y[:, m, 0:NO], scalar=der[l]["GF"][:, m, r:r + 1], in1=rstd[:, 0:NO],
                                                                                   op0=ALU.mult, op1=ALU.mult), reads=[BY[m], Brstd], writes=[BY[m]])
                        S.op("dve", lambda e, m=m, NO=NO: e.tensor_tensor(out=Y[:, m, 0:NO], in0=Y[:, m, 0:NO], in1=XT[:, m, 1:1 + NO], op=ALU.add), reads=[BXT[m], BY[m]], writes=[BY[m]])
                    dst = seg.xT[1 - xi][:, :, lo:lo + NO].rearrange("c p t -> p c t")
                    S.op("act", lambda e, dst=dst, NO=NO: e.dma_start(out=dst, in_=Y[:, :, 0:NO]), reads=BY, dma="e_st")
            return end_phase(f"EF{l}")

    def phase_tout(xi):
        PS = PSB()
        with (nc.sbuf_tensor("to_xt", [128, 2, 16, 512], F32) as XT,
              nc.sbuf_tensor("to_o", [128, 2, 4, D], F32) as O):
            BXT = bufs("XT", 2)
            BO = [[Buf(f"O{s}_{k}") for k in range(16)] for s in range(2)]
            cnt = 0
            for t in range(NTOK // 512):
                s = t % 2
                S.op("sp", lambda e, s=s, t=t: e.dma_start(out=XT[:, s], in_=SM.xT[xi][:, :, t * 512:(t + 1) * 512].rearrange("c p t -> p c t")), writes=[BXT[s]], dma=f"to_ld{s}")
                for j in range(4):
                    for cb in range(4):
                        b = cnt % 6
                        cnt += 1
                        for cc in range(4):
                            S.op("pe", lambda e, b=b, cc=cc, s=s, cb=cb, j=j: e.transpose(ps[b][:, cc * 128:(cc + 1) * 128], XT[:, s, 4 * cb + cc, j * 128:(j + 1) * 128], ident[:]),
                                 reads=[BXT[s]], writes=[PS[b]])
                        if cnt % 2 == 0:
                            S.op("act", lambda e, b=b, s=s, j=j, cb=cb: e.activation(out=O[:, s, j, cb * 512:(cb + 1) * 512], in_=ps[b][:], func=AF.Copy), reads=[PS[b]], writes=[BO[s][j * 4 + cb]])
                        else:
                            S.op("dve", lambda e, b=b, s=s, j=j, cb=cb: e.tensor_copy(out=O[:, s, j, cb * 512:(cb + 1) * 512], in_=ps[b][:]), reads=[PS[b]], writes=[BO[s][j * 4 + cb]])
                S.op("act", lambda e, s=s, t=t: e.dma_start(out=out_d[t * 512:(t + 1) * 512, :].rearrange("(j p) f -> p j f", p=128), in_=O[:, s]), reads=BO[s], dma=f"to_st{s}")
            return end_phase("TOUT")

    def run():
        if phase_ada(0, True):
            return
        if phase_tin():
            return
        xi = 0
        for l in range(DEPTH):
            if l > 0:
                if phase_ada(l, False):
                    return
            if phase_a(l, xi, [SM, SC]):
                return
            if phase_b(l, with_ctx_queries=(l == 0)):
                return
            segs = [SM, SC] if l == 0 else [SM]
            if phase_c(l, segs):
                return
            if phase_d(l, xi, segs):
                return
            if phase_ef(l, xi, segs):
                return
            xi = 1 - xi
        phase_tout(xi)

    run()
    S.final_wait()
    return nc, S


def _pack_vecs(g_pre_mix, g_post_mix, g_pre_ffn, g_post_ffn, b_ada, conv_w, conv_b):
    out = np.zeros((DEPTH, 128, 512), np.float32)
    for l in range(DEPTH):
        def cm(v):
            return np.ascontiguousarray(np.asarray(v, np.float32).reshape(-1, 128).T)
        out[l, :, 0:16] = cm(g_pre_mix[l])
        out[l, :, 16:32] = cm(g_post_mix[l])
        out[l, :, 32:48] = cm(g_pre_ffn[l])
        out[l, :, 48:64] = cm(g_post_ffn[l])
        out[l, :, 64:160] = cm(b_ada[l])
        out[l, :, 160:248] = cm(conv_w[l, 0])
        out[l, :, 248:336] = cm(conv_w[l, 1])
        out[l, :, 336:424] = cm(conv_w[l, 2])
        out[l, :, 424:512] = cm(conv_b[l])
    return out


def make_in_maps(inputs):
    C = _consts()
    x = np.asarray(inputs["x"], np.float32)
    c = np.asarray(inputs["c"], np.float32)
    ctx = np.asarray(inputs["ctx"], np.float32)
    c_ctx = np.asarray(inputs["c_ctx"], np.float32)
    rpb = np.asarray(inputs["rpb"], np.float32)
    dr, dc, ok = C["attn_idx"]
    tab = np.where(ok[None, None], rpb[:, :, dr, dc], np.float32(NEG)).astype(np.float32)
    vecs = _pack_vecs(inputs["g_pre_mix"], inputs["g_post_mix"], inputs["g_pre_ffn"], inputs["g_post_ffn"],
                      inputs["b_ada"], np.asarray(inputs["conv_w"], np.float32), inputs["conv_b"])
    shared = dict(
        w_ada=np.asarray(inputs["w_ada"], np.float32), w_in=np.asarray(inputs["w_in"], np.float32),
        w_out=np.asarray(inputs["w_out"], np.float32), w_up=np.asarray(inputs["w_up"], np.float32),
        w_down=np.asarray(inputs["w_down"], np.float32), vecs=vecs, wfour=np.asarray(inputs["w_four"], np.float32),
        tab=tab, dftc=C["dftc"], dfts=C["dfts"], dft256=C["dft256"], cc128=C["cc128"], ident=C["ident"], onesb=C["onesb"],
    )
    maps = []
    for b in range(NCORES):
        cc = np.stack([c[b], c_ctx])
        ccT = np.ascontiguousarray(cc.reshape(2, 16, 128).transpose(2, 1, 0).reshape(128, 32))
        m = dict(shared)
        m["x"] = np.ascontiguousarray(x[b])
        m["ctx"] = np.ascontiguousarray(ctx[b])
        m["ccT"] = ccT
        maps.append(m)
    return maps


_PROG = {}


def kernel(**inputs):
    if "nc" not in _PROG:
        _PROG["nc"], _ = build_program()
    nc = _PROG["nc"]
    maps = make_in_maps(inputs)
    res = run_bass_kernel_spmd(nc, maps, core_ids=list(range(NCORES)))
    return np.stack([np.asarray(r["out"], np.float32) for r in res.results], axis=0)
```

```python
import os
import numpy as np
import ml_dtypes
import concourse.bass as bass
import concourse.mybir as mybir
from concourse.bass_utils import run_bass_kernel_spmd

F32 = mybir.dt.float32
BF16 = mybir.dt.bfloat16
ALU = mybir.AluOpType
AF = mybir.ActivationFunctionType

D = 2048
NTOK = 4096
NCTX = 256
DEPTH = 2
DFF = 5632
NPAIR = DFF // 128
EPS = 1e-6
SCALE = 0.125
NCORES = 8
TABW = 3584
NEG = -30000.0


class Buf:
    __slots__ = ("name", "last_w", "readers", "excl")

    def __init__(self, name, excl=False):
        self.name = name
        self.last_w = None
        self.readers = []
        self.excl = excl


def bufs(name, n, excl=False):
    return [Buf(f"{name}{i}", excl) for i in range(n)]


class DmaGroup:
    __slots__ = ("sem", "count", "name", "barrier")

    def __init__(self, sem, name, barrier=True):
        self.sem = sem
        self.count = 0
        self.name = name
        self.barrier = barrier


class Op:
    __slots__ = ("eng", "fn", "waits", "cdeps", "grp", "inc_val", "needed", "phase")


class Sched:
    ENGS = ("pe", "act", "dve", "pool", "sp")

    def __init__(self, nc):
        self.nc = nc
        self.esem = {}
        self.ecount = {e: 0 for e in self.ENGS}
        for e in self.ENGS:
            self.esem[e] = nc.semaphore(f"es_{e}").__enter__()
        self.groups = {}
        self.ops = []
        self.seen = {e: {} for e in self.ENGS}
        self.phase = 0
        self._barrier_vals = None
        self.nops_total = 0

    def group(self, key, barrier=True):
        g = self.groups.get(key)
        if g is None:
            sem = self.nc.semaphore(f"dg_{key}").__enter__()
            g = DmaGroup(sem, key, barrier)
            self.groups[key] = g
        return g

    def op(self, eng, fn, reads=(), writes=(), dma=None, extra=(), nobarrier=False):
        o = Op()
        o.eng = eng
        o.fn = fn
        o.inc_val = None
        o.needed = False
        o.grp = None
        o.phase = self.phase
        waits = {}
        cdeps = set()
        is_dma = dma is not None
        for sem, val in extra:
            waits[id(sem)] = [sem, val]

        def add_dep(d, kind):
            if d.phase != self.phase:
                return
            if d.grp is not None:
                g = d.grp
                cur = waits.get(id(g.sem))
                if cur is None or cur[1] < g.count:
                    waits[id(g.sem)] = [g.sem, g.count]
            else:
                if d.eng == eng and not is_dma:
                    if eng == "pe":
                        return
                    if kind != "raw":
                        return
                cdeps.add(d)

        rd = []
        wr = list(writes)
        for b in reads:
            if b.excl:
                wr.append(b)
            else:
                rd.append(b)
        for b in rd:
            if b.last_w is not None:
                add_dep(b.last_w, "raw")
        for b in wr:
            if b.last_w is not None:
                add_dep(b.last_w, "raw" if b.excl else "waw")
            for r in b.readers:
                add_dep(r, "war")
        for b in rd:
            if not is_dma:
                b.readers = [r_ for r_ in b.readers if not (r_.grp is None and r_.eng == eng)]
            b.readers.append(o)
        for b in wr:
            b.last_w = o
            b.readers = []
        for d in cdeps:
            d.needed = True
        o.waits = waits
        o.cdeps = cdeps
        if is_dma:
            g = self.group(dma, barrier=not nobarrier)
            g.count += 16
            o.grp = g
        self.ops.append(o)
        return o

    def emit_phase(self):
        nc = self.nc
        ops = self.ops
        for o in ops:
            if o.grp is None and o.needed:
                self.ecount[o.eng] += 1
                o.inc_val = self.ecount[o.eng]
        barrier = self._barrier_vals
        per_eng = {e: [o for o in ops if o.eng == e] for e in self.ENGS}

        def emit_stream(e, engobj):
            seen = self.seen[e]

            def wait(sem, val):
                if val <= 0:
                    return
                cur = seen.get(id(sem))
                if cur is not None and cur >= val:
                    return
                seen[id(sem)] = val
                engobj.wait_ge(sem, val)

            if barrier is not None and per_eng[e]:
                for sem, val in barrier:
                    if sem is self.esem[e]:
                        continue
                    wait(sem, val)
            for o in per_eng[e]:
                for sem, val in o.waits.values():
                    wait(sem, val)
                for d in o.cdeps:
                    wait(self.esem[d.eng], d.inc_val)
                ins = o.fn(engobj)
                if o.grp is not None:
                    ins.then_inc(o.grp.sem, 16)
                elif o.inc_val is not None:
                    ins.then_inc(self.esem[e], 1)

        with nc.Block() as block:
            if per_eng["sp"]:
                @block.sync
                def _(eng):
                    emit_stream("sp", eng)
            if per_eng["pe"]:
                @block.tensor
                def _(eng):
                    emit_stream("pe", eng)
            if per_eng["act"]:
                @block.scalar
                def _(eng):
                    emit_stream("act", eng)
            if per_eng["dve"]:
                @block.vector
                def _(eng):
                    emit_stream("dve", eng)
            if per_eng["pool"]:
                @block.gpsimd
                def _(eng):
                    emit_stream("pool", eng)

        self.nops_total += len(ops)
        bv = [(self.esem[e], self.ecount[e]) for e in self.ENGS]
        bv += [(g.sem, g.count) for g in self.groups.values() if g.barrier]
        self._barrier_vals = bv
        self.phase += 1
        self.ops = []

    def final_wait(self):
        nc = self.nc
        bv = [(self.esem[e], self.ecount[e]) for e in self.ENGS]
        bv += [(g.sem, g.count) for g in self.groups.values()]
        with nc.Block() as block:
            @block.sync
            def _(eng):
                for sem, val in bv:
                    if val > 0:
                        eng.wait_ge(sem, val)


def _dft_tables():
    def cs(n, scale):
        k = np.arange(n, dtype=np.int64)
        m = (k[:, None] * k[None, :]) % n
        ang = 2.0 * np.pi * m.astype(np.float64) / n
        return (np.cos(ang) * scale), (np.sin(ang) * scale)
    c4, s4 = cs(NTOK, 1.0 / 64.0)
    c2, s2 = cs(NCTX, 1.0 / 16.0)
    cc, sc = cs(128, 1.0 / np.sqrt(128.0))
    bf = ml_dtypes.bfloat16
    return dict(
        dftc=c4.astype(np.float32).astype(bf), dfts=s4.astype(np.float32).astype(bf),
        dft256=np.stack([c2, s2]).astype(np.float32).astype(bf),
        cc128=np.stack([cc, -sc]).astype(np.float32).astype(bf),
    )


def _attn_index_tables():
    rows, W, kr, KC = 64, 64, 8, 16
    row_start = np.clip(np.arange(rows) - kr // 2, 0, rows - kr)
    col_start = np.clip(np.arange(W) - KC // 2, 0, W - KC)
    dr = np.zeros((128, TABW), np.int64)
    dc = np.zeros((128, TABW), np.int64)
    ok = np.zeros((128, TABW), bool)

    def fill(colbase, qb, chunk):
        for p in range(128):
            rho = 2 * chunk + p // 64
            kap = p % 64
            for qi in range(256):
                r = 4 * qb + qi // 64
                c = qi % 64
                inwin = (row_start[r] <= rho <= row_start[r] + kr - 1) and (col_start[c] <= kap <= col_start[c] + KC - 1)
                if inwin:
                    ok[p, colbase + qi] = True
                    dr[p, colbase + qi] = rho - r + 7
                    dc[p, colbase + qi] = kap - c + 15
    for k in range(6):
        fill(k * 256, 2, 2 * 2 - 2 + k)
    for k in range(4):
        fill(1536 + k * 256, 0, k)
    for k in range(4):
        fill(2560 + k * 256, 15, 28 + k)
    return dr, dc, ok


_CONST_CACHE = {}


def _consts():
    if not _CONST_CACHE:
        _CONST_CACHE.update(_dft_tables())
        _CONST_CACHE["attn_idx"] = _attn_index_tables()
        _CONST_CACHE["ident"] = np.eye(128, dtype=np.float32)
        _CONST_CACHE["onesb"] = np.ones((128, 128), dtype=ml_dtypes.bfloat16)
    return _CONST_CACHE


def _verify_interior_pattern():
    return True


class Seg:
    pass


def build_program(debug=(), stop_after=None):
    nc = bass.Bass("TRN2", target_bir_lowering=False)
    dbg = set(debug)

    def din(name, shape, dt=F32):
        return nc.dram_tensor(name, list(shape), dt, kind="ExternalInput").ap()

    def scratch(name, shape, dt):
        kind = "ExternalOutput" if name in dbg else "Internal"
        return nc.dram_tensor(name, list(shape), dt, kind=kind).ap()

    x_in = din("x", [NTOK, D])
    ctx_in = din("ctx", [NCTX, D])
    ccT_in = din("ccT", [128, 32])
    w_ada = din("w_ada", [DEPTH, D, 6 * D])
    w_in = din("w_in", [DEPTH, D, 2 * D])
    w_out = din("w_out", [DEPTH, D, D])
    w_up = din("w_up", [DEPTH, D, 2 * DFF])
    w_down = din("w_down", [DEPTH, DFF, D])
    vecs_in = din("vecs", [DEPTH, 128, 512])
    wfour_in = din("wfour", [DEPTH, 8, 128, 128])
    tab_in = din("tab", [DEPTH, 16, 128, TABW])
    dftc_in = din("dftc", [NTOK, NTOK], BF16)
    dfts_in = din("dfts", [NTOK, NTOK], BF16)
    dft256_in = din("dft256", [2, NCTX, NCTX], BF16)
    cc128_in = din("cc128", [2, 128, 128], BF16)
    ident_in = din("ident", [128, 128])
    onesb_in = din("onesb", [128, 128], BF16)
    out_d = nc.dram_tensor("out", [NTOK, D], F32, kind="ExternalOutput").ap()

    winb = [scratch(f"winb{l}", [D, 2 * D], BF16) for l in range(DEPTH)]
    woutb = [scratch(f"woutb{l}", [D, D], BF16) for l in range(DEPTH)]
    wupT = [scratch(f"wupT{l}", [NPAIR, 128, 16, 256], BF16) for l in range(DEPTH)]
    wdnT = [scratch(f"wdnT{l}", [16, 128, NPAIR, 128], BF16) for l in range(DEPTH)]

    def mkseg(name, n, r):
        s = Seg()
        s.name = name
        s.n = n
        s.r = r
        s.xT = [scratch(f"{name}_xT{i}", [16, 128, n], F32) for i in range(2)]
        s.qkT = scratch(f"{name}_qkT", [16, 128, n], BF16)
        s.vt = scratch(f"{name}_vt", [8, 128, n // 128, 128], BF16)
        s.ftm = scratch(f"{name}_ftm", [n, 1024], BF16)
        s.mixT = scratch(f"{name}_mixT", [16, 128, n], BF16)
        s.fT = scratch(f"{name}_fT", [16, 128, n], F32)
        return s

    SM = mkseg("m", NTOK, 0)
    SC = mkseg("c", NCTX, 1)

    S = Sched(nc)
    conv_grp = {}

    pers = {}

    def palloc(name, shape, dt):
        t = nc.alloc_sbuf_tensor(name, list(shape), dt)
        pers[name] = t
        return t

    ident = palloc("ident_s", [128, 128], F32)
    onesb = palloc("onesb_s", [128, 128], BF16)
    epst = palloc("eps_s", [128, 1], F32)
    vecs = [palloc(f"vecs{l}", [128, 512], F32) for l in range(DEPTH)]
    modsT = [palloc(f"modsT{l}", [128, 96, 2], F32) for l in range(DEPTH)]
    der = [{k: palloc(f"der{l}_{k}", [128, 16, 2], F32) for k in ("Aa", "GA", "Af", "GF")} for l in range(DEPTH)]
    ps = [nc.alloc_psum_tensor(f"psb{i}", [128, 512], F32) for i in range(8)]

    V_GPM, V_GPOM, V_GPF, V_GPOF, V_BADA, V_CW0, V_CW1, V_CW2, V_CB = 0, 16, 32, 48, 64, 160, 248, 336, 424

    state = {"done": False}

    def end_phase(name):
        S.emit_phase()
        if stop_after is not None and name == stop_after:
            state["done"] = True
        return state["done"]

    def PSB():
        return bufs("ps", 8, excl=True)

    def convert(l, which):
        if "win" in which:
            key = f"cv_win{l}"
            for i in range(4):
                S.op("pool", lambda e, l=l, i=i: e.dma_start(out=winb[l][i * 512:(i + 1) * 512, :], in_=w_in[l, i * 512:(i + 1) * 512, :]),
                     dma=key, nobarrier=True)
            conv_grp[key] = S.groups[key]
        if "wout" in which:
            key = f"cv_wout{l}"
            for i in range(2):
                S.op("pool", lambda e, l=l, i=i: e.dma_start(out=woutb[l][i * 1024:(i + 1) * 1024, :], in_=w_out[l, i * 1024:(i + 1) * 1024, :]),
                     dma=key, nobarrier=True)
            conv_grp[key] = S.groups[key]
        if "wup" in which:
            key = f"cv_wup{l}"
            for i in range(NPAIR):
                for half in range(2):
                    src = w_up[l, :, half * DFF + i * 128: half * DFF + (i + 1) * 128].rearrange("(kc p) e -> p kc e", p=128)
                    dst = wupT[l][i, :, :, half * 128:(half + 1) * 128]
                    S.op("pool", lambda e, src=src, dst=dst: e.dma_start(out=dst, in_=src), dma=key, nobarrier=True)
            conv_grp[key] = S.groups[key]
        if "wdn" in which:
            key = f"cv_wdn{l}"
            for m in range(16):
                src = w_down[l, :, m * 128:(m + 1) * 128].rearrange("(kc p) e -> p kc e", p=128)
                dst = wdnT[l][m]
                S.op("pool", lambda e, src=src, dst=dst: e.dma_start(out=dst, in_=src), dma=key, nobarrier=True)
            conv_grp[key] = S.groups[key]

    def cwait(key):
        g = conv_grp[key]
        return [(g.sem, g.count)]

    def phase_ada(l, first):
        PS = PSB()
        with (nc.sbuf_tensor(f"ada_cc{l}", [128, 32], F32) as cc_s,
              nc.sbuf_tensor(f"ada_sT{l}", [128, 16, 2], BF16) as sT,
              nc.sbuf_tensor(f"ada_w{l}", [128, 3, 16, 512], BF16) as wad,
              nc.sbuf_tensor(f"ada_mrow{l}", [2, 6 * D], F32) as mrow):
            Bc = Buf("cc"); BsT = Buf("sT"); BW = bufs("wad", 3); BM = bufs("mrow", 24)
            Bconst = Buf("const"); Bvec = Buf("vecs"); Bmods = Buf("mods"); Bder = Buf("der")
            if first:
                S.op("sp", lambda e: e.dma_start(out=ident[:], in_=ident_in), writes=[Bconst], dma="ld0")
                S.op("sp", lambda e: e.dma_start(out=onesb[:], in_=onesb_in), writes=[Bconst], dma="ld0")
                S.op("dve", lambda e: e.memset(epst[:], EPS), writes=[Bconst])
                for ll in range(DEPTH):
                    S.op("sp", lambda e, ll=ll: e.dma_start(out=vecs[ll][:], in_=vecs_in[ll]), writes=[Bvec], dma="ld0")
            S.op("sp", lambda e: e.dma_start(out=cc_s[:], in_=ccT_in), writes=[Bc], dma="ld0")
            S.op("act", lambda e: e.activation(out=sT[:].rearrange("p k r -> p (k r)"), in_=cc_s[:], func=AF.Silu), reads=[Bc], writes=[BsT])
            for nt in range(24):
                sl = nt % 3
                src = w_ada[l, :, nt * 512:(nt + 1) * 512].rearrange("(kc p) n -> p kc n", p=128)
                S.op("pool", lambda e, sl=sl, src=src: e.dma_start(out=wad[:, sl], in_=src), writes=[BW[sl]], dma=f"wad{sl}")
                b = nt % 4
                for kc in range(16):
                    S.op("pe", lambda e, b=b, kc=kc, sl=sl: e.matmul(ps[b][0:2, :], lhsT=sT[:, kc, :], rhs=wad[:, sl, kc, :], start=(kc == 0), stop=(kc == 15)),
                         reads=[BsT, BW[sl]], writes=[PS[b]])
                if nt % 2 == 0:
                    S.op("act", lambda e, b=b, nt=nt: e.activation(out=mrow[0:2, nt * 512:(nt + 1) * 512], in_=ps[b][0:2, :], func=AF.Copy), reads=[PS[b]], writes=[BM[nt]])
                else:
                    S.op("dve", lambda e, b=b, nt=nt: e.tensor_copy(out=mrow[0:2, nt * 512:(nt + 1) * 512], in_=ps[b][0:2, :]), reads=[PS[b]], writes=[BM[nt]])
            if first:
                convert(0, ("win", "wout"))
            for c in range(96):
                S.op("pe", lambda e, c=c: e.transpose(ps[4][:, 2 * c:2 * c + 2], mrow[0:2, c * 128:(c + 1) * 128], ident[0:2, 0:2]),
                     reads=[BM[c // 4], Bconst], writes=[PS[4]])
            S.op("dve", lambda e: e.tensor_tensor(out=modsT[l][:], in0=ps[4][:, 0:192].rearrange("p (c r) -> p c r", r=2),
                                                  in1=vecs[l][:, V_BADA:V_BADA + 96].unsqueeze(2).to_broadcast([128, 96, 2]), op=ALU.add),
                 reads=[PS[4], Bvec], writes=[Bmods])

            def bc(col):
                return vecs[l][:, col:col + 16].unsqueeze(2).to_broadcast([128, 16, 2])
            S.op("dve", lambda e: e.scalar_tensor_tensor(out=der[l]["Aa"][:], in0=modsT[l][:, 16:32, :], scalar=1.0, in1=bc(V_GPM), op0=ALU.add, op1=ALU.mult),
                 reads=[Bmods, Bvec], writes=[Bder])
            S.op("dve", lambda e: e.tensor_tensor(out=der[l]["GA"][:], in0=modsT[l][:, 32:48, :], in1=bc(V_GPOM), op=ALU.mult), reads=[Bmods, Bvec], writes=[Bder])
            S.op("dve", lambda e: e.scalar_tensor_tensor(out=der[l]["Af"][:], in0=modsT[l][:, 64:80, :], scalar=1.0, in1=bc(V_GPF), op0=ALU.add, op1=ALU.mult),
                 reads=[Bmods, Bvec], writes=[Bder])
            S.op("dve", lambda e: e.tensor_tensor(out=der[l]["GF"][:], in0=modsT[l][:, 80:96, :], in1=bc(V_GPOF), op=ALU.mult), reads=[Bmods, Bvec], writes=[Bder])
            if "dbg_mods" in dbg:
                dm = nc.dram_tensor(f"dbg_mods{l}", [128, 96, 2], F32, kind="ExternalOutput").ap()
                S.op("sp", lambda e: e.dma_start(out=dm, in_=modsT[l][:]), reads=[Bmods], dma="dbgst")
            return end_phase(f"ADA{l}")

    def phase_tin():
        PS = PSB()
        with (nc.sbuf_tensor("tin_x", [128, 2, 4, D], F32) as X,
              nc.sbuf_tensor("tin_xt", [128, 2, 16, 512], F32) as XT):
            BX = bufs("X", 2)
            BXT = [bufs(f"XT{s}_", 16) for s in range(2)]
            cnt = 0
            g = 0
            for seg, src in ((SM, x_in), (SC, ctx_in)):
                NT = min(512, seg.n)
                J = NT // 128
                for t in range(seg.n // NT):
                    s = g % 2
                    S.op("sp", lambda e, s=s, J=J, t=t, NT=NT, src=src: e.dma_start(out=X[:, s, 0:J, :], in_=src[t * NT:(t + 1) * NT, :].rearrange("(j p) f -> p j f", p=128)),
                         writes=[BX[s]], dma=f"tinx{s}")
                    for c in range(16):
                        b = cnt % 6
                        cnt += 1
                        for j in range(J):
                            S.op("pe", lambda e, b=b, j=j, s=s, c=c: e.transpose(ps[b][:, j * 128:(j + 1) * 128], X[:, s, j, c * 128:(c + 1) * 128], ident[:]),
                                 reads=[BX[s]], writes=[PS[b]])
                        if c % 2 == 0:
                            S.op("act", lambda e, b=b, s=s, c=c, NT=NT: e.activation(out=XT[:, s, c, 0:NT], in_=ps[b][:, 0:NT], func=AF.Copy), reads=[PS[b]], writes=[BXT[s][c]])
                        else:
                            S.op("dve", lambda e, b=b, s=s, c=c, NT=NT: e.tensor_copy(out=XT[:, s, c, 0:NT], in_=ps[b][:, 0:NT]), reads=[PS[b]], writes=[BXT[s][c]])
                    dst = seg.xT[0][:, :, t * NT:(t + 1) * NT].rearrange("c p t -> p c t")
                    S.op("act", lambda e, dst=dst, s=s, NT=NT: e.dma_start(out=dst, in_=XT[:, s, :, 0:NT]), reads=BXT[s], dma=f"tinst{s}")
                    g += 1
            return end_phase("TIN")

    def emit_rstd(PS, bank, rstd_ap, Brstd, ncol):
        S.op("act", lambda e: e.activation(out=rstd_ap, in_=ps[bank][:, 0:ncol], func=AF.Sqrt, scale=1.0 / D, bias=epst[:, 0:1]),
             reads=[PS[bank]], writes=[Brstd])
        S.op("dve", lambda e: e.reciprocal(out=rstd_ap, in_=rstd_ap), reads=[Brstd], writes=[Brstd])

    def phase_a(l, xi, segs, addf=False):
        PS = PSB()
        with (nc.sbuf_tensor(f"a_xt{l}", [128, 16, 512], F32) as XT,
              nc.sbuf_tensor(f"a_xf{l}", [128, 16, 512 if addf else 1], F32) as XF,
              nc.sbuf_tensor(f"a_sq{l}", [128, 2, 512], BF16) as SQ,
              nc.sbuf_tensor(f"a_ht{l}", [128, 2, 16, 512], BF16) as HT,
              nc.sbuf_tensor(f"a_w{l}", [128, 2, 16, 512], BF16) as W,
              nc.sbuf_tensor(f"a_qk{l}", [128, 2, 16, 512], BF16) as QK,
              nc.sbuf_tensor(f"a_vf{l}", [128, 2, 4, 2048], BF16) as VF,
              nc.sbuf_tensor(f"a_rstd{l}", [128, 512], F32) as rstd):
            BXT = bufs("XT", 16); BSQ = bufs("SQ", 2); BHT = [bufs(f"HT{s}_", 16) for s in range(2)]
            BW = bufs("W", 2); BQK = [bufs(f"QK{s}_", 16) for s in range(2)]
            BXF = bufs("XF", 16)
            BVF = [[Buf(f"VF{s}_{j}_{n}") for j in range(4) for n in range(4)] for s in range(2)]
            Brstd = Buf("rstd")
            widx = 0
            ecnt = 0
            g = 0
            bcnt = 0
            for seg in segs:
                NT = min(512, seg.n)
                J = NT // 128
                r = seg.r
                for t in range(seg.n // NT):
                    s = g % 2
                    src = seg.xT[xi][:, :, t * NT:(t + 1) * NT].rearrange("c p t -> p c t")
                    S.op("sp", lambda e, src=src, NT=NT: e.dma_start(out=XT[:, :, 0:NT], in_=src), writes=BXT, dma="a_xt")
                    if addf:
                        fsrc = seg.fT[:, :, t * NT:(t + 1) * NT].rearrange("c p t -> p c t")
                        S.op("sp", lambda e, fsrc=fsrc, NT=NT: e.dma_start(out=XF[:, :, 0:NT], in_=fsrc), writes=BXF, dma="a_xf")
                        for c in range(16):
                            S.op("pool", lambda e, c=c, NT=NT: e.tensor_tensor(out=XT[:, c, 0:NT], in0=XT[:, c, 0:NT], in1=XF[:, c, 0:NT], op=ALU.add),
                                 reads=[BXT[c], BXF[c]], writes=[BXT[c]])
                        S.op("act", lambda e, src=src, NT=NT: e.dma_start(out=src, in_=XT[:, :, 0:NT]), reads=BXT, dma="a_xst")
                    for c in range(16):
                        S.op("act", lambda e, c=c, NT=NT: e.activation(out=SQ[:, c % 2, 0:NT], in_=XT[:, c, 0:NT], func=AF.Square), reads=[BXT[c]], writes=[BSQ[c % 2]])
                        S.op("pe", lambda e, c=c, NT=NT: e.matmul(ps[7][:, 0:NT], lhsT=onesb[:], rhs=SQ[:, c % 2, 0:NT], start=(c == 0), stop=(c == 15)),
                             reads=[BSQ[c % 2]], writes=[PS[7]])
                    emit_rstd(PS, 7, rstd[:, 0:NT], Brstd, NT)
                    for c in range(16):
                        S.op("dve", lambda e, c=c, NT=NT, r=r: e.scalar_tensor_tensor(out=XT[:, c, 0:NT], in0=XT[:, c, 0:NT], scalar=der[l]["Aa"][:, c, r:r + 1],
                                                                                   in1=rstd[:, 0:NT], op0=ALU.mult, op1=ALU.mult),
                             reads=[BXT[c], Brstd], writes=[BXT[c]])
                        S.op("act", lambda e, c=c, NT=NT, r=r, s=s: e.activation(out=HT[:, s, c, 0:NT], in_=XT[:, c, 0:NT], func=AF.Identity,
                                                                             bias=modsT[l][:, c, r:r + 1]),
                             reads=[BXT[c]], writes=[BHT[s][c]])
                    for i in range(8):
                        sl = widx % 2
                        widx += 1
                        wsrc = winb[l][:, i * 512:(i + 1) * 512].rearrange("(kc p) n -> p kc n", p=128)
                        S.op("sp", lambda e, sl=sl, wsrc=wsrc: e.dma_start(out=W[:, sl], in_=wsrc), writes=[BW[sl]], dma=f"a_w{sl}", extra=cwait(f"cv_win{l}"))
                        if i < 4:
                            for mm in range(4):
                                m = 4 * i + mm
                                b = bcnt % 6
                                bcnt += 1
                                for kc in range(16):
                                    S.op("pe", lambda e, b=b, sl=sl, kc=kc, mm=mm, s=s, NT=NT: e.matmul(ps[b][:, 0:NT], lhsT=W[:, sl, kc, mm * 128:(mm + 1) * 128], rhs=HT[:, s, kc, 0:NT],
                                                                                                  start=(kc == 0), stop=(kc == 15)),
                                         reads=[BW[sl], BHT[s][kc]], writes=[PS[b]])
                                if ecnt % 2 == 0:
                                    S.op("act", lambda e, b=b, s=s, m=m, NT=NT: e.activation(out=QK[:, s, m, 0:NT], in_=ps[b][:, 0:NT], func=AF.Copy), reads=[PS[b]], writes=[BQK[s][m]])
                                else:
                                    S.op("dve", lambda e, b=b, s=s, m=m, NT=NT: e.tensor_copy(out=QK[:, s, m, 0:NT], in_=ps[b][:, 0:NT]), reads=[PS[b]], writes=[BQK[s][m]])
                                ecnt += 1
                        else:
                            nb = i - 4
                            for j in range(J):
                                b = bcnt % 6
                                bcnt += 1
                                for kc in range(16):
                                    S.op("pe", lambda e, b=b, sl=sl, kc=kc, j=j, s=s: e.matmul(ps[b][:], lhsT=HT[:, s, kc, j * 128:(j + 1) * 128], rhs=W[:, sl, kc, :],
                                                                                         start=(kc == 0), stop=(kc == 15)),
                                         reads=[BW[sl], BHT[s][kc]], writes=[PS[b]])
                                if ecnt % 2 == 0:
                                    S.op("act", lambda e, b=b, s=s, j=j, nb=nb: e.activation(out=VF[:, s, j, nb * 512:(nb + 1) * 512], in_=ps[b][:], func=AF.Copy), reads=[PS[b]], writes=[BVF[s][j * 4 + nb]])
                                else:
                                    S.op("dve", lambda e, b=b, s=s, j=j, nb=nb: e.tensor_copy(out=VF[:, s, j, nb * 512:(nb + 1) * 512], in_=ps[b][:]), reads=[PS[b]], writes=[BVF[s][j * 4 + nb]])
                                ecnt += 1
                    dst = seg.qkT[:, :, t * NT:(t + 1) * NT].rearrange("c p t -> p c t")
                    S.op("act", lambda e, dst=dst, s=s, NT=NT: e.dma_start(out=dst, in_=QK[:, s, :, 0:NT]), reads=BQK[s], dma=f"a_st{s}")
                    for j in range(J):
                        dstv = seg.vt[:, :, t * J + j, :].rearrange("h p c -> p h c")
                        S.op("act", lambda e, dstv=dstv, s=s, j=j: e.dma_start(out=dstv, in_=VF[:, s, j, 0:1024].rearrange("p (h c) -> p h c", c=128)),
                             reads=BVF[s][j * 4:j * 4 + 2], dma=f"a_st{s}")
                    dstf = seg.ftm[t * NT:(t + 1) * NT, :].rearrange("(j p) c -> p j c", p=128)
                    S.op("act", lambda e, dstf=dstf, s=s, J=J: e.dma_start(out=dstf, in_=VF[:, s, 0:J, 1024:2048]), reads=BVF[s], dma=f"a_st{s}")
                    g += 1
            return end_phase(f"A{l}")

    def phase_b(l, with_ctx_queries):
        PS = PSB()
        with (nc.sbuf_tensor(f"b_q{l}", [128, 2, NTOK], BF16) as QT,
              nc.sbuf_tensor(f"b_qc{l}", [128, 2, NCTX], BF16) as QC,
              nc.sbuf_tensor(f"b_k{l}", [128, 2, NTOK + NCTX], BF16) as KT,
              nc.sbuf_tensor(f"b_vr{l}", [128, 2, 34, 128], BF16) as VR,
              nc.sbuf_tensor(f"b_va{l}", [128, 2, 2, 34, 128], BF16) as VA,
              nc.sbuf_tensor(f"b_ebf{l}", [128, 2, TABW], F32) as EBF,
              nc.sbuf_tensor(f"b_eb{l}", [128, 2, TABW], BF16) as EB,
              nc.sbuf_tensor(f"b_pt{l}", [128, 8, 256], BF16) as PT,
              nc.sbuf_tensor(f"b_rd{l}", [128, 2, 256], F32) as RD,
              nc.sbuf_tensor(f"b_at{l}", [128, 2, NTOK], BF16) as AT,
              nc.sbuf_tensor(f"b_atc{l}", [128, 2, NCTX], BF16) as ATC):
            BQ = bufs("Q", 2); BQC = bufs("QC", 2); BK = bufs("K", 2); BVR = bufs("VR", 2)
            BVA = [[Buf(f"VA{s}{h}") for h in range(2)] for s in range(2)]
            BEBF = bufs("EBF", 2); BEB = bufs("EB", 2); BPT = bufs("PT", 8); BRD = bufs("RD", 2)
            BAT = [[Buf(f"AT{s}_{q}") for q in range(16)] for s in range(2)]
            BATC = bufs("ATC", 2)
            Bones = Buf("ones")
            convert(l, ("wup", "wdn"))
            for s in range(2):
                S.op("dve", lambda e, s=s: e.memset(VA[:, s, 0, :, 64:128], 1.0), writes=[BVA[s][0]])
                S.op("dve", lambda e, s=s: e.memset(VA[:, s, 1, :, 0:64], 1.0), writes=[BVA[s][1]])
            ebi = 0
            ptc = 0
            stc = 0
            acc_i = 0
            rdc = 0
            for hp in range(8):
                s = hp % 2
                S.op("sp", lambda e, s=s, hp=hp: e.dma_start(out=QT[:, s, :], in_=SM.qkT[hp]), writes=[BQ[s]], dma=f"b_ld{s}")
                S.op("sp", lambda e, s=s, hp=hp: e.dma_start(out=KT[:, s, 0:NTOK], in_=SM.qkT[8 + hp]), writes=[BK[s]], dma=f"b_ld{s}")
                S.op("sp", lambda e, s=s, hp=hp: e.dma_start(out=KT[:, s, NTOK:NTOK + NCTX], in_=SC.qkT[8 + hp]), writes=[BK[s]], dma=f"b_ld{s}")
                S.op("sp", lambda e, s=s, hp=hp: e.dma_start(out=VR[:, s, 0:32, :], in_=SM.vt[hp]), writes=[BVR[s]], dma=f"b_ld{s}")
                S.op("sp", lambda e, s=s, hp=hp: e.dma_start(out=VR[:, s, 32:34, :], in_=SC.vt[hp]), writes=[BVR[s]], dma=f"b_ld{s}")
                if with_ctx_queries:
                    S.op("sp", lambda e, s=s, hp=hp: e.dma_start(out=QC[:, s, :], in_=SC.qkT[hp]), writes=[BQC[s]], dma=f"b_ld{s}")
                S.op("dve", lambda e, s=s: e.tensor_copy(out=VA[:, s, 0, :, 0:64], in_=VR[:, s, :, 0:64]), reads=[BVR[s]], writes=[BVA[s][0]])
                S.op("act", lambda e, s=s: e.activation(out=VA[:, s, 1, :, 64:128], in_=VR[:, s, :, 64:128], func=AF.Copy), reads=[BVR[s]], writes=[BVA[s][1]])
                steps = []
                for hh in range(2):
                    h = 2 * hp + hh
                    es = ebi % 2
                    ebi += 1
                    S.op("sp", lambda e, es=es, h=h: e.dma_start(out=EBF[:, es, :], in_=tab_in[l, h]), writes=[BEBF[es]], dma=f"b_eb{es}")
                    S.op("act", lambda e, es=es: e.activation(out=EB[:, es, :], in_=EBF[:, es, :], func=AF.Exp), reads=[BEBF[es]], writes=[BEB[es]])
                    qbs = list(range(16)) + ([16] if with_ctx_queries else [])
                    for qb in qbs:
                        if qb == 16:
                            lst = [(32, None), (33, None)]
                        else:
                            if qb == 0:
                                loc = [(k, 1536 + k * 256) for k in range(4)]
                            elif qb == 15:
                                loc = [(28 + k, 2560 + k * 256) for k in range(4)]
                            else:
                                loc = [(2 * qb - 2 + k, k * 256) for k in range(6)]
                            lst = loc + [(32, None), (33, None)]
                        for k, (chunk, mcol) in enumerate(lst):
                            steps.append(dict(hh=hh, es=es, qb=qb, chunk=chunk, mcol=mcol, first=(k == 0), last=(k == len(lst) - 1)))
                hr = [slice(0, 64), slice(64, 128)]
                pending = []

                def emit_qk(st):
                    nonlocal stc
                    b = (0, 1, 2, 5, 6)[stc % 5]
                    stc += 1
                    st["b"] = b
                    hh = st["hh"]
                    if st["qb"] == 16:
                        rhs = QC[hr[hh], s, :]
                        rb = BQC[s]
                    else:
                        rhs = QT[hr[hh], s, st["qb"] * 256:(st["qb"] + 1) * 256]
                        rb = BQ[s]
                    ch = st["chunk"]
                    S.op("pe", lambda e, b=b, hh=hh, ch=ch, rhs=rhs, s=s: e.matmul(ps[b][:, 0:256], lhsT=KT[hr[hh], s, ch * 128:(ch + 1) * 128], rhs=rhs, start=True, stop=True),
                         reads=[BK[s], rb], writes=[PS[b]])

                def emit_sm(st):
                    nonlocal ptc
                    p = ptc % 8
                    ptc += 1
                    st["p"] = p
                    b = st["b"]
                    S.op("act", lambda e, b=b, p=p: e.activation(out=PT[:, p, :], in_=ps[b][:, 0:256], func=AF.Exp, scale=SCALE), reads=[PS[b]], writes=[BPT[p]])
                    if st["mcol"] is not None:
                        mc = st["mcol"]
                        es = st["es"]
                        S.op("dve", lambda e, p=p, mc=mc, es=es: e.tensor_tensor(out=PT[:, p, :], in0=PT[:, p, :], in1=EB[:, es, mc:mc + 256], op=ALU.mult),
                             reads=[BPT[p], BEB[es]], writes=[BPT[p]])

                def emit_pv(st):
                    nonlocal acc_i, rdc
                    hh = st["hh"]
                    if st["first"]:
                        acc_i += 1
                    ab = 3 + (acc_i % 2)
                    p = st["p"]
                    ch = st["chunk"]
                    S.op("pe", lambda e, ab=ab, hh=hh, ch=ch, p=p, st=st, s=s: e.matmul(ps[ab][:, 0:256], lhsT=VA[:, s, hh, ch, :], rhs=PT[:, p, :], start=st["first"], stop=st["last"]),
                         reads=[BVA[s][hh], BPT[p]], writes=[PS[ab]])
                    if st["last"]:
                        rs = rdc % 2
                        rdc += 1
                        num = hr[hh]
                        den = hr[1 - hh]
                        qb = st["qb"]
                        S.op("act", lambda e, ab=ab, rs=rs, num=num, den=den: e.activation(out=RD[num, rs, :], in_=ps[ab][den, 0:256], func=AF.Ln), reads=[PS[ab]], writes=[BRD[rs]])

                        def fin(ab=ab, rs=rs, num=num, qb=qb, s=s):
                            S.op("act", lambda e: e.activation(out=RD[num, rs, :], in_=RD[num, rs, :], func=AF.Exp, scale=-1.0), reads=[BRD[rs]], writes=[BRD[rs]])
                            if qb == 16:
                                S.op("dve", lambda e: e.tensor_tensor(out=ATC[num, s, :], in0=ps[ab][num, 0:256], in1=RD[num, rs, :], op=ALU.mult),
                                     reads=[PS[ab], BRD[rs]], writes=[BATC[s]])
                            else:
                                S.op("dve", lambda e: e.tensor_tensor(out=AT[num, s, qb * 256:(qb + 1) * 256], in0=ps[ab][num, 0:256], in1=RD[num, rs, :], op=ALU.mult),
                                     reads=[PS[ab], BRD[rs]], writes=[BAT[s][qb]])
                        pending.append(fin)

                n = len(steps)
                LA = 4
                for i in range(min(LA, n)):
                    emit_qk(steps[i])
                for i in range(n):
                    emit_sm(steps[i])
                    fl = pending[:]
                    del pending[:]
                    for f_ in fl:
                        f_()
                    if i + LA < n:
                        emit_qk(steps[i + LA])
                    emit_pv(steps[i])
                for f_ in pending:
                    f_()
                del pending[:]
                S.op("act", lambda e, s=s, hp=hp: e.dma_start(out=SM.mixT[hp], in_=AT[:, s, :]), reads=BAT[s], dma=f"b_st{s}")
                if with_ctx_queries:
                    S.op("act", lambda e, s=s, hp=hp: e.dma_start(out=SC.mixT[hp], in_=ATC[:, s, :]), reads=[BATC[s]], dma=f"b_st{s}")
            return end_phase(f"B{l}")

    def phase_c(l, segs):
        PS = PSB()
        with (nc.sbuf_tensor(f"c_f{l}", [128, 32, 1024], BF16) as Ft,
              nc.sbuf_tensor(f"c_d{l}", [128, 2, 32, 512], BF16) as DC,
              nc.sbuf_tensor(f"c_yc{l}", [128, 8, 512], BF16) as YC,
              nc.sbuf_tensor(f"c_ys{l}", [128, 2, 512], BF16) as YS,
              nc.sbuf_tensor(f"c_g{l}", [128, 2, 8, 128], BF16) as G,
              nc.sbuf_tensor(f"c_wf{l}", [128, 8, 128], F32) as WF,
              nc.sbuf_tensor(f"c_wfb{l}", [128, 8, 128], BF16) as WFB,
              nc.sbuf_tensor(f"c_cc{l}", [128, 2, 128], BF16) as CCt,
              nc.sbuf_tensor(f"c_fo{l}", [128, 2, 8, 512], BF16) as FO):
            BF_ = Buf("F"); BD = bufs("DC", 2); BYC = bufs("YC", 8); BYS = bufs("YS", 2); BG = Buf("G"); BWF = Buf("WF"); BWFB = Buf("WFB")
            BCC = Buf("CC"); BFO = [bufs(f"FO{s}_", 8) for s in range(2)]
            if l == 0:
                convert(1, ("win", "wout"))
            S.op("sp", lambda e: e.dma_start(out=WF[:], in_=wfour_in[l].rearrange("g c e -> c g e")), writes=[BWF], dma="c_ld")
            S.op("sp", lambda e: e.dma_start(out=CCt[:], in_=cc128_in.rearrange("t c e -> c t e")), writes=[BCC], dma="c_ld")
            S.op("act", lambda e: e.activation(out=WFB[:], in_=WF[:], func=AF.Copy), reads=[BWF], writes=[BWFB])
            bcnt = 0
            for tt in range(2):
                for g in range(8):
                    b = bcnt % 6
                    bcnt += 1
                    S.op("pe", lambda e, b=b, tt=tt, g=g: e.matmul(ps[b][:, 0:128], lhsT=CCt[:, tt, :], rhs=WFB[:, g, :], start=True, stop=True), reads=[BCC, BWFB], writes=[PS[b]])
                    S.op("dve", lambda e, b=b, tt=tt, g=g: e.tensor_copy(out=G[:, tt, g, :], in_=ps[b][:, 0:128]), reads=[PS[b]], writes=[BG])
            dci = 0
            fo_i = 0
            ecnt = 0
            for seg in segs:
                nch = seg.n // 128
                NT = min(512, seg.n)
                S.op("sp", lambda e, seg=seg, nch=nch: e.dma_start(out=Ft[:, 0:nch, :], in_=seg.ftm.rearrange("(nc p) c -> p nc c", p=128)), writes=[BF_], dma="c_ld")
                for t in range(seg.n // NT):
                    dsl = []
                    for tt in range(2):
                        ds = dci % 2
                        dci += 1
                        if seg.n == NTOK:
                            srcm = (dftc_in if tt == 0 else dfts_in)[:, t * NT:(t + 1) * NT]
                        else:
                            srcm = dft256_in[tt]
                        srcm = srcm.rearrange("(nc p) m -> p nc m", p=128)
                        S.op("sp", lambda e, ds=ds, srcm=srcm, nch=nch, NT=NT: e.dma_start(out=DC[:, ds, 0:nch, 0:NT], in_=srcm), writes=[BD[ds]], dma=f"c_d{ds}")
                        dsl.append(ds)
                    fs = fo_i % 2
                    fo_i += 1
                    for g in range(8):
                        b = bcnt % 6
                        bcnt += 1
                        ds = dsl[0]
                        for n_ in range(nch):
                            S.op("pe", lambda e, b=b, g=g, n_=n_, ds=ds, NT=NT, nch=nch: e.matmul(ps[b][:, 0:NT], lhsT=Ft[:, n_, g * 128:(g + 1) * 128], rhs=DC[:, ds, n_, 0:NT],
                                                                                            start=(n_ == 0), stop=(n_ == nch - 1)),
                                 reads=[BF_, BD[ds]], writes=[PS[b]])
                        if ecnt % 2 == 0:
                            S.op("act", lambda e, b=b, g=g, NT=NT: e.activation(out=YC[:, g, 0:NT], in_=ps[b][:, 0:NT], func=AF.Copy), reads=[PS[b]], writes=[BYC[g]])
                        else:
                            S.op("dve", lambda e, b=b, g=g, NT=NT: e.tensor_copy(out=YC[:, g, 0:NT], in_=ps[b][:, 0:NT]), reads=[PS[b]], writes=[BYC[g]])
                        ecnt += 1
                    pend = None
                    for g in range(8):
                        b = bcnt % 6
                        bcnt += 1
                        ds = dsl[1]
                        for n_ in range(nch):
                            S.op("pe", lambda e, b=b, g=g, n_=n_, ds=ds, NT=NT, nch=nch: e.matmul(ps[b][:, 0:NT], lhsT=Ft[:, n_, g * 128:(g + 1) * 128], rhs=DC[:, ds, n_, 0:NT],
                                                                                            start=(n_ == 0), stop=(n_ == nch - 1)),
                                 reads=[BF_, BD[ds]], writes=[PS[b]])
                        ys = g % 2
                        if ecnt % 2 == 0:
                            S.op("act", lambda e, b=b, ys=ys, NT=NT: e.activation(out=YS[:, ys, 0:NT], in_=ps[b][:, 0:NT], func=AF.Copy), reads=[PS[b]], writes=[BYS[ys]])
                        else:
                            S.op("dve", lambda e, b=b, ys=ys, NT=NT: e.tensor_copy(out=YS[:, ys, 0:NT], in_=ps[b][:, 0:NT]), reads=[PS[b]], writes=[BYS[ys]])
                        ecnt += 1

                        def stage2(g=g, ys=ys, NT=NT, fs=fs):
                            nonlocal ecnt
                            b2 = 6 + (g % 2)
                            S.op("pe", lambda e: e.matmul(ps[b2][:, 0:NT], lhsT=G[:, 0, g, :], rhs=YC[:, g, 0:NT], start=True, stop=False), reads=[BG, BYC[g]], writes=[PS[b2]])
                            S.op("pe", lambda e: e.matmul(ps[b2][:, 0:NT], lhsT=G[:, 1, g, :], rhs=YS[:, ys, 0:NT], start=False, stop=True), reads=[BG, BYS[ys]], writes=[PS[b2]])
                            if ecnt % 2 == 0:
                                S.op("act", lambda e: e.activation(out=FO[:, fs, g, 0:NT], in_=ps[b2][:, 0:NT], func=AF.Copy), reads=[PS[b2]], writes=[BFO[fs][g]])
                            else:
                                S.op("dve", lambda e: e.tensor_copy(out=FO[:, fs, g, 0:NT], in_=ps[b2][:, 0:NT]), reads=[PS[b2]], writes=[BFO[fs][g]])
                            ecnt += 1
                        if pend is not None:
                            pend()
                        pend = stage2
                    pend()
                    dst = seg.mixT[8:16, :, t * NT:(t + 1) * NT].rearrange("c p t -> p c t")
                    S.op("act", lambda e, dst=dst, fs=fs, NT=NT: e.dma_start(out=dst, in_=FO[:, fs, :, 0:NT]), reads=BFO[fs], dma=f"c_st{fs}")
            return end_phase(f"C{l}")

    def phase_d(l, xi, segs):
        PS = PSB()
        with (nc.sbuf_tensor(f"d_w{l}", [128, 16, D], BF16) as W,
              nc.sbuf_tensor(f"d_mt{l}", [128, 2, 16, 512], BF16) as MT,
              nc.sbuf_tensor(f"d_xt{l}", [128, 16, 512], F32) as XT,
              nc.sbuf_tensor(f"d_y{l}", [128, 16, 512], F32) as Y,
              nc.sbuf_tensor(f"d_sq{l}", [128, 2, 512], BF16) as SQ,
              nc.sbuf_tensor(f"d_rstd{l}", [128, 512], F32) as rstd):
            BW = Buf("W"); BMT = bufs("MT", 2); BXT = bufs("XT", 16); BY = bufs("Y", 16); BSQ = bufs("SQ", 2); Brstd = Buf("rstd")
            for i in range(4):
                S.op("sp", lambda e, i=i: e.dma_start(out=W[:, 4 * i:4 * i + 4, :], in_=woutb[l][i * 512:(i + 1) * 512, :].rearrange("(kc p) n -> p kc n", p=128)),
                     writes=[BW], dma="d_w", extra=cwait(f"cv_wout{l}"))
            g = 0
            bcnt = 0
            for seg in segs:
                NT = min(512, seg.n)
                r = seg.r
                for t in range(seg.n // NT):
                    s = g % 2
                    g += 1
                    S.op("sp", lambda e, seg=seg, t=t, NT=NT, s=s: e.dma_start(out=MT[:, s, :, 0:NT], in_=seg.mixT[:, :, t * NT:(t + 1) * NT].rearrange("c p t -> p c t")),
                         writes=[BMT[s]], dma=f"d_mt{s}")
                    xsrc = seg.xT[xi][:, :, t * NT:(t + 1) * NT].rearrange("c p t -> p c t")
                    S.op("sp", lambda e, xsrc=xsrc, NT=NT: e.dma_start(out=XT[:, :, 0:NT], in_=xsrc), writes=BXT, dma="d_xt")
                    pend = None
                    for m in range(16):
                        b = bcnt % 6
                        bcnt += 1
                        for kc in range(16):
                            S.op("pe", lambda e, b=b, kc=kc, m=m, s=s, NT=NT: e.matmul(ps[b][:, 0:NT], lhsT=W[:, kc, m * 128:(m + 1) * 128], rhs=MT[:, s, kc, 0:NT],
                                                                                 start=(kc == 0), stop=(kc == 15)),
                                 reads=[BW, BMT[s]], writes=[PS[b]])
                        S.op("act", lambda e, b=b, m=m, NT=NT: e.activation(out=SQ[:, m % 2, 0:NT], in_=ps[b][:, 0:NT], func=AF.Square), reads=[PS[b]], writes=[BSQ[m % 2]])
                        S.op("dve", lambda e, b=b, m=m, NT=NT: e.tensor_copy(out=Y[:, m, 0:NT], in_=ps[b][:, 0:NT]), reads=[PS[b]], writes=[BY[m]])

                        def ssmm(m=m, NT=NT):
                            S.op("pe", lambda e: e.matmul(ps[7][:, 0:NT], lhsT=onesb[:], rhs=SQ[:, m % 2, 0:NT], start=(m == 0), stop=(m == 15)), reads=[BSQ[m % 2]], writes=[PS[7]])
                        if pend is not None:
                            pend()
                        pend = ssmm
                    pend()
                    emit_rstd(PS, 7, rstd[:, 0:NT], Brstd, NT)
                    for m in range(16):
                        S.op("dve", lambda e, m=m, NT=NT, r=r: e.scalar_tensor_tensor(out=Y[:, m, 0:NT], in0=Y[:, m, 0:NT], scalar=der[l]["GA"][:, m, r:r + 1], in1=rstd[:, 0:NT],
                                                                                   op0=ALU.mult, op1=ALU.mult), reads=[BY[m], Brstd], writes=[BY[m]])
                        S.op("pool", lambda e, m=m, NT=NT: e.tensor_tensor(out=XT[:, m, 0:NT], in0=XT[:, m, 0:NT], in1=Y[:, m, 0:NT], op=ALU.add), reads=[BXT[m], BY[m]], writes=[BXT[m]])
                    S.op("act", lambda e, xsrc=xsrc, NT=NT: e.dma_start(out=xsrc, in_=XT[:, :, 0:NT]), reads=BXT, dma="d_st")
                    xdst2 = seg.xT[1 - xi][:, :, t * NT:(t + 1) * NT].rearrange("c p t -> p c t")
                    S.op("act", lambda e, xdst2=xdst2, NT=NT: e.dma_start(out=xdst2, in_=XT[:, :, 0:NT]), reads=BXT, dma="d_st")
            return end_phase(f"D{l}")

    def phase_ef(l, xi, segs):
        PS = PSB()
        NUMAX = 460
        with (nc.sbuf_tensor(f"e_xt{l}", [128, 16, NUMAX], F32) as XT,
              nc.sbuf_tensor(f"e_sqp{l}", [128, 16, NUMAX], BF16) as SQP,
              nc.sbuf_tensor(f"e_sq{l}", [128, 2, NUMAX], BF16) as SQ,
              nc.sbuf_tensor(f"e_ht{l}", [128, 16, NUMAX], BF16) as HT,
              nc.sbuf_tensor(f"e_tmp{l}", [128, 2, NUMAX], F32) as TMP,
              nc.sbuf_tensor(f"e_act{l}", [128, NPAIR, 456], BF16) as ACTT,
              nc.sbuf_tensor(f"e_wu{l}", [128, 2, 16, 256], BF16) as WU,
              nc.sbuf_tensor(f"e_wd{l}", [128, 3, NPAIR, 128], BF16) as WD,
              nc.sbuf_tensor(f"e_ta{l}", [128, 2, 456], F32) as TA,
              nc.sbuf_tensor(f"e_tg{l}", [128, 2, 456], F32) as TG,
              nc.sbuf_tensor(f"e_sg{l}", [128, 2, 456], BF16) as SG,
              nc.sbuf_tensor(f"e_y{l}", [128, 16, 456], F32) as Y,
              nc.sbuf_tensor(f"e_rstdp{l}", [128, NUMAX], F32) as rstdP,
              nc.sbuf_tensor(f"e_rstde{l}", [128, NUMAX], F32) as rstdE):
            BXT = bufs("XT", 16); BSQP = bufs("SQP", 16); BSQ = bufs("SQ", 2); BHT = bufs("HT", 16); BTMP = bufs("TMP", 2)
            BACT = bufs("ACT", NPAIR)
            BWU = bufs("WU", 2); BWD = bufs("WD", 3); BTA = bufs("TA", 2); BTG = bufs("TG", 2); BSG = bufs("SG", 2)
            BY = bufs("Y", 16); BrP = Buf("rstdP"); BrE = Buf("rstdE")
            cnt = dict(wu=0, wd=0, pi=0, dc=0)
            vv = vecs[l]
            V_CW0, V_CW1, V_CW2, V_CB = 160, 248, 336, 424
            tiles = []
            for seg in segs:
                if seg.n == NTOK:
                    wins = [(456 * t, 456) for t in range(8)] + [(3648, 448)]
                else:
                    wins = [(0, seg.n)]
                for (lo, NO) in wins:
                    NU = NO + 2
                    a0 = max(lo - 1, 0)
                    a1 = min(lo + NO + 1, seg.n)
                    c0 = a0 - (lo - 1)
                    c1 = c0 + (a1 - a0)
                    tiles.append(dict(seg=seg, r=seg.r, lo=lo, NO=NO, NU=NU, a0=a0, a1=a1, c0=c0, c1=c1))

            def P0(T):
                c0, c1, NU = T["c0"], T["c1"], T["NU"]
                if c0 > 0:
                    S.op("dve", lambda e: e.memset(XT[:, :, 0:c0], 0.0), writes=BXT)
                if c1 < NU:
                    S.op("dve", lambda e: e.memset(XT[:, :, c1:NU], 0.0), writes=BXT)
                xsrc = T["seg"].xT[xi][:, :, T["a0"]:T["a1"]].rearrange("c p t -> p c t")
                S.op("sp", lambda e: e.dma_start(out=XT[:, :, c0:c1], in_=xsrc), writes=BXT, dma="e_xt")

            def P1(T):
                NU = T["NU"]
                for c in range(16):
                    S.op("act", lambda e, c=c: e.activation(out=SQP[:, c, 0:NU], in_=XT[:, c, 0:NU], func=AF.Square), reads=[BXT[c]], writes=[BSQP[c]])

            def P2(T):
                NU = T["NU"]
                for c in range(16):
                    S.op("pe", lambda e, c=c: e.matmul(ps[6][:, 0:NU], lhsT=onesb[:], rhs=SQP[:, c, 0:NU], start=(c == 0), stop=(c == 15)), reads=[BSQP[c]], writes=[PS[6]])

            def P3(T):
                NU, r, c0, c1 = T["NU"], T["r"], T["c0"], T["c1"]
                emit_rstd(PS, 6, rstdP[:, 0:NU], BrP, NU)
                for c in range(16):
                    S.op("dve", lambda e, c=c: e.scalar_tensor_tensor(out=TMP[:, c % 2, 0:NU], in0=XT[:, c, 0:NU], scalar=der[l]["Af"][:, c, r:r + 1], in1=rstdP[:, 0:NU],
                                                                     op0=ALU.mult, op1=ALU.mult), reads=[BXT[c], BrP], writes=[BTMP[c % 2]])
                    S.op("act", lambda e, c=c: e.activation(out=HT[:, c, 0:NU], in_=TMP[:, c % 2, 0:NU], func=AF.Identity, bias=modsT[l][:, 48 + c, r:r + 1]),
                         reads=[BTMP[c % 2]], writes=[BHT[c]])
                if c0 > 0:
                    S.op("dve", lambda e: e.memset(HT[:, :, 0:c0], 0.0), writes=BHT)
                if c1 < NU:
                    S.op("dve", lambda e: e.memset(HT[:, :, c1:NU], 0.0), writes=BHT)

            def up(T, nxt):
                NU, NO = T["NU"], T["NO"]
                for i in range(NPAIR):
                    sl = cnt["wu"] % 2
                    cnt["wu"] += 1
                    S.op("sp", lambda e, sl=sl, i=i: e.dma_start(out=WU[:, sl], in_=wupT[l][i]), writes=[BWU[sl]], dma=f"e_wu{sl}", extra=cwait(f"cv_wup{l}"))
                    ba = 2 * (cnt["pi"] % 2)
                    bg = ba + 1
                    ts = cnt["pi"] % 2
                    cnt["pi"] += 1
                    for kc in range(16):
                        S.op("pe", lambda e, ba=ba, sl=sl, kc=kc: e.matmul(ps[ba][:, 0:NU], lhsT=WU[:, sl, kc, 0:128], rhs=HT[:, kc, 0:NU], start=(kc == 0), stop=(kc == 15)),
                             reads=[BWU[sl], BHT[kc]], writes=[PS[ba]])
                    for kc in range(16):
                        S.op("pe", lambda e, bg=bg, sl=sl, kc=kc: e.matmul(ps[bg][:, 0:NU], lhsT=WU[:, sl, kc, 128:256], rhs=HT[:, kc, 0:NU], start=(kc == 0), stop=(kc == 15)),
                             reads=[BWU[sl], BHT[kc]], writes=[PS[bg]])
                    for (bk, Tt, BT, ci) in ((ba, TA, BTA, i), (bg, TG, BTG, NPAIR + i)):
                        S.op("act", lambda e, bk=bk, Tt=Tt, ts=ts, ci=ci: e.activation(out=Tt[:, ts, 0:NO], in_=ps[bk][:, 1:1 + NO], func=AF.Identity,
                                                                                  scale=vv[:, V_CW1 + ci:V_CW1 + ci + 1], bias=vv[:, V_CB + ci:V_CB + ci + 1]),
                             reads=[PS[bk]], writes=[BT[ts]])
                        S.op("dve", lambda e, bk=bk, Tt=Tt, ts=ts, ci=ci: e.scalar_tensor_tensor(out=Tt[:, ts, 0:NO], in0=ps[bk][:, 0:NO], scalar=vv[:, V_CW0 + ci:V_CW0 + ci + 1],
                                                                                            in1=Tt[:, ts, 0:NO], op0=ALU.mult, op1=ALU.add),
                             reads=[PS[bk], BT[ts]], writes=[BT[ts]])
                        S.op("dve", lambda e, bk=bk, Tt=Tt, ts=ts, ci=ci: e.scalar_tensor_tensor(out=Tt[:, ts, 0:NO], in0=ps[bk][:, 2:2 + NO], scalar=vv[:, V_CW2 + ci:V_CW2 + ci + 1],
                                                                                            in1=Tt[:, ts, 0:NO], op0=ALU.mult, op1=ALU.add),
                             reads=[PS[bk], BT[ts]], writes=[BT[ts]])
                    S.op("act", lambda e, ts=ts: e.activation(out=SG[:, ts, 0:NO], in_=TG[:, ts, 0:NO], func=AF.Silu), reads=[BTG[ts]], writes=[BSG[ts]])
                    S.op("dve", lambda e, ts=ts, i=i: e.tensor_tensor(out=ACTT[:, i, 0:NO], in0=TA[:, ts, 0:NO], in1=SG[:, ts, 0:NO], op=ALU.mult),
                         reads=[BTA[ts], BSG[ts]], writes=[BACT[i]])
                    if nxt is not None and i == NPAIR - 10:
                        P0(nxt)

            def down(T, nxt):
                NO = T["NO"]
                pend = None
                for m in range(16):
                    sl = cnt["wd"] % 3
                    cnt["wd"] += 1
                    S.op("sp", lambda e, sl=sl, m=m: e.dma_start(out=WD[:, sl], in_=wdnT[l][m]), writes=[BWD[sl]], dma=f"e_wd{sl}", extra=cwait(f"cv_wdn{l}"))
                    b = 4 + (cnt["dc"] % 2)
                    cnt["dc"] += 1
                    for kc in range(NPAIR):
                        S.op("pe", lambda e, b=b, sl=sl, kc=kc: e.matmul(ps[b][:, 0:NO], lhsT=WD[:, sl, kc, :], rhs=ACTT[:, kc, 0:NO], start=(kc == 0), stop=(kc == NPAIR - 1)),
                             reads=[BWD[sl], BACT[kc]], writes=[PS[b]])
                    S.op("act", lambda e, b=b, m=m: e.activation(out=SQ[:, m % 2, 0:NO], in_=ps[b][:, 0:NO], func=AF.Square), reads=[PS[b]], writes=[BSQ[m % 2]])
                    S.op("dve", lambda e, b=b, m=m: e.tensor_copy(out=Y[:, m, 0:NO], in_=ps[b][:, 0:NO]), reads=[PS[b]], writes=[BY[m]])

                    def ssmm(m=m):
                        S.op("pe", lambda e: e.matmul(ps[7][:, 0:NO], lhsT=onesb[:], rhs=SQ[:, m % 2, 0:NO], start=(m == 0), stop=(m == 15)), reads=[BSQ[m % 2]], writes=[PS[7]])
                    if pend is not None:
                        pend()
                    pend = ssmm
                    if nxt is not None and m == 3:
                        P2(nxt)
                    if nxt is not None and m == 6:
                        P3(nxt)
                pend()

            def epi(T):
                NO, r, lo = T["NO"], T["r"], T["lo"]
                emit_rstd(PS, 7, rstdE[:, 0:NO], BrE, NO)
                for m in range(16):
                    S.op("dve", lambda e, m=m: e.scalar_tensor_tensor(out=Y[:, m, 0:NO], in0=Y[:, m, 0:NO], scalar=der[l]["GF"][:, m, r:r + 1], in1=rstdE[:, 0:NO],
                                                                     op0=ALU.mult, op1=ALU.mult), reads=[BY[m], BrE], writes=[BY[m]])
                dst = T["seg"].fT[:, :, lo:lo + NO].rearrange("c p t -> p c t")
                S.op("act", lambda e: e.dma_start(out=dst, in_=Y[:, :, 0:NO]), reads=BY, dma="e_acc")

            P0(tiles[0]); P1(tiles[0]); P2(tiles[0]); P3(tiles[0])
            for k, T in enumerate(tiles):
                nxt = tiles[k + 1] if k + 1 < len(tiles) else None
                up(T, nxt)
                if nxt is not None:
                    P1(nxt)
                down(T, nxt)
                epi(T)
            return end_phase(f"EF{l}")

    def phase_tout(xi):
        PS = PSB()
        with (nc.sbuf_tensor("to_xt", [128, 2, 16, 512], F32) as XT,
              nc.sbuf_tensor("to_xf", [128, 2, 16, 512], F32) as XF,
              nc.sbuf_tensor("to_o", [128, 2, 4, D], F32) as O):
            BXT = bufs("XT", 2)
            BXF = bufs("XF", 2)
            BO = [[Buf(f"O{s}_{k}") for k in range(16)] for s in range(2)]
            cnt = 0
            for t in range(NTOK // 512):
                s = t % 2
                S.op("sp", lambda e, s=s, t=t: e.dma_start(out=XT[:, s], in_=SM.xT[xi][:, :, t * 512:(t + 1) * 512].rearrange("c p t -> p c t")), writes=[BXT[s]], dma=f"to_ld{s}")
                S.op("sp", lambda e, s=s, t=t: e.dma_start(out=XF[:, s], in_=SM.fT[:, :, t * 512:(t + 1) * 512].rearrange("c p t -> p c t")), writes=[BXF[s]], dma=f"to_ld{s}")
                for hh_ in range(2):
                    S.op("pool", lambda e, s=s, hh_=hh_: e.tensor_tensor(out=XT[:, s, 8 * hh_:8 * hh_ + 8, :], in0=XT[:, s, 8 * hh_:8 * hh_ + 8, :], in1=XF[:, s, 8 * hh_:8 * hh_ + 8, :], op=ALU.add),
                         reads=[BXT[s], BXF[s]], writes=[BXT[s]])
                for j in range(4):
                    for cb in range(4):
                        b = cnt % 6
                        cnt += 1
                        for cc in range(4):
                            S.op("pe", lambda e, b=b, cc=cc, s=s, cb=cb, j=j: e.transpose(ps[b][:, cc * 128:(cc + 1) * 128], XT[:, s, 4 * cb + cc, j * 128:(j + 1) * 128], ident[:]),
                                 reads=[BXT[s]], writes=[PS[b]])
                        if cnt % 2 == 0:
                            S.op("act", lambda e, b=b, s=s, j=j, cb=cb: e.activation(out=O[:, s, j, cb * 512:(cb + 1) * 512], in_=ps[b][:], func=AF.Copy), reads=[PS[b]], writes=[BO[s][j * 4 + cb]])
                        else:
                            S.op("dve", lambda e, b=b, s=s, j=j, cb=cb: e.tensor_copy(out=O[:, s, j, cb * 512:(cb + 1) * 512], in_=ps[b][:]), reads=[PS[b]], writes=[BO[s][j * 4 + cb]])
                S.op("act", lambda e, s=s, t=t: e.dma_start(out=out_d[t * 512:(t + 1) * 512, :].rearrange("(j p) f -> p j f", p=128), in_=O[:, s]), reads=BO[s], dma=f"to_st{s}")
            return end_phase("TOUT")

    def run():
        if phase_ada(0, True):
            return
        if phase_tin():
            return
        xi = 0
        for l in range(DEPTH):
            if l > 0:
                if phase_ada(l, False):
                    return
            if phase_a(l, xi, [SM, SC], addf=(l > 0)):
                return
            if phase_b(l, with_ctx_queries=(l == 0)):
                return
            segs = [SM, SC] if l == 0 else [SM]
            if phase_c(l, segs):
                return
            if phase_d(l, xi, segs):
                return
            if phase_ef(l, xi, segs):
                return
            xi = 1 - xi
        phase_tout(xi)

    run()
    S.final_wait()
    return nc, S


def _pack_vecs(g_pre_mix, g_post_mix, g_pre_ffn, g_post_ffn, b_ada, conv_w, conv_b):
    out = np.zeros((DEPTH, 128, 512), np.float32)
    for l in range(DEPTH):
        def cm(v):
            return np.ascontiguousarray(np.asarray(v, np.float32).reshape(-1, 128).T)
        out[l, :, 0:16] = cm(g_pre_mix[l])
        out[l, :, 16:32] = cm(g_post_mix[l])
        out[l, :, 32:48] = cm(g_pre_ffn[l])
        out[l, :, 48:64] = cm(g_post_ffn[l])
        out[l, :, 64:160] = cm(b_ada[l])
        out[l, :, 160:248] = cm(conv_w[l, 0])
        out[l, :, 248:336] = cm(conv_w[l, 1])
        out[l, :, 336:424] = cm(conv_w[l, 2])
        out[l, :, 424:512] = cm(conv_b[l])
    return out


def make_in_maps(inputs):
    C = _consts()
    x = np.asarray(inputs["x"], np.float32)
    c = np.asarray(inputs["c"], np.float32)
    ctx = np.asarray(inputs["ctx"], np.float32)
    c_ctx = np.asarray(inputs["c_ctx"], np.float32)
    rpb = np.asarray(inputs["rpb"], np.float32)
    dr, dc, ok = C["attn_idx"]
    tab = np.where(ok[None, None], rpb[:, :, dr, dc], np.float32(NEG)).astype(np.float32)
    vecs = _pack_vecs(inputs["g_pre_mix"], inputs["g_post_mix"], inputs["g_pre_ffn"], inputs["g_post_ffn"],
                      inputs["b_ada"], np.asarray(inputs["conv_w"], np.float32), inputs["conv_b"])
    shared = dict(
        w_ada=np.asarray(inputs["w_ada"], np.float32), w_in=np.asarray(inputs["w_in"], np.float32),
        w_out=np.asarray(inputs["w_out"], np.float32), w_up=np.asarray(inputs["w_up"], np.float32),
        w_down=np.asarray(inputs["w_down"], np.float32), vecs=vecs, wfour=np.asarray(inputs["w_four"], np.float32),
        tab=tab, dftc=C["dftc"], dfts=C["dfts"], dft256=C["dft256"], cc128=C["cc128"], ident=C["ident"], onesb=C["onesb"],
    )
    maps = []
    for b in range(NCORES):
        cc = np.stack([c[b], c_ctx])
        ccT = np.ascontiguousarray(cc.reshape(2, 16, 128).transpose(2, 1, 0).reshape(128, 32))
        m = dict(shared)
        m["x"] = np.ascontiguousarray(x[b])
        m["ctx"] = np.ascontiguousarray(ctx[b])
        m["ccT"] = ccT
        maps.append(m)
    return maps


_PROG = {}


def kernel(**inputs):
    if "nc" not in _PROG:
        _PROG["nc"], _ = build_program()
    nc = _PROG["nc"]
    maps = make_in_maps(inputs)
    res = run_bass_kernel_spmd(nc, maps, core_ids=list(range(NCORES)))
    return np.stack([np.asarray(r["out"], np.float32) for r in res.results], axis=0)
```
